# Optimizing a Trainium2 kernel written in Bass

```python
import math
import jax
import jax.numpy as jnp
from jax import lax
import numpy as np

D_MODEL = 1024
BATCH = 8
SEQ = 4096
DEPTH = 4

GRID_W = 64
CTX_LEN = 256
N_EVEN = (DEPTH + 1) // 2
N_ODD = DEPTH // 2
HEAD_DIM = 64
A_WIDTH = D_MODEL // 2
A_HEADS = A_WIDTH // HEAD_DIM
DECAY_LORA = 64
ICLR_LORA = 64
GATE_LORA = 128
A_IN = 3 * A_WIDTH + 2 * DECAY_LORA + 2 * ICLR_LORA + GATE_LORA
RWKV_GN_EPS = 64e-5
B_WIDTH = D_MODEL - A_WIDTH
B_Q_HEADS = B_WIDTH // HEAD_DIM
B_KV_HEADS = 2
B_GROUP = B_Q_HEADS // B_KV_HEADS
B_IN = (B_Q_HEADS + 2 * B_KV_HEADS) * HEAD_DIM
EVEN_IN = A_IN + B_IN
WINDOW = 128
BLOCK = 128
ROPE_THETA = 10000.0
MASK_VALUE = -1e30
HY_ORDER = 2
HY_EMB = 33
HY_BANDS = (HY_EMB - 1) // 2
HY_FFN = 64
HY_TARGET = 1e-2
HY_FAST_PCT = 0.3
HY_SLOW_PCT = 1.5
HY_MOD_SHIFT = 0.05
FFN_HIDDEN = 2816
NORM_EPS = 1e-6

kernel_name = 'hybrid_rwkv7_swa_hyena_dit_block'


def rms_norm(x, g):
    xf = x.astype(jnp.float32)
    xf = xf * lax.rsqrt(jnp.mean(jnp.square(xf), axis=-1, keepdims=True) + NORM_EPS)
    return (xf * g).astype(x.dtype)


def modulate(h, shift, scale):
    return h * (1.0 + scale) + shift


def neighbours(z):
    zp = jnp.pad(z, ((0, 0), (1, 1), (0, 0)))
    return zp[:, :-2], zp[:, 2:]


def dwconv3(z, w, b):
    prev, nxt = neighbours(z)
    return prev * w[0] + z * w[1] + nxt * w[2] + b


def token_shift(z, mu_prev, mu_next):
    prev, nxt = neighbours(z)
    return z + mu_prev * (prev - z) + mu_next * (nxt - z)


def axial_rope(n_tokens):
    rows = n_tokens // GRID_W
    row = jnp.repeat(jnp.arange(rows), GRID_W).astype(jnp.float32)
    col = jnp.tile(jnp.arange(GRID_W), rows).astype(jnp.float32)
    n_freq = HEAD_DIM // 4
    inv = ROPE_THETA ** (-jnp.arange(n_freq, dtype=jnp.float32) / n_freq)
    ang = jnp.concatenate([row[:, None] * inv, col[:, None] * inv], axis=-1)
    return jnp.cos(ang), jnp.sin(ang)


def apply_rope(x, cos, sin):
    half = HEAD_DIM // 2
    x1, x2 = x[..., :half], x[..., half:]
    cos = cos[None, :, None, :].astype(x.dtype)
    sin = sin[None, :, None, :].astype(x.dtype)
    return jnp.concatenate([x1 * cos - x2 * sin, x1 * sin + x2 * cos], axis=-1)


def softmax_with_sink(s, sink):
    m = jnp.maximum(jnp.max(s, axis=-1, keepdims=True), sink)
    e = jnp.exp(s - m)
    return e / (jnp.sum(e, axis=-1, keepdims=True) + jnp.exp(sink - m))


def attn_heads(pb):
    bsz, n = pb.shape[:2]
    nq = B_Q_HEADS * HEAD_DIM
    nk = B_KV_HEADS * HEAD_DIM
    q = pb[..., :nq].reshape(bsz, n, B_Q_HEADS, HEAD_DIM)
    k = pb[..., nq:nq + nk].reshape(bsz, n, B_KV_HEADS, HEAD_DIM)
    v = pb[..., nq + nk:].reshape(bsz, n, B_KV_HEADS, HEAD_DIM)
    return q, k, v


def window_attention(q, k, v, kc, vc, sink):
    bsz, n = q.shape[:2]
    nb = n // BLOCK
    scale = HEAD_DIM ** -0.5
    qb = q.reshape(bsz, nb, BLOCK, B_KV_HEADS, B_GROUP, HEAD_DIM).transpose(1, 0, 2, 3, 4, 5)

    def band(t):
        tp = jnp.pad(t, ((0, 0), (BLOCK, BLOCK), (0, 0), (0, 0)))
        tp = tp.reshape(bsz, nb + 2, BLOCK, B_KV_HEADS, HEAD_DIM)
        tb = jnp.concatenate([tp[:, :-2], tp[:, 1:-1], tp[:, 2:]], axis=2)
        return tb.transpose(1, 0, 2, 3, 4)

    kb, vb = band(k), band(v)
    qi = jnp.arange(BLOCK)[:, None]
    kj = jnp.arange(3 * BLOCK)[None, :]
    in_window = jnp.abs(kj - BLOCK - qi) <= WINDOW
    sink_g = sink.reshape(B_KV_HEADS, B_GROUP)[None, :, :, None, None].astype(jnp.float32)

    def one_block(args):
        blk, q_blk, k_blk, v_blk = args
        kpos = blk * BLOCK - BLOCK + kj
        valid = in_window & (kpos >= 0) & (kpos < n)
        s_loc = jnp.einsum('bqhgd,bkhd->bhgqk', q_blk, k_blk).astype(jnp.float32) * scale
        s_loc = jnp.where(valid, s_loc, MASK_VALUE)
        s_ctx = jnp.einsum('bqhgd,bkhd->bhgqk', q_blk, kc).astype(jnp.float32) * scale
        p = softmax_with_sink(jnp.concatenate([s_loc, s_ctx], axis=-1), sink_g).astype(v.dtype)
        o = jnp.einsum('bhgqk,bkhd->bqhgd', p[..., :3 * BLOCK], v_blk)
        return o + jnp.einsum('bhgqk,bkhd->bqhgd', p[..., 3 * BLOCK:], vc)

    out = lax.map(one_block, (jnp.arange(nb), qb, kb, vb))
    return out.transpose(1, 0, 2, 3, 4, 5).reshape(bsz, n, B_Q_HEADS * HEAD_DIM)


def context_attention(qc, kc, vc, sink):
    bsz, n = qc.shape[:2]
    qg = qc.reshape(bsz, n, B_KV_HEADS, B_GROUP, HEAD_DIM)
    s = jnp.einsum('bqhgd,bkhd->bhgqk', qg, kc).astype(jnp.float32) * (HEAD_DIM ** -0.5)
    sink_g = sink.reshape(B_KV_HEADS, B_GROUP)[None, :, :, None, None].astype(jnp.float32)
    p = softmax_with_sink(s, sink_g).astype(vc.dtype)
    return jnp.einsum('bhgqk,bkhd->bqhgd', p, vc).reshape(bsz, n, B_Q_HEADS * HEAD_DIM)


def rwkv7_inputs(za, ep):
    bsz, n = za.shape[:2]
    C = A_WIDTH
    r = za[..., :C]
    k = za[..., C:2 * C]
    v = za[..., 2 * C:3 * C]
    o = 3 * C
    wd = za[..., o:o + 2 * DECAY_LORA].reshape(bsz, n, 2, DECAY_LORA)
    o += 2 * DECAY_LORA
    ad = za[..., o:o + 2 * ICLR_LORA].reshape(bsz, n, 2, ICLR_LORA)
    o += 2 * ICLR_LORA
    gd = za[..., o:o + GATE_LORA]
    w_log = -jax.nn.softplus(-(ep['w0'] + jnp.einsum('btdr,drc->btdc', jnp.tanh(wd), ep['w2']))) - 0.5
    decay = jnp.exp(-jnp.exp(w_log.astype(jnp.float32)))
    a = jax.nn.sigmoid(ep['a0'] + jnp.einsum('btdr,drc->btdc', ad, ep['a2']))
    g = jax.nn.sigmoid(gd) @ ep['g2']
    kk = (k * ep['k_k']).astype(jnp.float32).reshape(bsz, n, A_HEADS, HEAD_DIM)
    kk = kk / jnp.maximum(jnp.sqrt(jnp.sum(kk * kk, axis=-1, keepdims=True)), 1e-12)
    kk = kk.reshape(bsz, n, C)
    k_dir = k[:, :, None, :] * (1.0 + (a - 1.0) * ep['k_a'])
    return {'r': r, 'decay': decay, 'k': k_dir, 'v': v, 'kk': kk, 'a': a, 'g': g}


def dir_layout(t):
    t = jnp.stack([t[:, :, 0], jnp.flip(t[:, :, 1], axis=1)], axis=0)
    _, bsz, n, _ = t.shape
    return t.reshape(2, bsz, n, A_HEADS, HEAD_DIM).transpose(2, 0, 1, 3, 4).astype(jnp.float32)


def shared_layout(t):
    return dir_layout(jnp.stack([t, t], axis=2))


def rwkv7_run(state0, inp):
    def step(S, xs):
        r_t, w_t, k_t, v_t, kk_t, a_t = xs
        sa = -jnp.einsum('dbhij,dbhj->dbhi', S, kk_t)
        S = S * w_t[..., None, :] + sa[..., None] * (kk_t * a_t)[..., None, :] + v_t[..., None] * k_t[..., None, :]
        return S, jnp.einsum('dbhij,dbhj->dbhi', S, r_t)

    xs = (shared_layout(inp['r']), dir_layout(inp['decay']), dir_layout(inp['k']),
          shared_layout(inp['v']), shared_layout(inp['kk']), dir_layout(inp['a']))
    S, ys = lax.scan(step, state0, xs)
    ys = ys.transpose(1, 2, 0, 3, 4)
    return S, ys[0] + jnp.flip(ys[1], axis=1)


def rwkv7_output(y, inp, ep, dtype):
    bsz, n = y.shape[:2]
    mu = jnp.mean(y, axis=-1, keepdims=True)
    var = jnp.var(y, axis=-1, keepdims=True)
    yn = ((y - mu) * lax.rsqrt(var + RWKV_GN_EPS)).reshape(bsz, n, A_WIDTH) * ep['ln_w'] + ep['ln_b']
    r = inp['r'].reshape(bsz, n, A_HEADS, HEAD_DIM)
    kd = inp['k'].reshape(bsz, n, 2, A_HEADS, HEAD_DIM)
    v = inp['v'].reshape(bsz, n, A_HEADS, HEAD_DIM)
    bonus = jnp.einsum('bthn,btdhn,hn->bth', r, kd, ep['r_k'])[..., None] * v
    return ((yn + bonus.reshape(bsz, n, A_WIDTH)) * inp['g']).astype(dtype)


def even_mixer(h_lat, h_ctx, ep, rope_cos, rope_sin, need_ctx):
    p_lat = h_lat @ ep['w_in']
    p_ctx = h_ctx @ ep['w_in']
    in_ctx = rwkv7_inputs(token_shift(p_ctx[..., :A_IN], ep['mu_prev'], ep['mu_next']), ep)
    in_lat = rwkv7_inputs(token_shift(p_lat[..., :A_IN], ep['mu_prev'], ep['mu_next']), ep)
    state0 = jnp.zeros((2, h_lat.shape[0], A_HEADS, HEAD_DIM, HEAD_DIM), jnp.float32)
    s_ctx, y_ctx = rwkv7_run(state0, in_ctx)
    _, y_lat = rwkv7_run(s_ctx, in_lat)
    a_lat = rwkv7_output(y_lat, in_lat, ep, h_lat.dtype)
    q_l, k_l, v_l = attn_heads(p_lat[..., A_IN:])
    q_c, k_c, v_c = attn_heads(p_ctx[..., A_IN:])
    q_l = apply_rope(rms_norm(q_l, ep['q_norm']), rope_cos, rope_sin)
    k_l = apply_rope(rms_norm(k_l, ep['k_norm']), rope_cos, rope_sin)
    k_c = rms_norm(k_c, ep['k_norm'])
    b_lat = window_attention(q_l, k_l, v_l, k_c, v_c, ep['sink'])
    out_lat = jnp.concatenate([a_lat, b_lat], axis=-1) @ ep['w_out']
    if not need_ctx:
        return out_lat, None
    a_ctx = rwkv7_output(y_ctx, in_ctx, ep, h_ctx.dtype)
    b_ctx = context_attention(rms_norm(q_c, ep['q_norm']), k_c, v_c, ep['sink'])
    out_ctx = jnp.concatenate([a_ctx, b_ctx], axis=-1) @ ep['w_out']
    return out_lat, out_ctx


def hyena_filter_spectra(n, op):
    t = jnp.linspace(0.0, 1.0, n, dtype=jnp.float32)[:, None]
    ang = 2.0 * math.pi * jnp.arange(n, dtype=jnp.float32)[:, None] / n
    f = jnp.linspace(1e-4, HY_BANDS - 1, HY_BANDS, dtype=jnp.float32)[None, :]
    z = jnp.concatenate([t, jnp.cos(f * ang), -jnp.sin(f * ang)], axis=-1)
    hdn = jnp.sin(op['f_freq'] * (z @ op['f_w1'] + op['f_b1']))
    hdn = jnp.sin(op['f_freq'] * (hdn @ op['f_w2'] + op['f_b2']))
    hdn = jnp.sin(op['f_freq'] * (hdn @ op['f_w3'] + op['f_b3']))
    h = (hdn @ op['f_out']).astype(jnp.float32).reshape(n, HY_ORDER, 2, D_MODEL)
    deltas = jnp.abs(jnp.linspace(math.log(HY_TARGET) / HY_SLOW_PCT, math.log(HY_TARGET) / HY_FAST_PCT,
                                  D_MODEL, dtype=jnp.float32))
    h = h * (jnp.exp(-t * deltas) + HY_MOD_SHIFT)[:, None, None, :]
    h_fwd, h_bwd = h[:, :, 0], h[:, :, 1]
    kern = jnp.concatenate([h_fwd[:1] + h_bwd[:1], h_fwd[1:],
                            jnp.zeros((1, HY_ORDER, D_MODEL), jnp.float32), jnp.flip(h_bwd[1:], axis=0)], axis=0)
    return jnp.fft.rfft(kern, axis=0)


def long_conv(u, kern_f, skip):
    n = u.shape[1]
    uf = jnp.fft.rfft(u.astype(jnp.float32), n=2 * n, axis=1)
    y = jnp.fft.irfft(uf * kern_f[None], n=2 * n, axis=1)[:, :n]
    return (y + u * skip).astype(u.dtype)


def hyena_mixer(h, op):
    z = dwconv3(h @ op['w_in'] + op['b_in'], op['conv_w'], op['conv_b'])
    v, x1, x2 = jnp.split(z, 3, axis=-1)
    kf = hyena_filter_spectra(h.shape[1], op)
    y = x1 * long_conv(v, kf[:, 0], op['skip'][0])
    y = x2 * long_conv(y, kf[:, 1], op['skip'][1])
    return y @ op['w_out'] + op['b_out']


def conv_ffn(h, w_up, conv_w, conv_b, w_down):
    u = dwconv3(h @ w_up, conv_w, conv_b)
    gate, val = jnp.split(u, 2, axis=-1)
    return (jax.nn.silu(gate) * val) @ w_down


def setup_inputs(seed: int = 0) -> dict:
    key = jax.random.key(seed)
    keys = jax.random.split(key, 64)
    counter = [0]

    def nxt():
        counter[0] += 1
        return keys[counter[0] - 1]

    def nrm(shape, scale):
        return jax.random.normal(nxt(), shape, jnp.float32) * scale

    def unif(shape, lo, hi):
        return jax.random.uniform(nxt(), shape, jnp.float32, lo, hi)

    D = D_MODEL
    F = FFN_HIDDEN
    E = N_EVEN
    O = N_ODD
    return {
        'x': nrm((BATCH, SEQ, D), 1.0),
        'c': nrm((BATCH, D), 1.0),
        'ctx': nrm((BATCH, CTX_LEN, D), 1.0),
        'c_ctx': nrm((D,), 1.0),
        'ada_w': nrm((DEPTH, D, 6 * D), 0.5 * D ** -0.5),
        'ada_b': nrm((DEPTH, 6 * D), 0.02),
        'norm1_g': 1.0 + nrm((DEPTH, D), 0.02),
        'norm2_g': 1.0 + nrm((DEPTH, D), 0.02),
        'ffn_up': nrm((DEPTH, D, 2 * F), D ** -0.5),
        'ffn_conv_w': nrm((DEPTH, 3, 2 * F), 3 ** -0.5),
        'ffn_conv_b': nrm((DEPTH, 2 * F), 0.02),
        'ffn_down': nrm((DEPTH, F, D), F ** -0.5),
        'ev_w_in': nrm((E, D, EVEN_IN), D ** -0.5),
        'ev_mu_prev': unif((E, A_IN), 0.0, 0.5),
        'ev_mu_next': unif((E, A_IN), 0.0, 0.5),
        'ev_w0': unif((E, 2, A_WIDTH), -6.0, -1.0),
        'ev_w2': nrm((E, 2, DECAY_LORA, A_WIDTH), 0.1 * DECAY_LORA ** -0.5),
        'ev_a0': nrm((E, 2, A_WIDTH), 0.1),
        'ev_a2': nrm((E, 2, ICLR_LORA, A_WIDTH), 0.3 * ICLR_LORA ** -0.5),
        'ev_g2': nrm((E, GATE_LORA, A_WIDTH), GATE_LORA ** -0.5),
        'ev_k_k': 0.85 + nrm((E, A_WIDTH), 0.05),
        'ev_k_a': 1.0 + nrm((E, A_WIDTH), 0.05),
        'ev_r_k': nrm((E, A_HEADS, HEAD_DIM), 0.1),
        'ev_ln_w': 1.0 + nrm((E, A_WIDTH), 0.02),
        'ev_ln_b': nrm((E, A_WIDTH), 0.02),
        'ev_q_norm': 1.0 + nrm((E, HEAD_DIM), 0.02),
        'ev_k_norm': 1.0 + nrm((E, HEAD_DIM), 0.02),
        'ev_sink': nrm((E, B_Q_HEADS), 0.5),
        'ev_w_out': nrm((E, D, D), D ** -0.5),
        'od_w_in': nrm((O, D, 3 * D), D ** -0.5),
        'od_b_in': nrm((O, 3 * D), 0.02),
        'od_conv_w': nrm((O, 3, 3 * D), 3 ** -0.5),
        'od_conv_b': nrm((O, 3 * D), 0.02),
        'od_f_w1': nrm((O, HY_EMB, HY_FFN), HY_EMB ** -0.5),
        'od_f_b1': nrm((O, HY_FFN), 1.0),
        'od_f_w2': nrm((O, HY_FFN, HY_FFN), HY_FFN ** -0.5),
        'od_f_b2': nrm((O, HY_FFN), 0.1),
        'od_f_w3': nrm((O, HY_FFN, HY_FFN), HY_FFN ** -0.5),
        'od_f_b3': nrm((O, HY_FFN), 0.1),
        'od_f_freq': 1.0 + nrm((O, HY_FFN), 0.1),
        'od_f_out': nrm((O, HY_FFN, HY_ORDER * 2 * D), 0.05 * HY_FFN ** -0.5),
        'od_skip': nrm((O, HY_ORDER, D), 0.1),
        'od_w_out': nrm((O, D, D), D ** -0.5),
        'od_b_out': nrm((O, D), 0.02),
    }


def reference(x, c, ctx, c_ctx, ada_w, ada_b, norm1_g, norm2_g, ffn_up, ffn_conv_w, ffn_conv_b, ffn_down,
              ev_w_in, ev_mu_prev, ev_mu_next, ev_w0, ev_w2, ev_a0, ev_a2, ev_g2, ev_k_k, ev_k_a, ev_r_k,
              ev_ln_w, ev_ln_b, ev_q_norm, ev_k_norm, ev_sink, ev_w_out,
              od_w_in, od_b_in, od_conv_w, od_conv_b, od_f_w1, od_f_b1, od_f_w2, od_f_b2, od_f_w3, od_f_b3,
              od_f_freq, od_f_out, od_skip, od_w_out, od_b_out):
    rope_cos, rope_sin = axial_rope(x.shape[1])
    cond_lat = jax.nn.silu(c)
    cond_ctx = jax.nn.silu(c_ctx)
    for layer in range(DEPTH):
        need_ctx = layer < DEPTH - 1
        even = layer % 2 == 0
        j = layer // 2
        mod_l = jnp.split((cond_lat @ ada_w[layer] + ada_b[layer])[:, None, :], 6, axis=-1)
        h_lat = modulate(rms_norm(x, norm1_g[layer]), mod_l[0], mod_l[1])
        if even or need_ctx:
            mod_c = jnp.split((cond_ctx @ ada_w[layer] + ada_b[layer])[None, None, :], 6, axis=-1)
            h_ctx = modulate(rms_norm(ctx, norm1_g[layer]), mod_c[0], mod_c[1])
        if even:
            ep = {'w_in': ev_w_in[j], 'mu_prev': ev_mu_prev[j], 'mu_next': ev_mu_next[j],
                  'w0': ev_w0[j], 'w2': ev_w2[j], 'a0': ev_a0[j], 'a2': ev_a2[j], 'g2': ev_g2[j],
                  'k_k': ev_k_k[j], 'k_a': ev_k_a[j], 'r_k': ev_r_k[j], 'ln_w': ev_ln_w[j], 'ln_b': ev_ln_b[j],
                  'q_norm': ev_q_norm[j], 'k_norm': ev_k_norm[j], 'sink': ev_sink[j], 'w_out': ev_w_out[j]}
            o_lat, o_ctx = even_mixer(h_lat, h_ctx, ep, rope_cos, rope_sin, need_ctx)
        else:
            op = {'w_in': od_w_in[j], 'b_in': od_b_in[j], 'conv_w': od_conv_w[j], 'conv_b': od_conv_b[j],
                  'f_w1': od_f_w1[j], 'f_b1': od_f_b1[j], 'f_w2': od_f_w2[j], 'f_b2': od_f_b2[j],
                  'f_w3': od_f_w3[j], 'f_b3': od_f_b3[j], 'f_freq': od_f_freq[j], 'f_out': od_f_out[j],
                  'skip': od_skip[j], 'w_out': od_w_out[j], 'b_out': od_b_out[j]}
            o_lat = hyena_mixer(h_lat, op)
            o_ctx = hyena_mixer(h_ctx, op) if need_ctx else None
        x = x + mod_l[2] * o_lat
        x = x + mod_l[5] * conv_ffn(modulate(rms_norm(x, norm2_g[layer]), mod_l[3], mod_l[4]),
                                    ffn_up[layer], ffn_conv_w[layer], ffn_conv_b[layer], ffn_down[layer])
        if need_ctx:
            ctx = ctx + mod_c[2] * o_ctx
            ctx = ctx + mod_c[5] * conv_ffn(modulate(rms_norm(ctx, norm2_g[layer]), mod_c[3], mod_c[4]),
                                            ffn_up[layer], ffn_conv_w[layer], ffn_conv_b[layer], ffn_down[layer])
    return x
```

```python
import contextlib
import math
import numpy as np
import concourse.bass as bass
import concourse.mybir as mybir
from concourse.bass_utils import run_bass_kernel_spmd

F32 = mybir.dt.float32
BF16 = mybir.dt.bfloat16
AF = mybir.ActivationFunctionType
ALU = mybir.AluOpType
AX = mybir.AxisListType

D = 1024
SEQ = 4096
CTXL = 256
DEPTH = 4
NTOK = SEQ + CTXL
CTX0 = 1
LAT0 = CTX0 + CTXL + 2
NT = LAT0 + SEQ + 1
FF = 2816
A_W = 512
A_IN = 1920
EVEN_IN = 2688
TILES = [(CTX0, CTXL, 1)] + [(LAT0 + i * 512, 512, 0) for i in range(8)]

NPOOL = 12


class K:
    def __init__(self):
        self.nc = bass.Bass("TRN2", target_bir_lowering=False)
        self.es = contextlib.ExitStack()
        nc = self.nc
        self.eng = {"pe": nc.tensor, "dve": nc.vector, "act": nc.scalar, "pool": nc.gpsimd, "sp": nc.sync}
        self.sem = {e: self.es.enter_context(nc.semaphore("sem_" + e)) for e in self.eng}
        self.cnt = {e: 0 for e in self.eng}
        self.waited = {e: {} for e in self.eng}
        self.dsem = {e: [self.es.enter_context(nc.semaphore("dsem_%s_%d" % (e, i))) for i in range(NPOOL)]
                     for e in ("sp", "act", "pool")}
        self.dval = {e: [0] * NPOOL for e in self.dsem}
        self.dnext = {e: 0 for e in self.dsem}
        self.lastw = {}
        self.readers = {}
        self.ninst = 0
        self.scopes = []

    def sbuf(self, name, shape, dtype=F32):
        st = self.scopes[-1] if self.scopes else self.es
        self.uid = getattr(self, "uid", 0) + 1
        return st.enter_context(self.nc.sbuf_tensor("%s_%d" % (name, self.uid), list(shape), dtype))

    def psum(self, name, shape, dtype=F32):
        st = self.scopes[-1] if self.scopes else self.es
        self.uid = getattr(self, "uid", 0) + 1
        return st.enter_context(self.nc.psum_tensor("%s_%d" % (name, self.uid), list(shape), dtype))

    def dram(self, name, shape, dtype=F32, kind="Internal"):
        return self.nc.dram_tensor(name, list(shape), dtype, kind=kind).ap()

    @contextlib.contextmanager
    def scope(self):
        self.barrier()
        st = contextlib.ExitStack()
        self.scopes.append(st)
        try:
            yield
        finally:
            self.barrier()
            self.scopes.pop()
            st.close()

    def barrier(self):
        evs = [(self.sem[e], self.cnt[e]) for e in self.eng if self.cnt[e] > 0]
        for q in self.dsem:
            for i in range(NPOOL):
                if self.dval[q][i] > 0:
                    evs.append((self.dsem[q][i], self.dval[q][i]))
        for e in self.eng:
            need = {id(s): (s, v) for s, v in evs if s is not self.sem[e]}
            self._emit_waits(e, need)

    def _need(self, me, rd, wr):
        need = {}

        def add(ev):
            if ev is None:
                return
            sem, val, owner = ev
            if me == "pe" and owner == "pe":
                return
            k = id(sem)
            if k not in need or need[k][1] < val:
                need[k] = (sem, val)
        for r in rd:
            add(self.lastw.get(r))
        for w in wr:
            add(self.lastw.get(w))
            for ev in self.readers.get(w, ()):
                add(ev)
        return need

    def _emit_waits(self, e, need):
        wd = self.waited[e]
        for k, (sem, val) in need.items():
            if wd.get(k, 0) >= val:
                continue
            self.eng[e].wait_ge(sem, val)
            wd[k] = val

    def _record(self, ev, rd, wr):
        for w in wr:
            self.lastw[w] = ev
            self.readers[w] = []
        for r in rd:
            if r in wr:
                continue
            lst = self.readers.setdefault(r, [])
            lst.append(ev)
            if len(lst) > 48:
                best = {}
                for s, v, o in lst:
                    if id(s) not in best or best[id(s)][1] < v:
                        best[id(s)] = (s, v, o)
                self.readers[r] = list(best.values())

    def op(self, e, fn, rd=(), wr=(), **kw):
        self._emit_waits(e, self._need(e, rd, wr))
        inst = fn(**kw)
        self.cnt[e] += 1
        inst.then_inc(self.sem[e], 1)
        self._record((self.sem[e], self.cnt[e], e), rd, wr)
        self.ninst += 1
        return inst

    def dma(self, q, out, in_, rd=(), wr=(), **kw):
        q = "act" if (str(out.space) == "SB" and str(in_.space) == "DRAM") else "sp"
        need = self._need(None, rd, wr)
        i = self.dnext[q]
        self.dnext[q] = (i + 1) % NPOOL
        sem = self.dsem[q][i]
        if self.dval[q][i] > 0:
            k = id(sem)
            if k not in need or need[k][1] < self.dval[q][i]:
                need[k] = (sem, self.dval[q][i])
        self._emit_waits(q, need)
        inst = self.eng[q].dma_start(out=out, in_=in_, **kw)
        self.dval[q][i] += 16
        inst.then_inc(sem, 16)
        self._record((sem, self.dval[q][i], "dma"), rd, wr)
        self.ninst += 1
        return inst

    def mm(self, out, lhsT, rhs, start, stop, rd, wr):
        return self.op("pe", self.nc.tensor.matmul, rd=rd, wr=wr, out=out, lhsT=lhsT, rhs=rhs, start=start, stop=stop)

    def act(self, out, in_, func, rd, wr, bias=None, scale=None, accum_out=None):
        kw = dict(out=out, in_=in_, func=func)
        if bias is not None:
            kw["bias"] = bias
        if scale is not None:
            kw["scale"] = scale
        if accum_out is not None:
            kw["accum_out"] = accum_out
        return self.op("act", self.nc.scalar.activation, rd=rd, wr=wr, **kw)

    def tt(self, out, in0, in1, op, rd, wr, e="dve"):
        return self.op(e, self.eng[e].tensor_tensor, rd=rd, wr=wr, out=out, in0=in0, in1=in1, op=op)

    def ts(self, out, in0, s1, op0, rd, wr, s2=None, op1=None, e="dve"):
        kw = dict(out=out, in0=in0, scalar1=s1, scalar2=s2, op0=op0)
        if op1 is not None:
            kw["op1"] = op1
        return self.op(e, self.eng[e].tensor_scalar, rd=rd, wr=wr, **kw)

    def stt(self, out, in0, scalar, in1, op0, op1, rd, wr):
        return self.op("dve", self.nc.vector.scalar_tensor_tensor, rd=rd, wr=wr, out=out, in0=in0, scalar=scalar,
                       in1=in1, op0=op0, op1=op1)

    def cp(self, out, in_, rd, wr, e="dve"):
        return self.op(e, self.eng[e].tensor_copy, rd=rd, wr=wr, out=out, in_=in_)

    def memset(self, ap, val, wr, e="pool"):
        return self.op(e, self.eng[e].memset, rd=(), wr=wr, ap=ap, constant=val)


class ColPack:
    def __init__(self):
        self.cols = []
        self.off = {}
        self.n = 0

    def add(self, name, arr128xm):
        a = np.ascontiguousarray(arr128xm, dtype=np.float32)
        assert a.shape[0] == 128
        self.off[name] = self.n
        self.n += a.shape[1]
        self.cols.append(a)

    def addvec(self, name, v):
        v = np.asarray(v, dtype=np.float32).reshape(-1)
        if v.size < 128:
            v = np.concatenate([v, np.zeros(128 - v.size, np.float32)])
        assert v.size % 128 == 0
        self.add(name, v.reshape(-1, 128).T)

    def array(self):
        return np.ascontiguousarray(np.concatenate(self.cols, axis=1))


def pipelined(items, load_fn, compute_fn):
    if not items:
        return
    load_fn(0, items[0])
    for i, it in enumerate(items):
        if i + 1 < len(items):
            load_fn(i + 1, items[i + 1])
        compute_fn(i, it)


def bc(ap, shape):
    return ap.broadcast_to(list(shape))


class Prog:
    def __init__(self, cp_off, ncol, debug=False, stop_after=None, ext_in=()):
        self.k = K()
        k = self.k
        self.debug = debug
        self.stop_after = stop_after
        self.cp_off = cp_off
        EI = "ExternalInput"
        sk = "ExternalOutput" if debug else "Internal"
        self.in_names = []

        def inp(name, shape, dt=F32):
            self.in_names.append(name)
            return k.dram(name, shape, dt, EI)
        self.xT = inp("xT", [D, SEQ])
        self.ctxT = inp("ctxT", [D, CTXL])
        self.condc = inp("condc", [128, 16])
        self.colsD = inp("cols", [128, ncol])
        self.identD = inp("ident", [128, 128])
        self.ada_w = inp("ada_w", [DEPTH, D, 6 * D])
        self.ffn_up = inp("ffn_up", [DEPTH, D, 2 * FF])
        self.ffn_down = inp("ffn_down", [DEPTH, FF, D])
        self.ev_w_in = inp("ev_w_in", [2, D, EVEN_IN])
        self.ev_w_out = inp("ev_w_out", [2, D, D])
        self.od_w_in = inp("od_w_in", [2, D, 3 * D])
        self.od_w_out = inp("od_w_out", [2, D, D])
        self.outT = k.dram("outT", [D, SEQ], F32, "ExternalOutput")
        def scr(name, shape, dt=F32):
            if name in ext_in:
                return inp(name, shape, dt)
            return k.dram(name, shape, dt, sk)
        self.scr = scr
        self.XT = scr("XT", [D, NT])
        self.PT = scr("PT", [3 * D, NT])
        self.MT = scr("MT", [D, NT])
        self.AT = scr("AT", [FF, NT], BF16)
        self.cols = k.sbuf("cols_sb", [128, ncol])
        self.ident = k.sbuf("ident_sb", [128, 128])
        self.ones = k.sbuf("ones_sb", [128, 128])
        self.mod = k.sbuf("mod_sb", [128, DEPTH, 48, 2])
        self.gA = k.sbuf("gA_sb", [128, DEPTH, 2, 8, 2])
        k.dma("sp", self.cols[:], self.colsD[:, :], rd=[], wr=["cols"])
        k.dma("sp", self.ident[:], self.identD[:, :], rd=[], wr=["ident"])
        k.memset(self.ones[:], 1.0, wr=["ones"])

    def col(self, name, i=0, n=1):
        o = self.cp_off[name] + i
        return self.cols[:, o:o + n]

    def phase_init(self):
        k = self
        kk = self.k
        kk.dma("sp", self.XT[:, LAT0:LAT0 + SEQ], self.xT[:, :], rd=[], wr=["XT"])
        kk.dma("sp", self.XT[:, CTX0:CTX0 + CTXL], self.ctxT[:, :], rd=[], wr=["XT"])

    def phase_cond(self):
        kk = self.k
        nc = kk.nc
        with kk.scope():
            cond = kk.sbuf("cond", [128, 8, 2])
            craw = kk.sbuf("craw", [128, 16])
            aw = [kk.sbuf("aw%d" % i, [128, 6 * D]) for i in range(2)]
            ps = kk.psum("cond_ps", [128, 96])
            kk.dma("sp", craw[:], self.condc[:, :], rd=[], wr=["craw"])
            kk.act(cond[:].rearrange("p k s -> p s k"), craw[:].rearrange("p (s k) -> p s k", s=2), AF.Silu,
                   rd=["craw"], wr=["cond"])
            for L in range(DEPTH):
                acc = self.mod[:, L].rearrange("p c s -> p (c s)")
                for kc in range(8):
                    b = aw[kc % 2]
                    bk = "aw%d" % (kc % 2)
                    kk.dma("sp", b[:], self.ada_w[L, kc * 128:(kc + 1) * 128, :], rd=[], wr=[bk])
                    for oc in range(48):
                        kk.mm(ps[:, oc * 2:oc * 2 + 2], b[:, oc * 128:(oc + 1) * 128], cond[:, kc, :], True, True,
                              rd=[bk, "cond"], wr=["cond_ps"])
                    if kc == 0:
                        kk.cp(acc, ps[:, :], rd=["cond_ps"], wr=[("mod", L)])
                    else:
                        kk.tt(acc, acc, ps[:, :], ALU.add, rd=["cond_ps", ("mod", L)], wr=[("mod", L)])
                ab = self.col("ada_b%d" % L, 0, 48)
                kk.tt(self.mod[:, L], self.mod[:, L], bc(ab.unsqueeze(2), [128, 48, 2]), ALU.add,
                      rd=[("mod", L), "cols"], wr=[("mod", L)])
                for j, (part, gname) in enumerate([(1, "norm1_g%d" % L), (4, "norm2_g%d" % L)]):
                    g = self.col(gname, 0, 8)
                    kk.op("dve", nc.vector.scalar_tensor_tensor, rd=[("mod", L), "cols"], wr=[("gA", L)],
                          out=self.gA[:, L, j], in0=self.mod[:, L, part * 8:(part + 1) * 8, :], scalar=1.0,
                          in1=bc(g.unsqueeze(2), [128, 8, 2]), op0=ALU.add, op1=ALU.mult)

    def phase_norm(self, L, which, hT):
        kk = self.k
        nc = kk.nc
        shift_part = 0 if which == 0 else 3
        with kk.scope():
            xt = [kk.sbuf("nx%d" % i, [128, 8, 512]) for i in range(2)]
            sq = kk.sbuf("nsq", [128, 8, 512])
            rstd = kk.sbuf("nrstd", [128, 512])
            tmp = [kk.sbuf("ntmp%d" % i, [128, 512]) for i in range(2)]
            ps = kk.psum("nps", [128, 512])
            for c0, c1 in [(0, 1), (CTX0 + CTXL, LAT0), (NT - 1, NT)]:
                kk.memset(hT[:, :, c0:c1], 0.0, wr=["hT"])
            def ld(ti, t):
                c0, n, s = t
                kk.dma("sp", xt[ti % 2][:, :, :n], self.XT[:, c0:c0 + n].rearrange("(k p) t -> p k t", p=128),
                       rd=["XT"], wr=["nx%d" % (ti % 2)])

            def comp(ti, t):
                c0, n, s = t
                x = xt[ti % 2]
                xk = "nx%d" % (ti % 2)
                kk.act(sq[:, :, :n], x[:, :, :n], AF.Square, rd=[xk], wr=["nsq"])
                for c in range(8):
                    kk.mm(ps[:, :n], self.ones[:, :], sq[:, c, :n], c == 0, c == 7, rd=["nsq", "ones"], wr=["nps"])
                kk.act(rstd[:, :n], ps[:, :n], AF.Sqrt, rd=["nps", "cols"], wr=["nrstd"], bias=self.col("eps"), scale=1.0 / D)
                kk.op("dve", nc.vector.reciprocal, rd=["nrstd"], wr=["nrstd"], out=rstd[:, :n], in_=rstd[:, :n])
                for c in range(8):
                    t_ = tmp[c % 2]
                    tk = "ntmp%d" % (c % 2)
                    kk.tt(t_[:, :n], x[:, c, :n], rstd[:, :n], ALU.mult, rd=[xk, "nrstd"], wr=[tk])
                    kk.act(hT[:, c, c0:c0 + n], t_[:, :n], AF.Identity, rd=[tk, ("gA", L), ("mod", L)], wr=["hT"],
                           scale=self.gA[:, L, which, c, s:s + 1], bias=self.mod[:, L, shift_part * 8 + c, s:s + 1])
            pipelined(TILES, ld, comp)

    def load_w_chunk(self, wdst, wkey, wsrc_cols, stg, stgkey):
        kk = self.k
        kk.dma("sp", stg[:], wsrc_cols.rearrange("(k p) n -> p k n", p=128), rd=[], wr=[stgkey])
        kk.cp(wdst[:], stg[:], rd=[stgkey], wr=[wkey], e="pool")

    def proj_rows(self, hT, W, n_oc, row_fn, bias_name=None):
        kk = self.k
        stg = [kk.sbuf("pstg%d" % i, [128, 8, 128]) for i in range(2)]
        wb = [kk.sbuf("pwb%d" % i, [128, 8, 128], BF16) for i in range(2)]
        rows = [kk.sbuf("prow%d" % i, [128, NT]) for i in range(2)]
        pss = [kk.psum("pps%d" % i, [128, 512]) for i in range(4)]
        for i in range(2):
            kk.memset(rows[i][:, :], 0.0, wr=["prow%d" % i])
        pi = 0

        def ldw(oc):
            self.load_w_chunk(wb[oc % 2], "pwb%d" % (oc % 2), W[:, oc * 128:(oc + 1) * 128], stg[oc % 2], "pstg%d" % (oc % 2))
        ldw(0)
        for oc in range(n_oc):
            w = wb[oc % 2]
            wk = "pwb%d" % (oc % 2)
            if oc + 1 < n_oc:
                ldw(oc + 1)
            row = rows[oc % 2]
            rk = "prow%d" % (oc % 2)
            for (c0, n, s) in TILES:
                ps = pss[pi % 4]
                pk = "pps%d" % (pi % 4)
                pi += 1
                for c in range(8):
                    kk.mm(ps[:, :n], w[:, c, :], hT[:, c, c0:c0 + n], c == 0, c == 7, rd=[wk, "hT"], wr=[pk])
                if bias_name is None:
                    kk.act(row[:, c0:c0 + n], ps[:, :n], AF.Copy, rd=[pk], wr=[rk])
                else:
                    kk.act(row[:, c0:c0 + n], ps[:, :n], AF.Identity, rd=[pk, "cols"], wr=[rk], bias=self.col(bias_name, oc, 1))
            row_fn(oc, row, rk)

    def phase_outproj(self, L, W, bias_name):
        kk = self.k
        nc = kk.nc
        with kk.scope():
            stg = [kk.sbuf("ostg%d" % i, [128, 8, 128]) for i in range(2)]
            wb = kk.sbuf("owb", [128, 8, 8, 128], BF16)
            mt = [kk.sbuf("omt%d" % i, [128, 8, 512]) for i in range(2)]
            mb = [kk.sbuf("omb%d" % i, [128, 8, 512], BF16) for i in range(2)]
            xt = [kk.sbuf("oxt%d" % i, [128, 8, 512]) for i in range(2)]
            pss = [kk.psum("ops%d" % i, [128, 512]) for i in range(4)]
            for oc in range(8):
                kk.dma("sp", stg[oc % 2][:], W[:, oc * 128:(oc + 1) * 128].rearrange("(k p) n -> p k n", p=128),
                       rd=[], wr=["ostg%d" % (oc % 2)])
                kk.cp(wb[:, oc], stg[oc % 2][:], rd=["ostg%d" % (oc % 2)], wr=["owb"], e="pool")
            pi = [0]
            tiles = [t for t in TILES if not (L == DEPTH - 1 and t[2] == 1)]

            def ld(ti, t):
                c0, n, s = t
                kk.dma("sp", mt[ti % 2][:, :, :n], self.MT[:, c0:c0 + n].rearrange("(k p) t -> p k t", p=128), rd=[], wr=["omt%d" % (ti % 2)])
                kk.dma("sp", xt[ti % 2][:, :, :n], self.XT[:, c0:c0 + n].rearrange("(k p) t -> p k t", p=128), rd=[], wr=["oxt%d" % (ti % 2)])

            def comp(ti, t):
                c0, n, s = t
                m, mk_ = mt[ti % 2], "omt%d" % (ti % 2)
                b, bk = mb[ti % 2], "omb%d" % (ti % 2)
                x, xk = xt[ti % 2], "oxt%d" % (ti % 2)
                kk.cp(b[:, :, :n], m[:, :, :n], rd=[mk_], wr=[bk], e="pool")
                for oc in range(8):
                    ps, pk = pss[pi[0] % 4], "ops%d" % (pi[0] % 4)
                    pi[0] += 1
                    for c in range(8):
                        kk.mm(ps[:, :n], wb[:, oc, c, :], b[:, c, :n], c == 0, c == 7, rd=["owb", bk], wr=[pk])
                    if bias_name is not None:
                        kk.act(ps[:, :n], ps[:, :n], AF.Identity, rd=[pk, "cols"], wr=[pk], bias=self.col(bias_name, oc, 1))
                    kk.stt(x[:, oc, :n], ps[:, :n], self.mod[:, L, 2 * 8 + oc, s:s + 1], x[:, oc, :n], ALU.mult, ALU.add,
                           rd=[pk, xk, ("mod", L)], wr=[xk])
                kk.dma("sp", self.XT[:, c0:c0 + n].rearrange("(k p) t -> p k t", p=128), x[:, :, :n], rd=[xk], wr=[("XTo", ti)])
            pipelined(tiles, ld, comp)

    def phase_ffn(self, L):
        kk = self.k
        nc = kk.nc
        last = (L == DEPTH - 1)
        with kk.scope():
            hT = kk.sbuf("hT2", [128, 8, NT], BF16)
            self.phase_norm(L, 1, hT)
            with kk.scope():
                cvt = [kk.sbuf("fcv%d" % i, [128, NT]) for i in range(2)]
                gsb = kk.sbuf("fgs", [128, NT], BF16)
                arow = [kk.sbuf("far%d" % i, [128, NT], BF16) for i in range(2)]
                state = {}

                def conv_row(oc, row, rk, dst, dk):
                    w0 = self.col("ffn_cw%d_0" % L, oc, 1)
                    w1 = self.col("ffn_cw%d_1" % L, oc, 1)
                    w2 = self.col("ffn_cw%d_2" % L, oc, 1)
                    b = self.col("ffn_cb%d" % L, oc, 1)
                    kk.act(dst[:, 1:NT - 1], row[:, 1:NT - 1], AF.Identity, rd=[rk, "cols"], wr=[dk], scale=w1, bias=b)
                    kk.stt(dst[:, 1:NT - 1], row[:, 0:NT - 2], w0, dst[:, 1:NT - 1], ALU.mult, ALU.add,
                           rd=[rk, dk, "cols"], wr=[dk])
                    kk.stt(dst[:, 1:NT - 1], row[:, 2:NT], w2, dst[:, 1:NT - 1], ALU.mult, ALU.add,
                           rd=[rk, dk, "cols"], wr=[dk])

                def row_fn(oi, row, rk):
                    i = oi // 2
                    if oi % 2 == 0:
                        conv_row(i, row, rk, cvt[0], "fcv0")
                        kk.act(gsb[:, :], cvt[0][:, :], AF.Silu, rd=["fcv0"], wr=["fgs"])
                    else:
                        conv_row(22 + i, row, rk, cvt[1], "fcv1")
                        a, ak = arow[i % 2], "far%d" % (i % 2)
                        kk.tt(a[:, :], gsb[:, :], cvt[1][:, :], ALU.mult, rd=["fgs", "fcv1"], wr=[ak])
                        kk.dma("sp", self.AT[i * 128:(i + 1) * 128, :], a[:, :], rd=[ak], wr=["AT"])

                for i in range(2):
                    kk.memset(cvt[i][:, :], 0.0, wr=["fcv%d" % i])
                Wup = self.ffn_up[L]

                class WV:
                    def __getitem__(s_, idx):
                        rows, cs = idx
                        oi = cs.start // 128
                        i = oi // 2
                        col0 = (i if oi % 2 == 0 else 22 + i) * 128
                        return Wup[:, col0:col0 + 128]
                self.proj_rows(hT, WV(), 44, row_fn)
        with kk.scope():
            stg = [kk.sbuf("dstg%d" % i, [128, 1024]) for i in range(2)]
            wd = kk.sbuf("dwd", [128, 22, 1024], BF16)
            at = [kk.sbuf("dat%d" % i, [128, 22, 512], BF16) for i in range(2)]
            xt = [kk.sbuf("dxt%d" % i, [128, 8, 512]) for i in range(2)]
            pss = [kk.psum("dps%d" % i, [128, 512]) for i in range(4)]
            for c in range(22):
                kk.dma("sp", stg[c % 2][:], self.ffn_down[L, c * 128:(c + 1) * 128, :], rd=[], wr=["dstg%d" % (c % 2)])
                kk.cp(wd[:, c, :], stg[c % 2][:], rd=["dstg%d" % (c % 2)], wr=["dwd"], e="pool")
            pi = [0]
            tiles = [t for t in TILES if not (last and t[2] == 1)]

            def ld(ti, t):
                c0, n, s = t
                kk.dma("sp", at[ti % 2][:, :, :n], self.AT[:, c0:c0 + n].rearrange("(k p) t -> p k t", p=128), rd=[], wr=["dat%d" % (ti % 2)])
                kk.dma("sp", xt[ti % 2][:, :, :n], self.XT[:, c0:c0 + n].rearrange("(k p) t -> p k t", p=128), rd=[], wr=["dxt%d" % (ti % 2)])

            def comp(ti, t):
                c0, n, s = t
                a, ak = at[ti % 2], "dat%d" % (ti % 2)
                x, xk = xt[ti % 2], "dxt%d" % (ti % 2)
                for oc in range(8):
                    ps, pk = pss[pi[0] % 4], "dps%d" % (pi[0] % 4)
                    pi[0] += 1
                    for c in range(22):
                        kk.mm(ps[:, :n], wd[:, c, oc * 128:(oc + 1) * 128], a[:, c, :n], c == 0, c == 21,
                              rd=["dwd", ak], wr=[pk])
                    kk.stt(x[:, oc, :n], ps[:, :n], self.mod[:, L, 5 * 8 + oc, s:s + 1], x[:, oc, :n], ALU.mult, ALU.add,
                           rd=[pk, xk, ("mod", L)], wr=[xk])
                if last:
                    kk.dma("sp", self.outT[:, c0 - LAT0:c0 - LAT0 + n].rearrange("(k p) t -> p k t", p=128), x[:, :, :n],
                           rd=[xk], wr=[("outT", ti)])
                else:
                    kk.dma("sp", self.XT[:, c0:c0 + n].rearrange("(k p) t -> p k t", p=128), x[:, :, :n], rd=[xk], wr=[("XTo", ti)])
            pipelined(tiles, ld, comp)


def build_colpack(inp):
    cp = ColPack()
    cp.addvec("eps", np.full(128, 1e-6, np.float32))
    cp.addvec("gneps", np.full(128, 64e-5, np.float32))
    cp.addvec("negpi", np.full(128, -3.14159, np.float32))
    for L in range(DEPTH):
        cp.addvec("ada_b%d" % L, inp["ada_b"][L])
        cp.addvec("norm1_g%d" % L, inp["norm1_g"][L])
        cp.addvec("norm2_g%d" % L, inp["norm2_g"][L])
        for t in range(3):
            cp.addvec("ffn_cw%d_%d" % (L, t), inp["ffn_conv_w"][L, t])
        cp.addvec("ffn_cb%d" % L, inp["ffn_conv_b"][L])
    for j in range(2):
        cp.addvec("mu_prev%d" % j, inp["ev_mu_prev"][j])
        cp.addvec("mu_next%d" % j, inp["ev_mu_next"][j])
        for d in range(2):
            cp.addvec("w0_%d_%d" % (j, d), inp["ev_w0"][j, d])
            cp.addvec("a0_%d_%d" % (j, d), inp["ev_a0"][j, d])
        cp.addvec("k_k%d" % j, inp["ev_k_k"][j])
        cp.addvec("k_a%d" % j, inp["ev_k_a"][j])
        cp.addvec("r_k%d" % j, inp["ev_r_k"][j])
        cp.addvec("ln_w%d" % j, inp["ev_ln_w"][j])
        cp.addvec("ln_b%d" % j, inp["ev_ln_b"][j])
        cp.addvec("q_norm%d" % j, np.tile(inp["ev_q_norm"][j], 2))
        cp.addvec("k_norm%d" % j, np.tile(inp["ev_k_norm"][j], 2))
        cp.add("sink%d" % j, np.tile(inp["ev_sink"][j][None, :], (128, 1)))
        cp.addvec("b_in%d" % j, inp["od_b_in"][j])
        for t in range(3):
            cp.addvec("od_cw%d_%d" % (j, t), inp["od_conv_w"][j, t])
        cp.addvec("od_cb%d" % j, inp["od_conv_b"][j])
        cp.addvec("b_out%d" % j, inp["od_b_out"][j])
        for o in range(2):
            cp.addvec("skip%d_%d" % (j, o), inp["od_skip"][j, o])
        for nm in ("f_b1", "f_b2", "f_b3", "f_freq"):
            cp.addvec("%s%d" % (nm, j), inp["od_" + nm][j])
    return cp


def host_inputs(inp, cp):
    f32 = np.float32
    shared = {
        "cols": cp.array(),
        "ident": np.eye(128, dtype=f32),
    }
    for nm in ("ada_w", "ffn_up", "ffn_down", "ev_w_in", "ev_w_out", "od_w_in", "od_w_out"):
        shared[nm] = np.ascontiguousarray(inp[nm], dtype=f32)
    maps = []
    cc = np.asarray(inp["c_ctx"], f32).reshape(8, 128).T
    for b in range(8):
        m = dict(shared)
        m["xT"] = np.ascontiguousarray(np.asarray(inp["x"][b], f32).T)
        m["ctxT"] = np.ascontiguousarray(np.asarray(inp["ctx"][b], f32).T)
        cb = np.asarray(inp["c"][b], f32).reshape(8, 128).T
        m["condc"] = np.ascontiguousarray(np.concatenate([cb, cc], axis=1))
        maps.append(m)
    return maps


def host_even_extra(maps, inp):
    consts = host_consts()
    rc, rs = host_rope()
    sr = np.zeros((1, 2, 2, 512), np.float32)
    for j in range(2):
        for g in range(2):
            for cb in range(4):
                h = 4 * g + 2 * (cb % 2) + cb // 2
                sr[0, j, g, cb * 128:(cb + 1) * 128] = inp["ev_sink"][j][h]
    for m in maps:
        m["consts"] = consts
        m["rope_cos"] = rc
        m["rope_sin"] = rs
        m["sinkrow"] = sr


def host_rwkv_extra(maps, inp):
    rm = host_rwkv_consts()
    w2 = np.ascontiguousarray(np.asarray(inp["ev_w2"], np.float32).reshape(2, 128, 512))
    a2 = np.ascontiguousarray(np.asarray(inp["ev_a2"], np.float32).reshape(2, 128, 512))
    g2 = np.ascontiguousarray(np.asarray(inp["ev_g2"], np.float32))
    for m in maps:
        m["rmask"] = rm
        m["ev_w2"] = w2
        m["ev_a2"] = a2
        m["ev_g2"] = g2


def host_hy_extra(maps, inp):
    hc = host_hy_consts()
    for m in maps:
        m.update(hc)
        for nm in ("od_f_w1", "od_f_w2", "od_f_w3", "od_f_out", "od_skip"):
            m[nm] = np.ascontiguousarray(inp[nm], dtype=np.float32)
C_BONES, C_RMT, C_SEL0, C_SEL1, C_MASKL, C_MASKR, NCONST = 0, 1, 2, 3, 4, 5, 6


def host_consts():
    c = np.zeros((NCONST, 128, 128), np.float32)
    for p in range(128):
        for q in range(128):
            if p // 64 == q // 64:
                c[C_BONES, p, q] = 1.0
    for d in range(128):
        if d % 64 < 32:
            c[C_RMT, d + 32, d] = -1.0
        else:
            c[C_RMT, d - 32, d] = 1.0
    for g in range(2):
        for p in range(128):
            c[C_SEL0 + g, g * 64 + p % 64, p] = 1.0
    kj = np.arange(128)[:, None]
    qi = np.arange(128)[None, :]
    c[C_MASKL] = (kj >= qi)
    c[C_MASKR] = (kj <= qi)
    return np.ascontiguousarray(c.transpose(1, 0, 2))


def host_rope():
    t = np.arange(SEQ)
    row = (t // 64).astype(np.float32)
    colp = (t % 64).astype(np.float32)
    inv = (np.float32(10000.0) ** (-np.arange(16, dtype=np.float32) / np.float32(16))).astype(np.float32)
    ang = np.concatenate([row[:, None] * inv, colp[:, None] * inv], axis=-1).astype(np.float32)
    cos = np.cos(ang).astype(np.float32)
    sin = np.sin(ang).astype(np.float32)
    idx = np.arange(128) % 32
    return np.ascontiguousarray(cos[:, idx].T), np.ascontiguousarray(sin[:, idx].T)


def even_inputs(P):
    k = P.k
    P.in_names += ["consts", "rope_cos", "rope_sin", "sinkrow"]
    P.constsD = k.dram("consts", [128, NCONST, 128], F32, "ExternalInput")
    P.ropecD = k.dram("rope_cos", [128, SEQ], F32, "ExternalInput")
    P.ropesD = k.dram("rope_sin", [128, SEQ], F32, "ExternalInput")
    P.sinkrowD = k.dram("sinkrow", [1, 2, 2, 512], F32, "ExternalInput")
    P.consts = k.sbuf("consts_sb", [128, NCONST, 128])
    k.dma("sp", P.consts[:], P.constsD[:, :, :], rd=[], wr=["consts"])


def phase_proj_even(P, L):
    kk = P.k
    j = L // 2
    with kk.scope():
        hT = kk.sbuf("hT1", [128, 8, NT], BF16)
        P.phase_norm(L, 0, hT)
        with kk.scope():
            sh = [kk.sbuf("psh%d" % i, [128, NT]) for i in range(2)]
            c0t = kk.sbuf("muc0", [128, 15])
            mp = P.col("mu_prev%d" % j, 0, 15)
            mn = P.col("mu_next%d" % j, 0, 15)
            kk.tt(c0t[:, :], mp, mn, ALU.add, rd=["cols"], wr=["muc0"])
            kk.ts(c0t[:, :], c0t[:, :], -1.0, ALU.mult, rd=["muc0"], wr=["muc0"], s2=1.0, op1=ALU.add)

            def row_fn(oc, row, rk):
                if oc < 15:
                    s_, sk_ = sh[oc % 2], "psh%d" % (oc % 2)
                    kk.act(s_[:, 1:NT - 1], row[:, 1:NT - 1], AF.Identity, rd=[rk, "muc0"], wr=[sk_], scale=c0t[:, oc:oc + 1])
                    kk.stt(s_[:, 1:NT - 1], row[:, 0:NT - 2], P.col("mu_prev%d" % j, oc, 1), s_[:, 1:NT - 1], ALU.mult, ALU.add,
                           rd=[rk, sk_, "cols"], wr=[sk_])
                    kk.stt(s_[:, 1:NT - 1], row[:, 2:NT], P.col("mu_next%d" % j, oc, 1), s_[:, 1:NT - 1], ALU.mult, ALU.add,
                           rd=[rk, sk_, "cols"], wr=[sk_])
                    kk.dma("sp", P.PT[oc * 128:(oc + 1) * 128, 1:NT - 1], s_[:, 1:NT - 1], rd=[sk_], wr=[("PT", oc)])
                else:
                    kk.dma("sp", P.PT[oc * 128:(oc + 1) * 128, :], row[:, :], rd=[rk], wr=[("PT", oc)])
            P.proj_rows(hT, P.ev_w_in[j], 21, row_fn)


def phase_attn(P, L, stop=99):
    kk = P.k
    nc = kk.nc
    j = L // 2
    QOC, KOC, VOC = 15, 19, 20
    with kk.scope():
        qT = kk.sbuf("aqT", [128, 4, NT], BF16)
        kT2 = kk.sbuf("akT2", [128, 2, NT], BF16)
        vtok = kk.sbuf("avtok", [128, 34, 128], BF16)
        ones_bf = kk.sbuf("aones", [128, 64], BF16)
        esink = kk.sbuf("aesink", [1, 2, 512], BF16)
        srow = kk.sbuf("asrow", [1, 2, 512])
        maskb = kk.sbuf("amask", [128, 2, 128], BF16)
        kk.memset(ones_bf[:, :], 1.0, wr=["aones"])
        kk.dma("sp", srow[:], P.sinkrowD[:, j], rd=[], wr=["asrow"])
        kk.act(esink[:], srow[:], AF.Exp, rd=["asrow"], wr=["aesink"])
        kk.cp(maskb[:, :, :], P.consts[:, C_MASKL:C_MASKR + 1, :], rd=["consts"], wr=["amask"])
        with kk.scope():
            raw = [kk.sbuf("araw%d" % i, [128, 512]) for i in range(2)]
            sq = kk.sbuf("asq", [128, 512])
            rstd = kk.sbuf("arstd", [128, 512])
            qn = kk.sbuf("aqn", [128, 512])
            t1 = kk.sbuf("at1", [128, 512])
            t2 = kk.sbuf("at2", [128, 512])
            kf = kk.sbuf("akf", [128, 512])
            cs = [kk.sbuf("acos%d" % i, [128, 512]) for i in range(2)]
            sn = [kk.sbuf("asin%d" % i, [128, 512]) for i in range(2)]
            ps_ss = kk.psum("aps_ss", [128, 512])
            ps_rot = kk.psum("aps_rot", [128, 512])
            ps_sel = [kk.psum("aps_sel%d" % i, [128, 512]) for i in range(2)]
            it = 0
            for ch in range(5):
                isq = ch < 4
                oc = QOC + ch if isq else KOC
                nname = ("q_norm%d" if isq else "k_norm%d") % j
                for ti, (c0, n, s) in enumerate(TILES):
                    r_, rk = raw[it % 2], "araw%d" % (it % 2)
                    cs_, ck = cs[it % 2], "acos%d" % (it % 2)
                    sn_, snk = sn[it % 2], "asin%d" % (it % 2)
                    it += 1
                    kk.dma("sp", r_[:, :n], P.PT[oc * 128:(oc + 1) * 128, c0:c0 + n], rd=[], wr=[rk])
                    if s == 0:
                        kk.dma("sp", cs_[:, :n], P.ropecD[:, c0 - LAT0:c0 - LAT0 + n], rd=[], wr=[ck])
                        kk.dma("sp", sn_[:, :n], P.ropesD[:, c0 - LAT0:c0 - LAT0 + n], rd=[], wr=[snk])
                    kk.act(sq[:, :n], r_[:, :n], AF.Square, rd=[rk], wr=["asq"])
                    kk.mm(ps_ss[:, :n], P.consts[:, C_BONES, :], sq[:, :n], True, True, rd=["asq", "consts"], wr=["aps_ss"])
                    kk.act(rstd[:, :n], ps_ss[:, :n], AF.Sqrt, rd=["aps_ss", "cols"], wr=["arstd"], bias=P.col("eps"), scale=1.0 / 64)
                    kk.op("dve", nc.vector.reciprocal, rd=["arstd"], wr=["arstd"], out=rstd[:, :n], in_=rstd[:, :n])
                    kk.tt(qn[:, :n], r_[:, :n], rstd[:, :n], ALU.mult, rd=[rk, "arstd"], wr=["aqn"])
                    kk.ts(qn[:, :n], qn[:, :n], P.col(nname), ALU.mult, rd=["aqn", "cols"], wr=["aqn"],
                          s2=(0.125 if isq else 1.0), op1=ALU.mult)
                    fin = qn
                    fk = "aqn"
                    if s == 0:
                        kk.mm(ps_rot[:, :n], P.consts[:, C_RMT, :], qn[:, :n], True, True, rd=["aqn", "consts"], wr=["aps_rot"])
                        kk.tt(t1[:, :n], qn[:, :n], cs_[:, :n], ALU.mult, rd=["aqn", ck], wr=["at1"])
                        kk.tt(t2[:, :n], ps_rot[:, :n], sn_[:, :n], ALU.mult, rd=["aps_rot", snk], wr=["at2"])
                        fin = kf
                        fk = "akf"
                        kk.tt(kf[:, :n], t1[:, :n], t2[:, :n], ALU.add, rd=["at1", "at2"], wr=["akf"], e="pool")
                    if isq:
                        kk.act(qT[:, ch, c0:c0 + n], fin[:, :n], AF.Copy, rd=[fk], wr=["aqT"])
                    else:
                        for g in range(2):
                            kk.mm(ps_sel[g][:, :n], P.consts[:, C_SEL0 + g, :], fin[:, :n], True, True, rd=[fk, "consts"],
                                  wr=["aps_sel%d" % g])
                            kk.act(kT2[:, g, c0:c0 + n], ps_sel[g][:, :n], AF.Copy, rd=["aps_sel%d" % g], wr=["akT2"])
            vraw = kk.sbuf("avraw", [128, NT])
            ps_t = [kk.psum("aps_t%d" % i, [128, 128]) for i in range(2)]
            kk.dma("sp", vraw[:, :], P.PT[VOC * 128:(VOC + 1) * 128, :], rd=[], wr=["avraw"])
            for blk in range(34):
                c0 = CTX0 + blk * 128 if blk < 2 else LAT0 + (blk - 2) * 128
                pt, pk = ps_t[blk % 2], "aps_t%d" % (blk % 2)
                kk.op("pe", nc.tensor.transpose, rd=["avraw", "ident"], wr=[pk], out=pt[:, :], in_=vraw[:, c0:c0 + 128],
                      identity=P.ident[:, :])
                kk.act(vtok[:, blk, :], pt[:, :], AF.Copy, rd=[pk], wr=["avtok"])
        if stop < 1:
            return
        with kk.scope():
            NS = 2
            ps_s = [kk.psum("aps_s%d" % i, [128, 2, 512]) for i in range(NS)]
            ps_n = [kk.psum("aps_n%d" % i, [64, 512]) for i in range(2)]
            ps_d = [kk.psum("aps_d%d" % i, [64, 512]) for i in range(2)]
            eb = [kk.sbuf("aeb%d" % i, [128, 512], BF16) for i in range(NS)]
            rden = kk.sbuf("arden", [64, 512])
            osb = [kk.sbuf("aosb%d" % i, [64, 512]) for i in range(2)]
            si = 0
            ui = 0
            qblocks = [(1, i) for i in range(2)] + [(0, i) for i in range(32)]
            for (s, qi) in qblocks[:stop]:
                qc0 = (CTX0 if s == 1 else LAT0) + qi * 128
                kbs = [(0, CTX0, None), (1, CTX0 + 128, None)]
                if s == 0:
                    for dlt, mk in ((-1, 0), (0, None), (1, 1)):
                        kb = qi + dlt
                        if 0 <= kb < 32:
                            kbs.append((2 + kb, LAT0 + kb * 128, mk))
                for g in range(2):
                    pn, pnk = ps_n[ui % 2], "aps_n%d" % (ui % 2)
                    pd, pdk = ps_d[ui % 2], "aps_d%d" % (ui % 2)
                    o_, ok_ = osb[ui % 2], "aosb%d" % (ui % 2)
                    ui += 1
                    for bi, (vb, kc0, mk) in enumerate(kbs):
                        ps, psk = ps_s[si % NS], "aps_s%d" % (si % NS)
                        e_, ek = eb[si % NS], "aeb%d" % (si % NS)
                        si += 1
                        for half in range(2):
                            kk.mm(ps[:, half, 0:256].rearrange("p (a q) -> p a q", a=2),
                                  kT2[half * 64:(half + 1) * 64, g, kc0:kc0 + 128],
                                  qT[half * 64:(half + 1) * 64, 2 * g:2 * g + 2, qc0:qc0 + 128], True, True,
                                  rd=["akT2", "aqT"], wr=[psk])
                        kk.act(e_[:, :].rearrange("p (h c) -> p h c", h=2), ps[:, :, 0:256], AF.Exp, rd=[psk], wr=[ek])
                        if mk is not None:
                            ev = e_[:, :].rearrange("p (a q) -> p a q", a=4)
                            kk.tt(ev, ev, bc(maskb[:, mk, :].unsqueeze(1), [128, 4, 128]), ALU.mult, rd=[ek, "amask"], wr=[ek],
                                  e="pool")
                        first = bi == 0
                        kk.mm(pn[:, :], vtok[:, vb, g * 64:(g + 1) * 64], e_[:, :], first, bi == len(kbs) - 1, rd=["avtok", ek], wr=[pnk])
                        kk.mm(pd[:, :], ones_bf[:, :], e_[:, :], first, False, rd=["aones", ek], wr=[pdk])
                    kk.mm(pd[:, :], ones_bf[0:1, :], esink[0:1, g, :], False, True, rd=["aones", "aesink"], wr=[pdk])
                    kk.op("dve", nc.vector.reciprocal, rd=[pdk], wr=["arden"], out=rden[:, :], in_=pd[:, :])
                    kk.tt(o_[:, :], pn[:, :], rden[:, :], ALU.mult, rd=[pnk, "arden"], wr=[ok_])
                    dst = P.MT[512 + 4 * g * 64:512 + (4 * g + 4) * 64, qc0:qc0 + 128].rearrange(
                        "(b a r) t -> a r b t", a=2, b=2)
                    for a in range(2):
                        kk.dma("sp", dst[a], o_[:, a * 256:(a + 1) * 256].rearrange("p (b t) -> p b t", b=2), rd=[ok_],
                               wr=[("MT", s, qi, g, a)])
CH = 64
NCHUNK = NTOK // CH
R_OC, K_OC, V_OC, WD_OC, AD_OC, GD_OC = 0, 4, 8, 12, 13, 14
M_SU, M_IU, M_SL, M_IL = 0, 1, 2, 3
EXPM05 = math.exp(-0.5)


def host_rwkv_consts():
    r = np.arange(64)[:, None]
    c = np.arange(64)[None, :]
    m = np.stack([(r < c), (r <= c), (r > c), (r >= c)], 0).astype(np.float32)
    return np.ascontiguousarray(m.transpose(1, 0, 2))


def rwkv_inputs(P):
    k = P.k
    P.in_names += ["rmask", "ev_w2", "ev_a2", "ev_g2"]
    P.rmaskD = k.dram("rmask", [64, 4, 64], F32, "ExternalInput")
    P.ev_w2 = k.dram("ev_w2", [2, 128, 512], F32, "ExternalInput")
    P.ev_a2 = k.dram("ev_a2", [2, 128, 512], F32, "ExternalInput")
    P.ev_g2 = k.dram("ev_g2", [2, 128, 512], F32, "ExternalInput")
    P.RQ = P.scr("RQ", [12, 512, NT], BF16)
    P.PCD = P.scr("PCD", [2, 512, NCHUNK])
    P.GD = P.scr("GD", [512, NT])
    P.KS = P.scr("KS", [512, NT])
    P.YD = P.scr("YD", [2, 512, NT])


def chunk_base(c0, s):
    return 0 if s == 1 else 4 + (c0 - LAT0) // CH


def phase_rwkv_prep(P, L):
    kk = P.k
    nc = kk.nc
    j = L // 2
    with kk.scope():
        w2 = kk.sbuf("r_w2", [128, 512])
        a2 = kk.sbuf("r_a2", [128, 512])
        g2 = kk.sbuf("r_g2", [128, 512])
        onesf = kk.sbuf("r_ones", [128, 512])
        kt = kk.sbuf("r_k", [128, 4, 512])
        rt = kk.sbuf("r_r", [128, 4, 512])
        wd = kk.sbuf("r_wd", [128, 512])
        ad = kk.sbuf("r_ad", [128, 512])
        gd = kk.sbuf("r_gd", [128, 512])
        kkr = kk.sbuf("r_kkr", [128, 512])
        sq = kk.sbuf("r_sq", [128, 512])
        nrm = kk.sbuf("r_nrm", [128, 512])
        kkn = kk.sbuf("r_kkn", [128, 512])
        gsb = kk.sbuf("r_gsb", [128, 512])
        sg = kk.sbuf("r_sg", [128, 512])
        csz = kk.sbuf("r_csz", [128, 513])
        av = kk.sbuf("r_a", [128, 512])
        kd = kk.sbuf("r_kd", [128, 512])
        nb = kk.sbuf("r_nb", [128, 512])
        ks = kk.sbuf("r_ks", [128, 512])
        arg = [kk.sbuf("r_arg%d" % i, [128, 512]) for i in range(3)]
        ex = [kk.sbuf("r_ex%d" % i, [128, 512]) for i in range(4)]
        outq = [kk.sbuf("r_out%d" % i, [128, 6, 512], BF16) for i in range(2)]
        pcs = kk.sbuf("r_pcs", [128, 8])
        ps_g = kk.psum("r_psg", [128, 512])
        ps_ss = kk.psum("r_psss", [128, 512])
        ps_w = [kk.psum("r_psw%d" % i, [128, 512]) for i in range(2)]
        ps_a = [kk.psum("r_psa%d" % i, [128, 512]) for i in range(2)]
        kk.dma("sp", w2[:], P.ev_w2[j], rd=[], wr=["r_w2"])
        kk.dma("sp", a2[:], P.ev_a2[j], rd=[], wr=["r_a2"])
        kk.dma("sp", g2[:], P.ev_g2[j], rd=[], wr=["r_g2"])
        kk.memset(onesf[:, :], 1.0, wr=["r_ones"])
        kk.memset(csz[:, 0:1], 0.0, wr=["r_csz"])
        oi = 0
        for (c0, n, s) in TILES:
            nch = n // CH
            cb = chunk_base(c0, s)
            kk.dma("sp", kt[:, :, :n], P.PT[K_OC * 128:(K_OC + 4) * 128, c0:c0 + n].rearrange("(c p) t -> p c t", p=128), rd=[], wr=["r_k"])
            kk.dma("sp", rt[:, :, :n], P.PT[R_OC * 128:(R_OC + 4) * 128, c0:c0 + n].rearrange("(c p) t -> p c t", p=128), rd=[], wr=["r_r"])
            kk.dma("sp", wd[:, :n], P.PT[WD_OC * 128:(WD_OC + 1) * 128, c0:c0 + n], rd=[], wr=["r_wd"])
            kk.dma("sp", ad[:, :n], P.PT[AD_OC * 128:(AD_OC + 1) * 128, c0:c0 + n], rd=[], wr=["r_ad"])
            kk.dma("sp", gd[:, :n], P.PT[GD_OC * 128:(GD_OC + 1) * 128, c0:c0 + n], rd=[], wr=["r_gd"])
            kk.act(wd[:, :n], wd[:, :n], AF.Tanh, rd=["r_wd"], wr=["r_wd"])
            kk.act(gd[:, :n], gd[:, :n], AF.Sigmoid, rd=["r_gd"], wr=["r_gd"])
            for fc in range(4):
                fs = slice(fc * 128, (fc + 1) * 128)
                kk.mm(ps_g[:, :n], g2[:, fs], gd[:, :n], True, True, rd=["r_g2", "r_gd"], wr=["r_psg"])
                kk.act(gsb[:, :n], ps_g[:, :n], AF.Copy, rd=["r_psg"], wr=["r_gsb"])
                kk.dma("sp", P.GD[fs, c0:c0 + n], gsb[:, :n], rd=["r_gsb"], wr=[("GD", fc, c0)])
                kk.ts(kkr[:, :n], kt[:, fc, :n], P.col("k_k%d" % j, fc), ALU.mult, rd=["r_k", "cols"], wr=["r_kkr"])
                kk.act(sq[:, :n], kkr[:, :n], AF.Square, rd=["r_kkr"], wr=["r_sq"])
                kk.mm(ps_ss[:, :n], P.consts[:, C_BONES, :], sq[:, :n], True, True, rd=["r_sq", "consts"], wr=["r_psss"])
                kk.act(nrm[:, :n], ps_ss[:, :n], AF.Sqrt, rd=["r_psss"], wr=["r_nrm"])
                kk.ts(nrm[:, :n], nrm[:, :n], 1e-12, ALU.max, rd=["r_nrm"], wr=["r_nrm"])
                kk.op("dve", nc.vector.reciprocal, rd=["r_nrm"], wr=["r_nrm"], out=nrm[:, :n], in_=nrm[:, :n])
                kk.tt(kkn[:, :n], kkr[:, :n], nrm[:, :n], ALU.mult, rd=["r_kkr", "r_nrm"], wr=["r_kkn"])
                for d in range(2):
                    ds_ = slice(d * 64, (d + 1) * 64)
                    kk.mm(ps_w[d][:, :n], w2[ds_, fs], wd[ds_, :n], True, True, rd=["r_w2", "r_wd"], wr=["r_psw%d" % d])
                    kk.act(sg[:, :n], ps_w[d][:, :n], AF.Sigmoid, rd=["r_psw%d" % d, "cols"], wr=["r_sg"],
                           bias=P.col("w0_%d_%d" % (j, d), fc))
                    kk.ts(sg[:, :n], sg[:, :n], -EXPM05, ALU.mult, rd=["r_sg"], wr=["r_sg"])
                    kk.op("dve", nc.vector.tensor_tensor_scan, rd=["r_sg", "r_ones"], wr=["r_csz"], out=csz[:, 1:1 + n],
                          data0=onesf[:, :n], data1=sg[:, :n], initial=0.0, op0=ALU.mult, op1=ALU.add)
                    kk.mm(ps_a[d][:, :n], a2[ds_, fs], ad[ds_, :n], True, True, rd=["r_a2", "r_ad"], wr=["r_psa%d" % d])
                    kk.act(av[:, :n], ps_a[d][:, :n], AF.Sigmoid, rd=["r_psa%d" % d, "cols"], wr=["r_a"],
                           bias=P.col("a0_%d_%d" % (j, d), fc))
                    kk.ts(kd[:, :n], av[:, :n], -1.0, ALU.add, rd=["r_a", "cols"], wr=["r_kd"], s2=P.col("k_a%d" % j, fc), op1=ALU.mult)
                    kk.stt(kd[:, :n], kd[:, :n], 1.0, kt[:, fc, :n], ALU.add, ALU.mult, rd=["r_kd", "r_k"], wr=["r_kd"])
                    kk.stt(nb[:, :n], kkn[:, :n], -1.0, av[:, :n], ALU.mult, ALU.mult, rd=["r_kkn", "r_a"], wr=["r_nb"])
                    if d == 0:
                        kk.cp(ks[:, :n], kd[:, :n], rd=["r_kd"], wr=["r_ks"], e="pool")
                    else:
                        kk.tt(ks[:, :n], ks[:, :n], kd[:, :n], ALU.add, rd=["r_kd", "r_ks"], wr=["r_ks"], e="pool")
                        kk.dma("sp", P.KS[fs, c0:c0 + n], ks[:, :n], rd=["r_ks"], wr=[("KS", fc, c0)])
                    csv = csz[:, 1:1 + n].rearrange("p (c t) -> p c t", t=CH)
                    csp = csz[:, 0:n].rearrange("p (c t) -> p c t", t=CH)
                    sb_ = bc(csz[:, 0:n:CH].unsqueeze(2), [128, nch, CH])
                    eb_ = bc(csz[:, CH:n + 1:CH].unsqueeze(2), [128, nch, CH])
                    a3 = [arg[i][:, :n].rearrange("p (c t) -> p c t", t=CH) for i in range(3)]
                    if d == 0:
                        kk.tt(a3[0], csv, sb_, ALU.subtract, rd=["r_csz"], wr=["r_arg0"])
                        kk.tt(a3[1], csp, sb_, ALU.subtract, rd=["r_csz"], wr=["r_arg1"])
                        kk.tt(a3[2], eb_, csv, ALU.subtract, rd=["r_csz"], wr=["r_arg2"])
                    else:
                        kk.tt(a3[0], eb_, csp, ALU.subtract, rd=["r_csz"], wr=["r_arg0"])
                        kk.tt(a3[1], eb_, csv, ALU.subtract, rd=["r_csz"], wr=["r_arg1"])
                        kk.tt(a3[2], csp, sb_, ALU.subtract, rd=["r_csz"], wr=["r_arg2"])
                    kk.tt(pcs[:, :nch], csz[:, CH:n + 1:CH], csz[:, 0:n:CH], ALU.subtract, rd=["r_csz"], wr=["r_pcs"])
                    kk.act(pcs[:, :nch], pcs[:, :nch], AF.Exp, rd=["r_pcs"], wr=["r_pcs"])
                    kk.dma("sp", P.PCD[d, fs, cb:cb + nch], pcs[:, :nch], rd=["r_pcs"], wr=[("PCD", d, fc, c0)])
                    kk.act(ex[0][:, :n], arg[0][:, :n], AF.Exp, rd=["r_arg0"], wr=["r_ex0"])
                    kk.act(ex[1][:, :n], arg[1][:, :n], AF.Exp, rd=["r_arg1"], wr=["r_ex1"])
                    kk.act(ex[2][:, :n], arg[0][:, :n], AF.Exp, rd=["r_arg0"], wr=["r_ex2"], scale=-1.0)
                    kk.act(ex[3][:, :n], arg[2][:, :n], AF.Exp, rd=["r_arg2"], wr=["r_ex3"])
                    o_, ok_ = outq[oi % 2], "r_out%d" % (oi % 2)
                    oi += 1
                    kk.tt(o_[:, 0, :n], kkn[:, :n], ex[1][:, :n], ALU.mult, rd=["r_kkn", "r_ex1"], wr=[ok_])
                    kk.tt(o_[:, 1, :n], rt[:, fc, :n], ex[0][:, :n], ALU.mult, rd=["r_r", "r_ex0"], wr=[ok_], e="pool")
                    kk.tt(o_[:, 2, :n], kd[:, :n], ex[2][:, :n], ALU.mult, rd=["r_kd", "r_ex2"], wr=[ok_])
                    kk.tt(o_[:, 3, :n], nb[:, :n], ex[2][:, :n], ALU.mult, rd=["r_nb", "r_ex2"], wr=[ok_], e="pool")
                    kk.tt(o_[:, 4, :n], kd[:, :n], ex[3][:, :n], ALU.mult, rd=["r_kd", "r_ex3"], wr=[ok_])
                    kk.tt(o_[:, 5, :n], nb[:, :n], ex[3][:, :n], ALU.mult, rd=["r_nb", "r_ex3"], wr=[ok_], e="pool")
                    kk.dma("sp", P.RQ[d * 6:(d + 1) * 6, fs, c0:c0 + n].rearrange("q p t -> p q t"), o_[:, :, :n], rd=[ok_],
                           wr=[("RQ", d, fc, c0)])


def chunk_col(cg):
    return CTX0 + cg * CH if cg < 4 else LAT0 + (cg - 4) * CH


def phase_rwkv_main(P, L, nsteps=NCHUNK):
    kk = P.k
    nc = kk.nc
    with kk.scope():
        masks = kk.sbuf("m_masks", [64, 4, 64])
        identb = kk.sbuf("m_identb", [64, 64], BF16)
        PCs = kk.sbuf("m_pcs", [64, 2, 8, NCHUNK])
        H = [kk.sbuf("m_H%d" % d, [64, 8, 64]) for d in range(2)]
        Hb = [kk.sbuf("m_Hb%d" % d, [64, 8, 64], BF16) for d in range(2)]
        QS = [[kk.sbuf("m_QS%d_%d" % (d, p), [64, 6, 8, 128], BF16) for p in range(2)] for d in range(2)]
        VS = [[kk.sbuf("m_VS%d_%d" % (d, p), [64, 8, 128]) for p in range(2)] for d in range(2)]
        vtok = kk.sbuf("m_vtok", [64, 8, 64], BF16)
        katok = kk.sbuf("m_katok", [64, 2, 8, 64], BF16)
        akl = kk.sbuf("m_akl", [64, 8, 2, 64], BF16)
        lra = kk.sbuf("m_lra", [64, 8, 64], BF16)
        X = [kk.sbuf("m_X%d" % i, [64, 8, 64]) for i in range(2)]
        Y = [kk.sbuf("m_Y%d" % i, [64, 8, 64]) for i in range(2)]
        TT = kk.sbuf("m_TT", [64, 8, 64])
        TTb = kk.sbuf("m_TTb", [64, 8, 64], BF16)
        x1 = kk.sbuf("m_x1", [64, 8, 64], BF16)
        usb = kk.sbuf("m_usb", [64, 8, 64], BF16)
        ysb = [kk.sbuf("m_ysb%d" % i, [64, 8, 64]) for i in range(2)]
        htmp = kk.sbuf("m_htmp", [64, 8, 64])
        pb = [kk.psum("m_pb%d" % i, [64, 512]) for i in range(7)]
        ptb = kk.psum("m_ptb", [64, 2, 8, 64], BF16)

        def pbv(i, shape):
            ap = pb[i][:, :]
            if len(shape) == 2:
                return ap.rearrange("p (a b) -> p a b", a=shape[0])
            return ap.rearrange("p (a b c) -> p a b c", a=shape[0], b=shape[1])
        kk.dma("sp", masks[:], P.rmaskD[:, :, :], rd=[], wr=["m_masks"])
        kk.cp(identb[:, :], P.ident[0:64, 0:64], rd=["ident"], wr=["m_identb"])
        for d in range(2):
            kk.dma("sp", PCs[:, d], P.PCD[d].rearrange("(h j) c -> j h c", j=64), rd=[], wr=["m_pcs"])
            kk.memset(H[d][:], 0.0, wr=["m_H%d" % d])
            kk.memset(Hb[d][:], 0.0, wr=["m_Hb%d" % d])
        order = [list(range(NCHUNK)), [3, 2, 1, 0] + list(range(NCHUNK - 1, 3, -1))]
        yi = 0

        def load_group(d, gs):
            cg0 = min(order[d][2 * gs], order[d][2 * gs + 1])
            c0 = chunk_col(cg0)
            par = gs % 2
            for q in range(6):
                kk.dma("sp", QS[d][par][:, q], P.RQ[d * 6 + q].rearrange("(h j) t -> j h t", j=64)[:, :, c0:c0 + 128], rd=[],
                       wr=[("m_QS", d, par)])
            kk.dma("sp", VS[d][par][:], P.PT[V_OC * 128:(V_OC + 4) * 128, :].rearrange("(h j) t -> j h t", j=64)[:, :, c0:c0 + 128],
                   rd=[], wr=[("m_VS", d, par)])
            return cg0

        for d in range(2):
            load_group(d, 0)
        for n in range(nsteps):
            gs = n // 2
            for d in range(2):
                cg = order[d][n]
                if n % 2 == 0 and 2 * (gs + 1) < NCHUNK:
                    load_group(d, gs + 1)
                par = gs % 2
                cg0 = min(order[d][2 * gs], order[d][2 * gs + 1])
                co = (cg - cg0) * CH
                qs, qk = QS[d][par], ("m_QS", d, par)
                vs, vk = VS[d][par], ("m_VS", d, par)
                csl = slice(co, co + CH)
                m_strict, m_incl, m_a = (M_SU, M_IU, M_SL) if d == 0 else (M_SL, M_IL, M_SU)
                for h in range(8):
                    kk.op("pe", nc.tensor.transpose, rd=[vk, "ident"], wr=["m_pb0"], out=pbv(0, [8, 64])[:, h, :], in_=vs[:, h, csl],
                          identity=P.ident[0:64, 0:64])
                kk.act(vtok[:], pbv(0, [8, 64]), AF.Copy, rd=["m_pb0"], wr=["m_vtok"])
                for qi_, q in enumerate((4, 5)):
                    for h in range(8):
                        kk.op("pe", nc.tensor.transpose, rd=[qk, "m_identb"], wr=["m_ptb"], out=ptb[:, qi_, h, :], in_=qs[:, q, h, csl],
                              identity=identb[:, :])
                kk.act(katok[:], ptb[:], AF.Copy, rd=["m_ptb"], wr=["m_katok"])
                for u in range(8):
                    br = qs[:, 0:2, u, csl]
                    kk.mm(pbv(1 + u // 4, [4, 2, 64])[:, u % 4], qs[:, 2, u, csl], br, True, True, rd=[qk], wr=["m_pb%d" % (1 + u // 4)])
                    kk.mm(pbv(3 + u // 4, [4, 2, 64])[:, u % 4], qs[:, 3, u, csl], br, True, True, rd=[qk], wr=["m_pb%d" % (3 + u // 4)])
                    kk.mm(pbv(5, [8, 64])[:, u], qs[:, 0, u, csl], qs[:, 3, u, csl], True, True, rd=[qk], wr=["m_pb5"])
                mk2 = masks[:, m_strict:m_strict + 2, :]
                for hb in range(2):
                    kk.tt(akl[:, hb * 4:(hb + 1) * 4], pbv(1 + hb, [4, 2, 64]), bc(mk2.unsqueeze(1), [64, 4, 2, 64]), ALU.mult,
                          rd=["m_pb%d" % (1 + hb), "m_masks"], wr=["m_akl"])
                    kk.tt(Y[0][:, hb * 4:(hb + 1) * 4], pbv(3 + hb, [4, 2, 64])[:, :, 0, :],
                          bc(masks[:, m_strict, :].unsqueeze(1), [64, 4, 64]), ALU.mult, rd=["m_pb%d" % (3 + hb), "m_masks"], wr=["m_Y0"])
                    kk.tt(lra[:, hb * 4:(hb + 1) * 4], pbv(3 + hb, [4, 2, 64])[:, :, 1, :],
                          bc(masks[:, m_incl, :].unsqueeze(1), [64, 4, 64]), ALU.mult, rd=["m_pb%d" % (3 + hb), "m_masks"], wr=["m_lra"])
                kk.tt(X[0][:], pbv(5, [8, 64]), bc(masks[:, m_a, :].unsqueeze(1), [64, 8, 64]), ALU.mult, rd=["m_pb5", "m_masks"], wr=["m_X0"])
                kk.tt(TT[:], Y[0][:], bc(P.ident[0:64, 0:64].unsqueeze(1), [64, 8, 64]), ALU.add, rd=["m_Y0", "ident"], wr=["m_TT"])
                for lv in range(5):
                    xi, xo = X[lv % 2], X[(lv + 1) % 2]
                    yi_, yo = Y[lv % 2], Y[(lv + 1) % 2]
                    xik, xok = "m_X%d" % (lv % 2), "m_X%d" % ((lv + 1) % 2)
                    yik, yok = "m_Y%d" % (lv % 2), "m_Y%d" % ((lv + 1) % 2)
                    for u in range(8):
                        kk.mm(pbv(1, [8, 64])[:, u], yi_[:, u], xi[:, u], True, True, rd=[xik, yik], wr=["m_pb1"])
                    kk.act(xo[:], pbv(1, [8, 64]), AF.Copy, rd=["m_pb1"], wr=[xok])
                    for u in range(8):
                        if lv < 4:
                            kk.op("pe", nc.tensor.transpose, rd=[xok, "ident"], wr=["m_pb2"], out=pbv(2, [8, 64])[:, u, :], in_=xo[:, u, :],
                                  identity=P.ident[0:64, 0:64])
                        kk.mm(pbv(3, [8, 64])[:, u], xo[:, u], TT[:, u], True, True, rd=[xok, "m_TT"], wr=["m_pb3"])
                    if lv < 4:
                        kk.cp(yo[:], pbv(2, [8, 64]), rd=["m_pb2"], wr=[yok])
                    kk.tt(TT[:], TT[:], pbv(3, [8, 64]), ALU.add, rd=["m_pb3", "m_TT"], wr=["m_TT"])
                kk.cp(TTb[:], TT[:], rd=["m_TT"], wr=["m_TTb"])
                hk, hbk = "m_H%d" % d, "m_Hb%d" % d
                for u in range(8):
                    kk.mm(pbv(4, [8, 64])[:, u], qs[:, 0, u, csl], Hb[d][:, u], True, False, rd=[qk, hbk], wr=["m_pb4"])
                    kk.mm(pbv(4, [8, 64])[:, u], akl[:, u, 0, :], vtok[:, u], False, True, rd=["m_akl", "m_vtok"], wr=["m_pb4"])
                kk.act(x1[:], pbv(4, [8, 64]), AF.Copy, rd=["m_pb4"], wr=["m_x1"])
                for u in range(8):
                    kk.mm(pbv(5, [8, 64])[:, u], TTb[:, u], x1[:, u], True, True, rd=["m_TTb", "m_x1"], wr=["m_pb5"])
                kk.act(usb[:], pbv(5, [8, 64]), AF.Copy, rd=["m_pb5"], wr=["m_usb"])
                for u in range(8):
                    kk.mm(pbv(6, [8, 64])[:, u], Hb[d][:, u], qs[:, 1, u, csl], True, False, rd=[qk, hbk], wr=["m_pb6"])
                    kk.mm(pbv(6, [8, 64])[:, u], vtok[:, u], akl[:, u, 1, :], False, False, rd=["m_akl", "m_vtok"], wr=["m_pb6"])
                    kk.mm(pbv(6, [8, 64])[:, u], usb[:, u], lra[:, u], False, True, rd=["m_usb", "m_lra"], wr=["m_pb6"])
                    kk.mm(pbv(0, [8, 64])[:, u], katok[:, 0, u], vtok[:, u], True, False, rd=["m_katok", "m_vtok"], wr=["m_pb0"])
                    kk.mm(pbv(0, [8, 64])[:, u], katok[:, 1, u], usb[:, u], False, True, rd=["m_katok", "m_usb"], wr=["m_pb0"])
                ys, ysk = ysb[yi % 2], "m_ysb%d" % (yi % 2)
                yi += 1
                kk.act(ys[:], pbv(6, [8, 64]), AF.Copy, rd=["m_pb6"], wr=[ysk])
                c0 = chunk_col(cg)
                kk.dma("sp", P.YD[d].rearrange("(h i) t -> i h t", i=64)[:, :, c0:c0 + CH], ys[:], rd=[ysk], wr=[("YD", d, cg)])
                kk.tt(htmp[:], H[d][:], bc(PCs[:, d, :, cg].unsqueeze(2), [64, 8, 64]), ALU.mult, rd=[hk, "m_pcs"], wr=["m_htmp"])
                kk.tt(H[d][:], htmp[:], pbv(0, [8, 64]), ALU.add, rd=["m_htmp", "m_pb0"], wr=[hk])
                kk.cp(Hb[d][:], H[d][:], rd=[hk], wr=[hbk], e="pool")


def phase_rwkv_out(P, L):
    kk = P.k
    nc = kk.nc
    j = L // 2
    with kk.scope():
        names = ["o_y0", "o_y1", "o_r", "o_v", "o_ks", "o_g"]
        inb = {nm: [kk.sbuf("%s_%d" % (nm, i), [128, 512]) for i in range(2)] for nm in names}
        mean = kk.sbuf("o_mean", [128, 512])
        yc = kk.sbuf("o_yc", [128, 512])
        sq = kk.sbuf("o_sq", [128, 512])
        rstd = kk.sbuf("o_rstd", [128, 512])
        rk = kk.sbuf("o_rk", [128, 512])
        res = [kk.sbuf("o_res%d" % i, [128, 512]) for i in range(2)]
        ps_m = kk.psum("o_psm", [128, 512])
        ps_v = kk.psum("o_psv", [128, 512])
        ps_b = kk.psum("o_psb", [128, 512])
        items = [(c0, n, s, fc) for (c0, n, s) in TILES for fc in range(4)]

        def ld(i, it):
            c0, n, s, fc = it
            fs = slice(fc * 128, (fc + 1) * 128)
            p = i % 2
            srcs = {"o_y0": P.YD[0, fs, c0:c0 + n], "o_y1": P.YD[1, fs, c0:c0 + n],
                    "o_r": P.PT[(R_OC + fc) * 128:(R_OC + fc + 1) * 128, c0:c0 + n],
                    "o_v": P.PT[(V_OC + fc) * 128:(V_OC + fc + 1) * 128, c0:c0 + n],
                    "o_ks": P.KS[fs, c0:c0 + n], "o_g": P.GD[fs, c0:c0 + n]}
            for nm in names:
                kk.dma("sp", inb[nm][p][:, :n], srcs[nm], rd=[], wr=["%s_%d" % (nm, p)])

        def comp(i, it):
            c0, n, s, fc = it
            fs = slice(fc * 128, (fc + 1) * 128)
            p = i % 2
            y0, y1, rr, vv, ksb, gg = [inb[nm][p] for nm in names]
            k0, k1, kr, kv, kks, kg = ["%s_%d" % (nm, p) for nm in names]
            kk.tt(y0[:, :n], y0[:, :n], y1[:, :n], ALU.add, rd=[k0, k1], wr=[k0])
            kk.mm(ps_m[:, :n], P.consts[:, C_BONES, :], y0[:, :n], True, True, rd=[k0, "consts"], wr=["o_psm"])
            kk.stt(yc[:, :n], ps_m[:, :n], -1.0 / 64, y0[:, :n], ALU.mult, ALU.add, rd=["o_psm", k0], wr=["o_yc"])
            kk.act(sq[:, :n], yc[:, :n], AF.Square, rd=["o_yc"], wr=["o_sq"])
            kk.mm(ps_v[:, :n], P.consts[:, C_BONES, :], sq[:, :n], True, True, rd=["o_sq", "consts"], wr=["o_psv"])
            kk.act(rstd[:, :n], ps_v[:, :n], AF.Sqrt, rd=["o_psv", "cols"], wr=["o_rstd"], bias=P.col("gneps"), scale=1.0 / 64)
            kk.op("dve", nc.vector.reciprocal, rd=["o_rstd"], wr=["o_rstd"], out=rstd[:, :n], in_=rstd[:, :n])
            kk.tt(yc[:, :n], yc[:, :n], rstd[:, :n], ALU.mult, rd=["o_yc", "o_rstd"], wr=["o_yc"])
            kk.ts(yc[:, :n], yc[:, :n], P.col("ln_w%d" % j, fc), ALU.mult, rd=["o_yc", "cols"], wr=["o_yc"],
                  s2=P.col("ln_b%d" % j, fc), op1=ALU.add)
            kk.stt(rk[:, :n], rr[:, :n], P.col("r_k%d" % j, fc), ksb[:, :n], ALU.mult, ALU.mult, rd=[kr, kks, "cols"], wr=["o_rk"])
            kk.mm(ps_b[:, :n], P.consts[:, C_BONES, :], rk[:, :n], True, True, rd=["o_rk", "consts"], wr=["o_psb"])
            kk.tt(rk[:, :n], ps_b[:, :n], vv[:, :n], ALU.mult, rd=["o_psb", kv], wr=["o_rk"])
            kk.tt(yc[:, :n], yc[:, :n], rk[:, :n], ALU.add, rd=["o_yc", "o_rk"], wr=["o_yc"], e="pool")
            r_, rk_ = res[i % 2], "o_res%d" % (i % 2)
            kk.tt(r_[:, :n], yc[:, :n], gg[:, :n], ALU.mult, rd=["o_yc", kg], wr=[rk_], e="pool")
            kk.dma("sp", P.MT[fs, c0:c0 + n], r_[:, :n], rd=[rk_], wr=[("MT", fc, c0)])
        pipelined(items, ld, comp)
NFFT = 2 * SEQ
CG = 4


def host_hy_consts():
    f32 = np.float32
    t1 = np.arange(64)[:, None]
    f1 = np.arange(64)[None, :]
    ph1 = 2 * np.pi * (t1 * f1 % 64) / 64.0
    F1 = np.concatenate([np.cos(ph1), -np.sin(ph1)], 1).astype(f32)
    t2 = np.arange(128)[:, None]
    tw = 2 * np.pi * (t2 * f1) / float(NFFT)
    TW1 = np.stack([np.cos(tw), -np.sin(tw)], 1).astype(f32)
    f2 = np.arange(128)[None, :]
    th = 2 * np.pi * (t2 * f2 % 128) / 128.0
    C2, S2 = np.cos(th), np.sin(th)
    F2 = np.stack([C2, S2, -S2, C2], 1).astype(f32)
    tw2 = 2 * np.pi * (np.arange(64)[:, None] * np.arange(128)[None, :]) / float(NFFT)
    TW2 = np.stack([np.cos(tw2), np.sin(tw2)], 1).astype(f32)
    ph = 2 * np.pi * (np.arange(64)[:, None] * np.arange(32)[None, :] % 64) / 64.0
    FI = np.stack([np.cos(ph) / NFFT, -np.sin(ph) / NFFT], 1).astype(f32)
    out = {"hyF1": F1, "hyTW1": TW1, "hyF2": F2, "hyTW2": TW2, "hyFI": FI}
    for n, nm in ((SEQ, "lat"), (CTXL, "ctx")):
        t = np.linspace(0.0, 1.0, n, dtype=f32)[:, None]
        ang = (f32(2.0 * math.pi) * np.arange(n, dtype=f32)[:, None] / f32(n)).astype(f32)
        f = np.linspace(1e-4, 15, 16, dtype=f32)[None, :]
        z = np.concatenate([t, np.cos(f * ang), -np.sin(f * ang)], -1).astype(f32)
        out["hyZ_" + nm] = np.ascontiguousarray(z.T)
        out["hyT_" + nm] = np.ascontiguousarray(np.tile(t.T, (128, 1)))
    zl = out["hyZ_lat"]
    tl = out["hyT_lat"]
    idx = (SEQ - np.arange(SEQ)) % SEQ
    out["hyZ_rev"] = np.ascontiguousarray(zl[:, idx])
    out["hyT_rev"] = np.ascontiguousarray(tl[:, idx])
    deltas = np.abs(np.linspace(math.log(1e-2) / 1.5, math.log(1e-2) / 0.3, D, dtype=f32)).astype(f32)
    out["hyDcol"] = np.ascontiguousarray(-deltas.reshape(8, 128).T)
    out["hyDrow"] = np.ascontiguousarray(np.tile(deltas[None, :], (128, 1)))
    tc = np.linspace(0.0, 1.0, CTXL, dtype=f32)
    out["hyTcol"] = np.ascontiguousarray(-tc.reshape(2, 128).T)
    tt_ = np.arange(256)[:, None]
    ff_ = np.arange(512)[None, :]
    a = 2 * np.pi * (tt_ * ff_ % 512) / 512.0
    out["hyCF"] = np.stack([np.cos(a), -np.sin(a)], 0).astype(f32).reshape(2, 2, 128, 512)
    out["hyCI"] = np.stack([np.cos(a.T) / 512.0, -np.sin(a.T) / 512.0], 0).astype(f32).reshape(2, 4, 128, 256)
    return out


HY_SHAPES = {"hyF1": [64, 128], "hyTW1": [128, 2, 64], "hyF2": [128, 4, 128], "hyTW2": [64, 2, 128], "hyFI": [64, 2, 32],
             "hyZ_lat": [33, SEQ], "hyT_lat": [128, SEQ], "hyZ_rev": [33, SEQ], "hyT_rev": [128, SEQ], "hyZ_ctx": [33, CTXL], "hyT_ctx": [128, CTXL],
             "hyDcol": [128, 8], "hyDrow": [128, D], "hyTcol": [128, 2], "hyCF": [2, 2, 128, 512], "hyCI": [2, 4, 128, 256]
             }


def hy_inputs(P):
    k = P.k
    P.hy = {}
    for nm, shp in HY_SHAPES.items():
        P.in_names.append(nm)
        P.hy[nm] = k.dram(nm, shp, F32, "ExternalInput")
    for nm, shp in (("od_f_w1", [2, 33, 64]), ("od_f_w2", [2, 64, 64]), ("od_f_w3", [2, 64, 64]), ("od_f_out", [2, 64, 4096]),
                    ("od_skip", [2, 2, D])):
        P.in_names.append(nm)
        P.hy[nm] = k.dram(nm, shp, F32, "ExternalInput")
    P.HFD = P.scr("HFD", [2 * D, NFFT])
    P.KF = P.scr("KF", [2, 128, D, 2, 64], BF16)
    P.HFC = P.scr("HFC", [CTXL, 4 * D])


def phase_hy_proj(P, L):
    kk = P.k
    j = L // 2
    with kk.scope():
        hT = kk.sbuf("hT1", [128, 8, NT], BF16)
        P.phase_norm(L, 0, hT)
        with kk.scope():
            cv = [kk.sbuf("hcv%d" % i, [128, NT]) for i in range(2)]

            def row_fn(oc, row, rk):
                dst, dk = cv[oc % 2], "hcv%d" % (oc % 2)
                w0 = P.col("od_cw%d_0" % j, oc)
                w1 = P.col("od_cw%d_1" % j, oc)
                w2 = P.col("od_cw%d_2" % j, oc)
                b = P.col("od_cb%d" % j, oc)
                kk.act(dst[:, 1:NT - 1], row[:, 1:NT - 1], AF.Identity, rd=[rk, "cols"], wr=[dk], scale=w1, bias=b)
                kk.stt(dst[:, 1:NT - 1], row[:, 0:NT - 2], w0, dst[:, 1:NT - 1], ALU.mult, ALU.add, rd=[rk, dk, "cols"], wr=[dk])
                kk.stt(dst[:, 1:NT - 1], row[:, 2:NT], w2, dst[:, 1:NT - 1], ALU.mult, ALU.add, rd=[rk, dk, "cols"], wr=[dk])
                kk.dma("sp", P.PT[oc * 128:(oc + 1) * 128, 1:NT - 1], dst[:, 1:NT - 1], rd=[dk], wr=[("PT", oc)])
            P.proj_rows(hT, P.od_w_in[j], 24, row_fn, bias_name="b_in%d" % j)


def hy_mlp(P, j, nm, n, sink_fn):
    kk = P.k
    nc = kk.nc
    w1 = kk.sbuf("f_w1", [33, 64])
    w2 = kk.sbuf("f_w2", [64, 64])
    w3 = kk.sbuf("f_w3", [64, 64])
    kk.dma("sp", w1[:], P.hy["od_f_w1"][j], rd=[], wr=["f_w1"])
    kk.dma("sp", w2[:], P.hy["od_f_w2"][j], rd=[], wr=["f_w2"])
    kk.dma("sp", w3[:], P.hy["od_f_w3"][j], rd=[], wr=["f_w3"])
    zt = kk.sbuf("f_zt", [33, 512])
    hb = [kk.sbuf("f_h%d" % i, [64, 512]) for i in range(2)]
    r = kk.sbuf("f_r", [64, 512])
    ri = kk.sbuf("f_ri", [64, 512], mybir.dt.int32)
    rf = kk.sbuf("f_rf", [64, 512])
    msk = kk.sbuf("f_m", [64, 512])
    ps = kk.psum("f_ps", [64, 512])
    freq = P.col("f_freq%d" % j)[0:64, :]
    KOFF = 64.5

    def sin_layer(src_ps, bname, dst, dk, nt):
        b = P.col(bname)[0:64, :]
        kk.ts(r[:, :nt], src_ps[:, :nt], b, ALU.add, rd=["f_ps", "cols"], wr=["f_r"], s2=freq, op1=ALU.mult)
        kk.ts(r[:, :nt], r[:, :nt], 1.0 / (2 * math.pi), ALU.mult, rd=["f_r"], wr=["f_r"], s2=KOFF, op1=ALU.add)
        kk.cp(ri[:, :nt], r[:, :nt], rd=["f_r"], wr=["f_ri"])
        kk.cp(rf[:, :nt], ri[:, :nt], rd=["f_ri"], wr=["f_rf"])
        kk.tt(r[:, :nt], r[:, :nt], rf[:, :nt], ALU.subtract, rd=["f_r", "f_rf"], wr=["f_r"])
        kk.ts(msk[:, :nt], r[:, :nt], 0.0, ALU.is_lt, rd=["f_r"], wr=["f_m"])
        kk.tt(r[:, :nt], r[:, :nt], msk[:, :nt], ALU.add, rd=["f_r", "f_m"], wr=["f_r"])
        kk.act(dst[:, :nt], r[:, :nt], AF.Sin, rd=["f_r", "cols"], wr=[dk], scale=6.28318, bias=P.col("negpi")[0:64, :])

    ti = 0
    for c0 in range(0, n, 512):
        nt = min(512, n - c0)
        kk.dma("sp", zt[:, :nt], P.hy["hyZ_" + nm][:, c0:c0 + nt], rd=[], wr=["f_zt"])
        kk.mm(ps[:, :nt], w1[:, :], zt[:, :nt], True, True, rd=["f_w1", "f_zt"], wr=["f_ps"])
        sin_layer(ps, "f_b1%d" % j, hb[0], "f_h0", nt)
        kk.mm(ps[:, :nt], w2[:, :], hb[0][:, :nt], True, True, rd=["f_w2", "f_h0"], wr=["f_ps"])
        sin_layer(ps, "f_b2%d" % j, hb[1], "f_h1", nt)
        kk.mm(ps[:, :nt], w3[:, :], hb[1][:, :nt], True, True, rd=["f_w3", "f_h1"], wr=["f_ps"])
        sin_layer(ps, "f_b3%d" % j, hb[0], "f_h0", nt)
        sink_fn(ti, c0, nt, hb[0], "f_h0")
        ti += 1


def phase_hy_filter_lat(P, L):
    kk = P.k
    j = L // 2
    with kk.scope():
        fo = kk.sbuf("f_out", [64, 4096])
        tl = kk.sbuf("f_tl", [128, 512])
        win = kk.sbuf("f_win", [128, 8, 512])
        dcol = kk.sbuf("f_dcol", [128, 8])
        hb0 = kk.sbuf("f_hb0", [128, 16])
        ob = [kk.sbuf("f_ob%d" % i, [128, 512]) for i in range(2)]
        pso = [kk.psum("f_pso%d" % i, [128, 512]) for i in range(2)]
        kk.dma("sp", fo[:], P.hy["od_f_out"][j], rd=[], wr=["f_out"])
        kk.dma("sp", dcol[:], P.hy["hyDcol"][:, :], rd=[], wr=["f_dcol"])
        cnt = [0]

        def make_sink(mode):
            tname = "hyT_rev" if mode == 2 else "hyT_lat"

            def sink(ti, c0, nt, h3, hk):
                kk.dma("sp", tl[:, :nt], P.hy[tname][:, c0:c0 + nt], rd=[], wr=["f_tl"])
                for cc in range(8):
                    kk.act(win[:, cc, :nt], tl[:, :nt], AF.Exp, rd=["f_tl", "f_dcol"], wr=["f_win"], scale=dcol[:, cc:cc + 1])
                for o in range(2):
                    for cc in range(8):
                        oc = (o * 2 + (1 if mode != 1 else 0)) * 8 + cc
                        i = cnt[0]
                        cnt[0] += 1
                        ps, pk = pso[i % 2], "f_pso%d" % (i % 2)
                        o_, ok_ = ob[i % 2], "f_ob%d" % (i % 2)
                        kk.mm(ps[:, :nt], fo[:, oc * 128:(oc + 1) * 128], h3[:, :nt], True, True, rd=["f_out", hk], wr=[pk])
                        if mode == 0:
                            kk.stt(hb0[:, o * 8 + cc:o * 8 + cc + 1], win[:, cc, 0:1], 0.05, ps[:, 0:1], ALU.add, ALU.mult,
                                   rd=["f_win", pk], wr=["f_hb0"])
                            continue
                        kk.stt(o_[:, :nt], win[:, cc, :nt], 0.05, ps[:, :nt], ALU.add, ALU.mult, rd=["f_win", pk], wr=[ok_])
                        if c0 == 0 and mode == 1:
                            kk.tt(o_[:, 0:1], o_[:, 0:1], hb0[:, o * 8 + cc:o * 8 + cc + 1], ALU.add, rd=[ok_, "f_hb0"], wr=[ok_])
                            kk.tt(o_[:, 0:1], o_[:, 0:1], P.col("skip%d_%d" % (j, o), cc), ALU.add, rd=[ok_, "cols"], wr=[ok_])
                        if c0 == 0 and mode == 2:
                            kk.memset(o_[:, 0:1], 0.0, wr=[ok_], e="dve")
                        col0 = c0 if mode == 1 else SEQ + c0
                        kk.dma("sp", P.HFD[(o * 8 + cc) * 128:(o * 8 + cc + 1) * 128, col0:col0 + nt], o_[:, :nt], rd=[ok_],
                               wr=[("HFD", mode, o, cc, c0)])
            return sink
        with kk.scope():
            hy_mlp(P, j, "lat", 2, make_sink(0))
        with kk.scope():
            hy_mlp(P, j, "lat", SEQ, make_sink(1))
        with kk.scope():
            hy_mlp(P, j, "rev", SEQ, make_sink(2))


class HyFFT:
    def __init__(self, P, C, tag, ka=32, fwd_only=False):
        kk = P.k
        self.P = P
        self.ka = ka
        self.C = C
        self.tag = tag
        g = tag
        self.psA = kk.psum("y_psA" + g, [128, CG, 128])
        self.psX = kk.psum("y_psX" + g, [128, 2, CG, 64])
        self.tmp = [kk.sbuf("y_tmp%d%s" % (i, g), [128, CG, 64]) for i in range(4)]
        self.Ap = kk.sbuf("y_Ap" + g, [128, CG, 2, 64], BF16)
        if not fwd_only:
            self.psB = kk.psum("y_psB" + g, [64, CG, 2, 128])
            self.tmpB = [kk.sbuf("y_tmpB%d%s" % (i, g), [64, CG, 128]) for i in range(4)]
            self.Z = kk.sbuf("y_Z" + g, [128, CG, 2, 64], BF16)
            self.Bp = kk.sbuf("y_Bp" + g, [64, CG, 2, 128], BF16)
        self.ub = kk.sbuf("y_ub" + g, [ka, CG, 128], BF16)
        self.k = lambda nm: nm + g

    def cmul(self, out_re, out_im, ar, ai, br, bi, tmps, tkeys, rd, wr):
        kk = self.P.k
        kk.tt(tmps[0], ar, br, ALU.mult, rd=rd, wr=[tkeys[0]])
        kk.tt(tmps[1], ai, bi, ALU.mult, rd=rd, wr=[tkeys[1]])
        kk.tt(tmps[2], ar, bi, ALU.mult, rd=rd, wr=[tkeys[2]])
        kk.tt(tmps[3], ai, br, ALU.mult, rd=rd, wr=[tkeys[3]])
        kk.tt(out_re, tmps[0], tmps[1], ALU.subtract, rd=[tkeys[0], tkeys[1]], wr=wr, e="pool")
        kk.tt(out_im, tmps[2], tmps[3], ALU.add, rd=[tkeys[2], tkeys[3]], wr=wr, e="pool")

    def s_cast(self, src, srck):
        self.P.k.act(self.ub[:], src, AF.Copy, rd=[srck], wr=[self.k("y_ub")])

    def s_A(self):
        kk, C = self.P.k, self.C
        for ch in range(CG):
            kk.mm(self.psA[:, ch, :], self.ub[:, ch, :], C["F1"][0:self.ka, :], True, True, rd=[self.k("y_ub"), "y_F1"], wr=[self.k("y_psA")])

    def s_tw1(self):
        C = self.C
        t = [x[:] for x in self.tmp]
        tk = [self.k("y_tmp%d" % i) for i in range(4)]
        self.cmul(self.Ap[:, :, 0, :], self.Ap[:, :, 1, :], self.psA[:, :, 0:64], self.psA[:, :, 64:128],
                  bc(C["TW1"][:, 0, :].unsqueeze(1), [128, CG, 64]), bc(C["TW1"][:, 1, :].unsqueeze(1), [128, CG, 64]),
                  t, tk, rd=[self.k("y_psA"), "y_TW1"], wr=[self.k("y_Ap")])

    def s_C(self):
        kk, C = self.P.k, self.C
        are, aim = self.Ap[:, :, 0, :], self.Ap[:, :, 1, :]
        rd = [self.k("y_Ap"), "y_F2"]
        kk.mm(self.psX[:, 0], C["F2"][:, 0, :], are, True, False, rd=rd, wr=[self.k("y_psX")])
        kk.mm(self.psX[:, 0], C["F2"][:, 1, :], aim, False, True, rd=rd, wr=[self.k("y_psX")])
        kk.mm(self.psX[:, 1], C["F2"][:, 2, :], are, True, False, rd=rd, wr=[self.k("y_psX")])
        kk.mm(self.psX[:, 1], C["F2"][:, 0, :], aim, False, True, rd=rd, wr=[self.k("y_psX")])

    def s_Z(self, kf, kfk):
        t = [x[:] for x in self.tmp]
        tk = [self.k("y_tmp%d" % i) for i in range(4)]
        self.cmul(self.Z[:, :, 0, :], self.Z[:, :, 1, :], self.psX[:, 0], self.psX[:, 1], kf[:, :, 0, :], kf[:, :, 1, :], t, tk,
                  rd=[self.k("y_psX"), kfk], wr=[self.k("y_Z")])

    def s_Cp(self):
        kk, C = self.P.k, self.C
        for ch in range(CG):
            kk.mm(self.psB[:, ch], self.Z[:, ch, 0, :], C["F2"][:, 0:2, :], True, False, rd=[self.k("y_Z"), "y_F2"], wr=[self.k("y_psB")])
            kk.mm(self.psB[:, ch], self.Z[:, ch, 1, :], C["F2"][:, 2:4, :], False, True, rd=[self.k("y_Z"), "y_F2"], wr=[self.k("y_psB")])

    def s_tw2(self):
        C = self.C
        tb = [x[:] for x in self.tmpB]
        tbk = [self.k("y_tmpB%d" % i) for i in range(4)]
        self.cmul(self.Bp[:, :, 0, :], self.Bp[:, :, 1, :], self.psB[:, :, 0, :], self.psB[:, :, 1, :],
                  bc(C["TW2"][:, 0, :].unsqueeze(1), [64, CG, 128]), bc(C["TW2"][:, 1, :].unsqueeze(1), [64, CG, 128]),
                  tb, tbk, rd=[self.k("y_psB"), "y_TW2"], wr=[self.k("y_Bp")])

    def s_Ap(self):
        kk, C = self.P.k, self.C
        kk.mm(self.psA[0:32, :, :], C["FI"][:, 0, :], self.Bp[:, :, 0, :], True, False, rd=[self.k("y_Bp"), "y_FI"], wr=[self.k("y_psA")])
        kk.mm(self.psA[0:32, :, :], C["FI"][:, 1, :], self.Bp[:, :, 1, :], False, True, rd=[self.k("y_Bp"), "y_FI"], wr=[self.k("y_psA")])


def hy_consts(P):
    kk = P.k
    C = {}
    f1 = kk.sbuf("y_f1s", [64, 128])
    C["F1"] = kk.sbuf("y_F1", [64, 128], BF16)
    f2 = kk.sbuf("y_f2s", [128, 4, 128])
    C["F2"] = kk.sbuf("y_F2", [128, 4, 128], BF16)
    fi = kk.sbuf("y_fis", [64, 2, 32])
    C["FI"] = kk.sbuf("y_FI", [64, 2, 32], BF16)
    C["TW1"] = kk.sbuf("y_TW1", [128, 2, 64])
    C["TW2"] = kk.sbuf("y_TW2", [64, 2, 128])
    kk.dma("sp", f1[:], P.hy["hyF1"][:, :], rd=[], wr=["y_f1s"])
    kk.dma("sp", f2[:], P.hy["hyF2"][:, :, :], rd=[], wr=["y_f2s"])
    kk.dma("sp", fi[:], P.hy["hyFI"][:, :, :], rd=[], wr=["y_fis"])
    kk.dma("sp", C["TW1"][:], P.hy["hyTW1"][:, :, :], rd=[], wr=["y_TW1"])
    kk.dma("sp", C["TW2"][:], P.hy["hyTW2"][:, :, :], rd=[], wr=["y_TW2"])
    kk.cp(C["F1"][:], f1[:], rd=["y_f1s"], wr=["y_F1"])
    kk.cp(C["F2"][:], f2[:], rd=["y_f2s"], wr=["y_F2"])
    kk.cp(C["FI"][:], fi[:], rd=["y_fis"], wr=["y_FI"])
    return C


def emit_skewed(seq0, seq1, lag):
    n = max(len(seq0), len(seq1) + lag)
    for k in range(n):
        if k < len(seq0):
            seq0[k]()
        if 0 <= k - lag < len(seq1):
            seq1[k - lag]()


def phase_hy_kf(P, L, ngroups=D // CG):
    kk = P.k
    with kk.scope():
        C = hy_consts(P)
        S = [HyFFT(P, C, "_s%d" % i, ka=64, fwd_only=True) for i in range(2)]
        us = [[kk.sbuf("k_us%d_%d" % (st, i), [64, CG, 128]) for i in range(2)] for st in range(2)]
        ko = [[kk.sbuf("k_ko%d_%d" % (st, i), [128, CG, 2, 64], BF16) for i in range(2)] for st in range(2)]
        items = [(o, g) for o in range(2) for g in range(0, ngroups, 2)]

        def ld(i, st):
            o, g = items[i]
            row0 = o * D + (g + st) * CG
            kk.dma("sp", us[st][i % 2][:], P.HFD[row0:row0 + CG, :].rearrange("c (a b) -> a c b", b=128), rd=[],
                   wr=["k_us%d_%d" % (st, i % 2)])

        def out(i, st):
            o, g = items[i]
            k_, kk_ = ko[st][i % 2], "k_ko%d_%d" % (st, i % 2)
            kk.act(k_[:].rearrange("p c r f -> p r c f"), S[st].psX[:], AF.Copy, rd=[S[st].k("y_psX")], wr=[kk_])
            kk.dma("sp", P.KF[o, :, (g + st) * CG:(g + st + 1) * CG], k_[:], rd=[kk_], wr=[("KF", o, g + st)])
        seqs = [[], []]
        for st in range(2):
            ld(0, st)
        for i in range(len(items)):
            for st in range(2):
                q = seqs[st]
                if i + 1 < len(items):
                    q.append(lambda i=i, st=st: ld(i + 1, st))
                else:
                    q.append(lambda: None)
                q.append(lambda i=i, st=st: S[st].s_cast(us[st][i % 2][:], "k_us%d_%d" % (st, i % 2)))
                q.append(lambda st=st: S[st].s_A())
                q.append(lambda st=st: S[st].s_tw1())
                q.append(lambda st=st: S[st].s_C())
                q.append(lambda i=i, st=st: out(i, st))
        emit_skewed(seqs[0], seqs[1], 0)


def kf_side(P, L, ngroups=D // CG):
    kk = P.k
    C = hy_consts(P)
    S = HyFFT(P, C, "_k", ka=64, fwd_only=True)
    us = [kk.sbuf("k_us%d" % i, [64, CG, 128]) for i in range(2)]
    ko = [kk.sbuf("k_ko%d" % i, [128, CG, 2, 64], BF16) for i in range(2)]
    items = [(o, g) for o in range(2) for g in range(ngroups)]

    def ld(i, it):
        o, g = it
        row0 = o * D + g * CG
        kk.dma("sp", us[i % 2][:], P.HFD[row0:row0 + CG, :].rearrange("c (a b) -> a c b", b=128), rd=[], wr=["k_us%d" % (i % 2)])

    def comp(i, it):
        o, g = it
        S.s_cast(us[i % 2][:], "k_us%d" % (i % 2))
        S.s_A()
        S.s_tw1()
        S.s_C()
        k_, kk_ = ko[i % 2], "k_ko%d" % (i % 2)
        kk.act(k_[:].rearrange("p c r f -> p r c f"), S.psX[:], AF.Copy, rd=[S.k("y_psX")], wr=[kk_])
        kk.dma("sp", P.KF[o, :, g * CG:(g + 1) * CG], k_[:], rd=[kk_], wr=[("KF", o, g)])
    ld(0, items[0])
    for i, it in enumerate(items):
        if i + 1 < len(items):
            ld(i + 1, items[i + 1])
        comp(i, it)
        yield


def phase_hy_conv(P, L, ngroups=D // CG):
    kk = P.k
    with kk.scope():
        C = hy_consts(P)
        S = [HyFFT(P, C, "_s%d" % i) for i in range(2)]
        vx = [[[kk.sbuf("c_vx%d_%d_%d" % (q, st, i), [32, CG, 128]) for i in range(2)] for st in range(2)] for q in range(3)]
        kf = [[[kk.sbuf("c_kf%d_%d_%d" % (o, st, i), [128, CG, 2, 64], BF16) for i in range(2)] for st in range(2)] for o in range(2)]
        y1 = [kk.sbuf("c_y1_%d" % st, [32, CG, 128]) for st in range(2)]
        y2 = [[kk.sbuf("c_y2_%d_%d" % (st, i), [32, CG, 128]) for i in range(2)] for st in range(2)]
        items = list(range(0, ngroups, 2))

        def ld(i, st):
            c0 = (items[i] + st) * CG
            for q in range(3):
                kk.dma("sp", vx[q][st][i % 2][:], P.PT[q * D + c0:q * D + c0 + CG, LAT0:LAT0 + SEQ].rearrange("c (a b) -> a c b", b=128),
                       rd=[], wr=["c_vx%d_%d_%d" % (q, st, i % 2)])
            for o in range(2):
                kk.dma("sp", kf[o][st][i % 2][:], P.KF[o, :, c0:c0 + CG], rd=[], wr=["c_kf%d_%d_%d" % (o, st, i % 2)])

        def cast(i, st, o):
            p = i % 2
            if o == 0:
                S[st].s_cast(vx[0][st][p][:], "c_vx0_%d_%d" % (st, p))
            else:
                S[st].s_cast(y1[st][:], "c_y1_%d" % st)

        def gate(i, st, o):
            p = i % 2
            if o == 0:
                kk.tt(y1[st][:], S[st].psA[0:32, :, :], vx[1][st][p][:], ALU.mult,
                      rd=[S[st].k("y_psA"), "c_vx1_%d_%d" % (st, p)], wr=["c_y1_%d" % st])
            else:
                o_, ok_ = y2[st][p], "c_y2_%d_%d" % (st, p)
                kk.tt(o_[:], S[st].psA[0:32, :, :], vx[2][st][p][:], ALU.mult,
                      rd=[S[st].k("y_psA"), "c_vx2_%d_%d" % (st, p)], wr=[ok_])
                c0 = (items[i] + st) * CG
                kk.dma("sp", P.MT[c0:c0 + CG, LAT0:LAT0 + SEQ].rearrange("c (a b) -> a c b", b=128), o_[:], rd=[ok_],
                       wr=[("MT", items[i] + st)])
        seqs = [[], []]
        for st in range(2):
            ld(0, st)
        for i in range(len(items)):
            for st in range(2):
                q = seqs[st]
                for o in range(2):
                    q.append(lambda i=i, st=st, o=o: cast(i, st, o))
                    q.append(lambda st=st: S[st].s_A())
                    q.append(lambda st=st: S[st].s_tw1())
                    q.append(lambda st=st: S[st].s_C())
                    q.append(lambda i=i, st=st, o=o: S[st].s_Z(kf[o][st][i % 2], "c_kf%d_%d_%d" % (o, st, i % 2)))
                    q.append(lambda st=st: S[st].s_Cp())
                    if o == 0:
                        if i + 1 < len(items):
                            q.append(lambda i=i, st=st: ld(i + 1, st))
                        else:
                            q.append(lambda: None)
                    q.append(lambda st=st: S[st].s_tw2())
                    q.append(lambda st=st: S[st].s_Ap())
                    q.append(lambda i=i, st=st, o=o: gate(i, st, o))
        emit_skewed(seqs[0], seqs[1], 0)


def phase_hy_ctx(P, L):
    kk = P.k
    nc = kk.nc
    j = L // 2
    HC = 512
    with kk.scope():
        cf = kk.sbuf("x_cf", [128, 2, 2, 512])
        ci = kk.sbuf("x_ci", [128, 2, 4, 256])
        fo = kk.sbuf("x_fo", [64, 4096])
        h3s = kk.sbuf("x_h3", [64, 256])
        drow = kk.sbuf("x_drow", [128, D])
        tcol = kk.sbuf("x_tcol", [128, 2])
        win = kk.sbuf("x_win", [128, 2, D])
        skipb = kk.sbuf("x_skip", [128, 2, D])
        hfc = kk.sbuf("x_hfc", [128, 2, 4, HC])
        kfc = kk.sbuf("x_kfc", [128, 4, 2, HC])
        Z = kk.sbuf("x_Z", [128, 4, 2, HC])
        raw = [kk.sbuf("x_raw%d" % i, [128, 256]) for i in range(2)]
        vtok = kk.sbuf("x_vtok", [128, 2, 3, HC])
        y1 = kk.sbuf("x_y1", [128, 2, HC])
        y2 = kk.sbuf("x_y2", [128, 2, HC])
        tmp = [kk.sbuf("x_tmp%d" % i, [128, HC]) for i in range(4)]
        orow = [kk.sbuf("x_orow%d" % i, [128, 256]) for i in range(2)]
        psr = kk.psum("x_psr", [128, 512])
        psi = kk.psum("x_psi", [128, 512])
        psy = [kk.psum("x_psy%d" % i, [128, 512]) for i in range(2)]
        pst = [kk.psum("x_pst%d" % i, [128, 128]) for i in range(2)]
        kk.dma("sp", cf[:], P.hy["hyCF"].rearrange("r c p f -> p r c f"), rd=[], wr=["x_cf"])
        kk.dma("sp", ci[:], P.hy["hyCI"].rearrange("r c p t -> p r c t"), rd=[], wr=["x_ci"])
        kk.dma("sp", fo[:], P.hy["od_f_out"][j], rd=[], wr=["x_fo"])
        kk.dma("sp", drow[:], P.hy["hyDrow"][:, :], rd=[], wr=["x_drow"])
        kk.dma("sp", tcol[:], P.hy["hyTcol"][:, :], rd=[], wr=["x_tcol"])
        kk.dma("sp", skipb[:], P.hy["od_skip"][j].partition_broadcast(128), rd=[], wr=["x_skip"])
        for tt in range(2):
            kk.act(win[:, tt, :], drow[:, :], AF.Exp, rd=["x_drow", "x_tcol"], wr=["x_win"], scale=tcol[:, tt:tt + 1])

        def sink(ti, c0, nt, h3, hk):
            kk.cp(h3s[:, :], h3[:, :256], rd=[hk], wr=["x_h3"])
        with kk.scope():
            hy_mlp(P, j, "ctx", CTXL, sink)

        def cmul(ore, oim, ar, ai, br, bi, rd, wr):
            tk = ["x_tmp%d" % i for i in range(4)]
            kk.tt(tmp[0][:], ar, br, ALU.mult, rd=rd, wr=[tk[0]])
            kk.tt(tmp[1][:], ai, bi, ALU.mult, rd=rd, wr=[tk[1]])
            kk.tt(tmp[2][:], ar, bi, ALU.mult, rd=rd, wr=[tk[2]])
            kk.tt(tmp[3][:], ai, br, ALU.mult, rd=rd, wr=[tk[3]])
            kk.tt(ore, tmp[0][:], tmp[1][:], ALU.subtract, rd=[tk[0], tk[1]], wr=wr, e="pool")
            kk.tt(oim, tmp[2][:], tmp[3][:], ALU.add, rd=[tk[2], tk[3]], wr=wr, e="pool")

        def fwd(src_fn, srckeys, fk):
            for tt in range(2):
                kk.mm(psr[:, :], cf[:, 0, tt, fk * 128:(fk + 1) * 128], src_fn(tt), tt == 0, tt == 1, rd=["x_cf"] + srckeys, wr=["x_psr"])
            for tt in range(2):
                kk.mm(psi[:, :], cf[:, 1, tt, fk * 128:(fk + 1) * 128], src_fn(tt), tt == 0, tt == 1, rd=["x_cf"] + srckeys, wr=["x_psi"])

        oi = 0
        for half in range(2):
            hc0 = half * HC
            for tt in range(2):
                for sig in range(4):
                    kk.mm(psr[:, :], h3s[:, tt * 128:(tt + 1) * 128], fo[:, sig * D + hc0:sig * D + hc0 + HC], True, True,
                          rd=["x_h3", "x_fo"], wr=["x_psr"])
                    kk.stt(hfc[:, tt, sig, :], win[:, tt, hc0:hc0 + HC], 0.05, psr[:, :], ALU.add, ALU.mult, rd=["x_win", "x_psr"], wr=["x_hfc"])
            ri_ = 0
            for q in range(3):
                for cc in range(HC // 128):
                    r_, rk = raw[ri_ % 2], "x_raw%d" % (ri_ % 2)
                    ri_ += 1
                    row0 = q * D + hc0 + cc * 128
                    kk.dma("sp", r_[:], P.PT[row0:row0 + 128, CTX0:CTX0 + CTXL], rd=[], wr=[rk])
                    for tt in range(2):
                        p_, pk = pst[tt], "x_pst%d" % tt
                        kk.op("pe", nc.tensor.transpose, rd=[rk, "ident"], wr=[pk], out=p_[:, :], in_=r_[:, tt * 128:(tt + 1) * 128],
                              identity=P.ident[:, :])
                        kk.act(vtok[:, tt, q, cc * 128:(cc + 1) * 128], p_[:, :], AF.Copy, rd=[pk], wr=["x_vtok"])
            cur = lambda tt: vtok[:, tt, 0, :]
            curk = ["x_vtok"]
            for o in range(2):
                for fk in range(4):
                    fwd(lambda tt: hfc[:, tt, o * 2 + 0, :], ["x_hfc"], fk)
                    kk.act(kfc[:, fk, 0, :], psr[:, :], AF.Copy, rd=["x_psr"], wr=["x_kfc"])
                    kk.act(kfc[:, fk, 1, :], psi[:, :], AF.Copy, rd=["x_psi"], wr=["x_kfc"])
                    fwd(lambda tt: hfc[:, tt, o * 2 + 1, :], ["x_hfc"], fk)
                    kk.tt(kfc[:, fk, 0, :], kfc[:, fk, 0, :], psr[:, :], ALU.add, rd=["x_kfc", "x_psr"], wr=["x_kfc"])
                    kk.tt(kfc[:, fk, 1, :], kfc[:, fk, 1, :], psi[:, :], ALU.subtract, rd=["x_kfc", "x_psi"], wr=["x_kfc"])
                for fk in range(4):
                    fwd(cur, curk, fk)
                    cmul(Z[:, fk, 0, :], Z[:, fk, 1, :], psr[:, :], psi[:, :], kfc[:, fk, 0, :], kfc[:, fk, 1, :],
                         rd=["x_psr", "x_psi", "x_kfc"], wr=["x_Z"])
                dst = y1 if o == 0 else y2
                dk = "x_y1" if o == 0 else "x_y2"
                for tt in range(2):
                    p_, pk = psy[tt], "x_psy%d" % tt
                    n_ = 0
                    for fk in range(4):
                        for r in range(2):
                            kk.mm(p_[:, :], ci[:, r, fk, tt * 128:(tt + 1) * 128], Z[:, fk, r, :], n_ == 0, n_ == 7, rd=["x_ci", "x_Z"], wr=[pk])
                            n_ += 1
                    kk.tt(tmp[0][:], cur(tt), skipb[:, o, hc0:hc0 + HC], ALU.mult, rd=curk + ["x_skip"], wr=["x_tmp0"])
                    kk.tt(tmp[0][:], tmp[0][:], p_[:, :], ALU.add, rd=["x_tmp0", pk], wr=["x_tmp0"])
                    kk.tt(dst[:, tt, :], tmp[0][:], vtok[:, tt, 1 + o, :], ALU.mult, rd=["x_tmp0", "x_vtok"], wr=[dk])
                cur = lambda tt: y1[:, tt, :]
                curk = ["x_y1"]
            for cc in range(HC // 128):
                o_, ok_ = orow[oi % 2], "x_orow%d" % (oi % 2)
                oi += 1
                for tt in range(2):
                    p_, pk = pst[tt], "x_pst%d" % tt
                    kk.op("pe", nc.tensor.transpose, rd=["x_y2", "ident"], wr=[pk], out=p_[:, :], in_=y2[:, tt, cc * 128:(cc + 1) * 128],
                          identity=P.ident[:, :])
                    kk.act(o_[:, tt * 128:(tt + 1) * 128], p_[:, :], AF.Copy, rd=[pk], wr=[ok_])
                kk.dma("sp", P.MT[hc0 + cc * 128:hc0 + (cc + 1) * 128, CTX0:CTX0 + CTXL], o_[:, :], rd=[ok_], wr=[("MTc", half, cc)])


def build_program(cp, debug=False, nlayers=DEPTH):
    P = Prog(cp.off, cp.n, debug=debug)
    even_inputs(P)
    rwkv_inputs(P)
    hy_inputs(P)
    P.phase_init()
    P.phase_cond()
    for L in range(nlayers):
        j = L // 2
        if L % 2 == 0:
            phase_proj_even(P, L)
            phase_rwkv_prep(P, L)
            phase_rwkv_main(P, L)
            phase_rwkv_out(P, L)
            phase_attn(P, L)
            P.phase_outproj(L, P.ev_w_out[j], None)
        else:
            phase_hy_proj(P, L)
            phase_hy_filter_lat(P, L)
            phase_hy_kf(P, L)
            phase_hy_conv(P, L)
            if L < DEPTH - 1:
                phase_hy_ctx(P, L)
            P.phase_outproj(L, P.od_w_out[j], "b_out%d" % j)
        P.phase_ffn(L)
    P.k.barrier()
    return P


def kernel(**inputs):
    inp = {k: np.asarray(v) for k, v in inputs.items()}
    cp = build_colpack(inp)
    maps = host_inputs(inp, cp)
    host_even_extra(maps, inp)
    host_rwkv_extra(maps, inp)
    host_hy_extra(maps, inp)
    P = build_program(cp)
    need = set(P.in_names)
    maps = [{k: v for k, v in m.items() if k in need} for m in maps]
    res = run_bass_kernel_spmd(P.k.nc, maps, core_ids=list(range(8)))
    out = np.stack([np.ascontiguousarray(np.asarray(r["outT"], dtype=np.float32).T) for r in res.results], axis=0)
    return out
```

```python
import contextlib
import math
import numpy as np
import concourse.bass as bass
import concourse.mybir as mybir
from concourse.bass_utils import run_bass_kernel_spmd

F32 = mybir.dt.float32
BF16 = mybir.dt.bfloat16
AF = mybir.ActivationFunctionType
ALU = mybir.AluOpType
AX = mybir.AxisListType

D = 1024
SEQ = 4096
CTXL = 256
DEPTH = 4
NTOK = SEQ + CTXL
CTX0 = 1
LAT0 = CTX0 + CTXL + 2
NT = LAT0 + SEQ + 1
FF = 2816
A_W = 512
A_IN = 1920
EVEN_IN = 2688
TILES = [(CTX0, CTXL, 1)] + [(LAT0 + i * 512, 512, 0) for i in range(8)]

NPOOL = 12


class K:
    def __init__(self):
        self.nc = bass.Bass("TRN2", target_bir_lowering=False)
        self.es = contextlib.ExitStack()
        nc = self.nc
        self.eng = {"pe": nc.tensor, "dve": nc.vector, "act": nc.scalar, "pool": nc.gpsimd, "sp": nc.sync}
        self.sem = {e: self.es.enter_context(nc.semaphore("sem_" + e)) for e in self.eng}
        self.cnt = {e: 0 for e in self.eng}
        self.waited = {e: {} for e in self.eng}
        self.dsem = {e: [self.es.enter_context(nc.semaphore("dsem_%s_%d" % (e, i))) for i in range(NPOOL)]
                     for e in ("sp", "act", "pool")}
        self.dval = {e: [0] * NPOOL for e in self.dsem}
        self.dnext = {e: 0 for e in self.dsem}
        self.lastw = {}
        self.readers = {}
        self.ninst = 0
        self.scopes = []

    def sbuf(self, name, shape, dtype=F32):
        st = self.scopes[-1] if self.scopes else self.es
        self.uid = getattr(self, "uid", 0) + 1
        return st.enter_context(self.nc.sbuf_tensor("%s_%d" % (name, self.uid), list(shape), dtype))

    def psum(self, name, shape, dtype=F32):
        st = self.scopes[-1] if self.scopes else self.es
        self.uid = getattr(self, "uid", 0) + 1
        return st.enter_context(self.nc.psum_tensor("%s_%d" % (name, self.uid), list(shape), dtype))

    def dram(self, name, shape, dtype=F32, kind="Internal"):
        return self.nc.dram_tensor(name, list(shape), dtype, kind=kind).ap()

    @contextlib.contextmanager
    def scope(self):
        self.barrier()
        st = contextlib.ExitStack()
        self.scopes.append(st)
        try:
            yield
        finally:
            self.barrier()
            self.scopes.pop()
            st.close()

    def barrier(self):
        evs = [(self.sem[e], self.cnt[e]) for e in self.eng if self.cnt[e] > 0]
        for q in self.dsem:
            for i in range(NPOOL):
                if self.dval[q][i] > 0:
                    evs.append((self.dsem[q][i], self.dval[q][i]))
        for e in self.eng:
            need = {id(s): (s, v) for s, v in evs if s is not self.sem[e]}
            self._emit_waits(e, need)

    def _need(self, me, rd, wr):
        need = {}

        def add(ev):
            if ev is None:
                return
            sem, val, owner = ev
            if me == "pe" and owner == "pe":
                return
            k = id(sem)
            if k not in need or need[k][1] < val:
                need[k] = (sem, val)
        for r in rd:
            add(self.lastw.get(r))
        for w in wr:
            add(self.lastw.get(w))
            for ev in self.readers.get(w, ()):
                add(ev)
        return need

    def _emit_waits(self, e, need):
        wd = self.waited[e]
        for k, (sem, val) in need.items():
            if wd.get(k, 0) >= val:
                continue
            self.eng[e].wait_ge(sem, val)
            wd[k] = val

    def _record(self, ev, rd, wr):
        for w in wr:
            self.lastw[w] = ev
            self.readers[w] = []
        for r in rd:
            if r in wr:
                continue
            lst = self.readers.setdefault(r, [])
            lst.append(ev)
            if len(lst) > 48:
                best = {}
                for s, v, o in lst:
                    if id(s) not in best or best[id(s)][1] < v:
                        best[id(s)] = (s, v, o)
                self.readers[r] = list(best.values())

    def op(self, e, fn, rd=(), wr=(), **kw):
        self._emit_waits(e, self._need(e, rd, wr))
        inst = fn(**kw)
        self.cnt[e] += 1
        inst.then_inc(self.sem[e], 1)
        self._record((self.sem[e], self.cnt[e], e), rd, wr)
        self.ninst += 1
        return inst

    def dma(self, q, out, in_, rd=(), wr=(), **kw):
        q = "act" if (str(out.space) == "SB" and str(in_.space) == "DRAM") else "sp"
        need = self._need(None, rd, wr)
        i = self.dnext[q]
        self.dnext[q] = (i + 1) % NPOOL
        sem = self.dsem[q][i]
        if self.dval[q][i] > 0:
            k = id(sem)
            if k not in need or need[k][1] < self.dval[q][i]:
                need[k] = (sem, self.dval[q][i])
        self._emit_waits(q, need)
        inst = self.eng[q].dma_start(out=out, in_=in_, **kw)
        self.dval[q][i] += 16
        inst.then_inc(sem, 16)
        self._record((sem, self.dval[q][i], "dma"), rd, wr)
        self.ninst += 1
        return inst

    def mm(self, out, lhsT, rhs, start, stop, rd, wr):
        return self.op("pe", self.nc.tensor.matmul, rd=rd, wr=wr, out=out, lhsT=lhsT, rhs=rhs, start=start, stop=stop)

    def act(self, out, in_, func, rd, wr, bias=None, scale=None, accum_out=None):
        kw = dict(out=out, in_=in_, func=func)
        if bias is not None:
            kw["bias"] = bias
        if scale is not None:
            kw["scale"] = scale
        if accum_out is not None:
            kw["accum_out"] = accum_out
        return self.op("act", self.nc.scalar.activation, rd=rd, wr=wr, **kw)

    def tt(self, out, in0, in1, op, rd, wr, e="dve"):
        return self.op(e, self.eng[e].tensor_tensor, rd=rd, wr=wr, out=out, in0=in0, in1=in1, op=op)

    def ts(self, out, in0, s1, op0, rd, wr, s2=None, op1=None, e="dve"):
        kw = dict(out=out, in0=in0, scalar1=s1, scalar2=s2, op0=op0)
        if op1 is not None:
            kw["op1"] = op1
        return self.op(e, self.eng[e].tensor_scalar, rd=rd, wr=wr, **kw)

    def stt(self, out, in0, scalar, in1, op0, op1, rd, wr):
        return self.op("dve", self.nc.vector.scalar_tensor_tensor, rd=rd, wr=wr, out=out, in0=in0, scalar=scalar,
                       in1=in1, op0=op0, op1=op1)

    def cp(self, out, in_, rd, wr, e="dve"):
        return self.op(e, self.eng[e].tensor_copy, rd=rd, wr=wr, out=out, in_=in_)

    def memset(self, ap, val, wr, e="pool"):
        return self.op(e, self.eng[e].memset, rd=(), wr=wr, ap=ap, constant=val)


class ColPack:
    def __init__(self):
        self.cols = []
        self.off = {}
        self.n = 0

    def add(self, name, arr128xm):
        a = np.ascontiguousarray(arr128xm, dtype=np.float32)
        assert a.shape[0] == 128
        self.off[name] = self.n
        self.n += a.shape[1]
        self.cols.append(a)

    def addvec(self, name, v):
        v = np.asarray(v, dtype=np.float32).reshape(-1)
        if v.size < 128:
            v = np.concatenate([v, np.zeros(128 - v.size, np.float32)])
        assert v.size % 128 == 0
        self.add(name, v.reshape(-1, 128).T)

    def array(self):
        return np.ascontiguousarray(np.concatenate(self.cols, axis=1))


def pipelined(items, load_fn, compute_fn):
    if not items:
        return
    load_fn(0, items[0])
    for i, it in enumerate(items):
        if i + 1 < len(items):
            load_fn(i + 1, items[i + 1])
        compute_fn(i, it)


def bc(ap, shape):
    return ap.broadcast_to(list(shape))


class Prog:
    def __init__(self, cp_off, ncol, debug=False, stop_after=None, ext_in=()):
        self.k = K()
        k = self.k
        self.debug = debug
        self.stop_after = stop_after
        self.cp_off = cp_off
        EI = "ExternalInput"
        sk = "ExternalOutput" if debug else "Internal"
        self.in_names = []

        def inp(name, shape, dt=F32):
            self.in_names.append(name)
            return k.dram(name, shape, dt, EI)
        self.xT = inp("xT", [D, SEQ])
        self.ctxT = inp("ctxT", [D, CTXL])
        self.condc = inp("condc", [128, 16])
        self.colsD = inp("cols", [128, ncol])
        self.identD = inp("ident", [128, 128])
        self.ada_w = inp("ada_w", [DEPTH, D, 6 * D])
        self.ffn_up = inp("ffn_up", [DEPTH, D, 2 * FF])
        self.ffn_down = inp("ffn_down", [DEPTH, FF, D])
        self.ev_w_in = inp("ev_w_in", [2, D, EVEN_IN])
        self.ev_w_out = inp("ev_w_out", [2, D, D])
        self.od_w_in = inp("od_w_in", [2, D, 3 * D])
        self.od_w_out = inp("od_w_out", [2, D, D])
        self.outT = k.dram("outT", [D, SEQ], F32, "ExternalOutput")
        def scr(name, shape, dt=F32):
            if name in ext_in:
                return inp(name, shape, dt)
            return k.dram(name, shape, dt, sk)
        self.scr = scr
        self.XT = scr("XT", [D, NT])
        self.PT = scr("PT", [3 * D, NT])
        self.MT = scr("MT", [D, NT])
        self.AT = scr("AT", [FF, NT], BF16)
        self.cols = k.sbuf("cols_sb", [128, ncol])
        self.ident = k.sbuf("ident_sb", [128, 128])
        self.ones = k.sbuf("ones_sb", [128, 128])
        self.mod = k.sbuf("mod_sb", [128, DEPTH, 48, 2])
        self.gA = k.sbuf("gA_sb", [128, DEPTH, 2, 8, 2])
        k.dma("sp", self.cols[:], self.colsD[:, :], rd=[], wr=["cols"])
        k.dma("sp", self.ident[:], self.identD[:, :], rd=[], wr=["ident"])
        k.memset(self.ones[:], 1.0, wr=["ones"])

    def col(self, name, i=0, n=1):
        o = self.cp_off[name] + i
        return self.cols[:, o:o + n]

    def phase_init(self):
        k = self
        kk = self.k
        kk.dma("sp", self.XT[:, LAT0:LAT0 + SEQ], self.xT[:, :], rd=[], wr=["XT"])
        kk.dma("sp", self.XT[:, CTX0:CTX0 + CTXL], self.ctxT[:, :], rd=[], wr=["XT"])

    def phase_cond(self):
        kk = self.k
        nc = kk.nc
        with kk.scope():
            cond = kk.sbuf("cond", [128, 8, 2])
            craw = kk.sbuf("craw", [128, 16])
            aw = [kk.sbuf("aw%d" % i, [128, 6 * D]) for i in range(2)]
            ps = kk.psum("cond_ps", [128, 96])
            kk.dma("sp", craw[:], self.condc[:, :], rd=[], wr=["craw"])
            kk.act(cond[:].rearrange("p k s -> p s k"), craw[:].rearrange("p (s k) -> p s k", s=2), AF.Silu,
                   rd=["craw"], wr=["cond"])
            for L in range(DEPTH):
                acc = self.mod[:, L].rearrange("p c s -> p (c s)")
                for kc in range(8):
                    b = aw[kc % 2]
                    bk = "aw%d" % (kc % 2)
                    kk.dma("sp", b[:], self.ada_w[L, kc * 128:(kc + 1) * 128, :], rd=[], wr=[bk])
                    for oc in range(48):
                        kk.mm(ps[:, oc * 2:oc * 2 + 2], b[:, oc * 128:(oc + 1) * 128], cond[:, kc, :], True, True,
                              rd=[bk, "cond"], wr=["cond_ps"])
                    if kc == 0:
                        kk.cp(acc, ps[:, :], rd=["cond_ps"], wr=[("mod", L)])
                    else:
                        kk.tt(acc, acc, ps[:, :], ALU.add, rd=["cond_ps", ("mod", L)], wr=[("mod", L)])
                ab = self.col("ada_b%d" % L, 0, 48)
                kk.tt(self.mod[:, L], self.mod[:, L], bc(ab.unsqueeze(2), [128, 48, 2]), ALU.add,
                      rd=[("mod", L), "cols"], wr=[("mod", L)])
                for j, (part, gname) in enumerate([(1, "norm1_g%d" % L), (4, "norm2_g%d" % L)]):
                    g = self.col(gname, 0, 8)
                    kk.op("dve", nc.vector.scalar_tensor_tensor, rd=[("mod", L), "cols"], wr=[("gA", L)],
                          out=self.gA[:, L, j], in0=self.mod[:, L, part * 8:(part + 1) * 8, :], scalar=1.0,
                          in1=bc(g.unsqueeze(2), [128, 8, 2]), op0=ALU.add, op1=ALU.mult)

    def phase_norm(self, L, which, hT):
        kk = self.k
        nc = kk.nc
        shift_part = 0 if which == 0 else 3
        with kk.scope():
            xt = [kk.sbuf("nx%d" % i, [128, 8, 512]) for i in range(2)]
            sq = kk.sbuf("nsq", [128, 8, 512])
            rstd = kk.sbuf("nrstd", [128, 512])
            tmp = [kk.sbuf("ntmp%d" % i, [128, 512]) for i in range(2)]
            ps = kk.psum("nps", [128, 512])
            for c0, c1 in [(0, 1), (CTX0 + CTXL, LAT0), (NT - 1, NT)]:
                kk.memset(hT[:, :, c0:c1], 0.0, wr=["hT"])
            def ld(ti, t):
                c0, n, s = t
                kk.dma("sp", xt[ti % 2][:, :, :n], self.XT[:, c0:c0 + n].rearrange("(k p) t -> p k t", p=128),
                       rd=["XT"], wr=["nx%d" % (ti % 2)])

            def comp(ti, t):
                c0, n, s = t
                x = xt[ti % 2]
                xk = "nx%d" % (ti % 2)
                kk.act(sq[:, :, :n], x[:, :, :n], AF.Square, rd=[xk], wr=["nsq"])
                for c in range(8):
                    kk.mm(ps[:, :n], self.ones[:, :], sq[:, c, :n], c == 0, c == 7, rd=["nsq", "ones"], wr=["nps"])
                kk.act(rstd[:, :n], ps[:, :n], AF.Sqrt, rd=["nps", "cols"], wr=["nrstd"], bias=self.col("eps"), scale=1.0 / D)
                kk.op("dve", nc.vector.reciprocal, rd=["nrstd"], wr=["nrstd"], out=rstd[:, :n], in_=rstd[:, :n])
                for c in range(8):
                    t_ = tmp[c % 2]
                    tk = "ntmp%d" % (c % 2)
                    kk.tt(t_[:, :n], x[:, c, :n], rstd[:, :n], ALU.mult, rd=[xk, "nrstd"], wr=[tk])
                    kk.act(hT[:, c, c0:c0 + n], t_[:, :n], AF.Identity, rd=[tk, ("gA", L), ("mod", L)], wr=["hT"],
                           scale=self.gA[:, L, which, c, s:s + 1], bias=self.mod[:, L, shift_part * 8 + c, s:s + 1])
            pipelined(TILES, ld, comp)

    def load_w_chunk(self, wdst, wkey, wsrc_cols, stg, stgkey):
        kk = self.k
        kk.dma("sp", stg[:], wsrc_cols.rearrange("(k p) n -> p k n", p=128), rd=[], wr=[stgkey])
        kk.cp(wdst[:], stg[:], rd=[stgkey], wr=[wkey], e="pool")

    def proj_rows(self, hT, W, n_oc, row_fn, bias_name=None):
        kk = self.k
        stg = [kk.sbuf("pstg%d" % i, [128, 8, 128]) for i in range(2)]
        wb = [kk.sbuf("pwb%d" % i, [128, 8, 128], BF16) for i in range(2)]
        rows = [kk.sbuf("prow%d" % i, [128, NT]) for i in range(2)]
        pss = [kk.psum("pps%d" % i, [128, 512]) for i in range(4)]
        for i in range(2):
            kk.memset(rows[i][:, :], 0.0, wr=["prow%d" % i])
        pi = 0

        def ldw(oc):
            self.load_w_chunk(wb[oc % 2], "pwb%d" % (oc % 2), W[:, oc * 128:(oc + 1) * 128], stg[oc % 2], "pstg%d" % (oc % 2))
        ldw(0)
        for oc in range(n_oc):
            w = wb[oc % 2]
            wk = "pwb%d" % (oc % 2)
            if oc + 1 < n_oc:
                ldw(oc + 1)
            row = rows[oc % 2]
            rk = "prow%d" % (oc % 2)
            for (c0, n, s) in TILES:
                ps = pss[pi % 4]
                pk = "pps%d" % (pi % 4)
                pi += 1
                for c in range(8):
                    kk.mm(ps[:, :n], w[:, c, :], hT[:, c, c0:c0 + n], c == 0, c == 7, rd=[wk, "hT"], wr=[pk])
                if bias_name is None:
                    kk.act(row[:, c0:c0 + n], ps[:, :n], AF.Copy, rd=[pk], wr=[rk])
                else:
                    kk.act(row[:, c0:c0 + n], ps[:, :n], AF.Identity, rd=[pk, "cols"], wr=[rk], bias=self.col(bias_name, oc, 1))
            row_fn(oc, row, rk)

    def phase_outproj(self, L, W, bias_name):
        kk = self.k
        nc = kk.nc
        with kk.scope():
            stg = [kk.sbuf("ostg%d" % i, [128, 8, 128]) for i in range(2)]
            wb = kk.sbuf("owb", [128, 8, 8, 128], BF16)
            mt = [kk.sbuf("omt%d" % i, [128, 8, 512]) for i in range(2)]
            mb = [kk.sbuf("omb%d" % i, [128, 8, 512], BF16) for i in range(2)]
            xt = [kk.sbuf("oxt%d" % i, [128, 8, 512]) for i in range(2)]
            pss = [kk.psum("ops%d" % i, [128, 512]) for i in range(4)]
            for oc in range(8):
                kk.dma("sp", stg[oc % 2][:], W[:, oc * 128:(oc + 1) * 128].rearrange("(k p) n -> p k n", p=128),
                       rd=[], wr=["ostg%d" % (oc % 2)])
                kk.cp(wb[:, oc], stg[oc % 2][:], rd=["ostg%d" % (oc % 2)], wr=["owb"], e="pool")
            pi = [0]
            tiles = [t for t in TILES if not (L == DEPTH - 1 and t[2] == 1)]

            def ld(ti, t):
                c0, n, s = t
                kk.dma("sp", mt[ti % 2][:, :, :n], self.MT[:, c0:c0 + n].rearrange("(k p) t -> p k t", p=128), rd=[], wr=["omt%d" % (ti % 2)])
                kk.dma("sp", xt[ti % 2][:, :, :n], self.XT[:, c0:c0 + n].rearrange("(k p) t -> p k t", p=128), rd=[], wr=["oxt%d" % (ti % 2)])

            def comp(ti, t):
                c0, n, s = t
                m, mk_ = mt[ti % 2], "omt%d" % (ti % 2)
                b, bk = mb[ti % 2], "omb%d" % (ti % 2)
                x, xk = xt[ti % 2], "oxt%d" % (ti % 2)
                kk.cp(b[:, :, :n], m[:, :, :n], rd=[mk_], wr=[bk], e="pool")
                for oc in range(8):
                    ps, pk = pss[pi[0] % 4], "ops%d" % (pi[0] % 4)
                    pi[0] += 1
                    for c in range(8):
                        kk.mm(ps[:, :n], wb[:, oc, c, :], b[:, c, :n], c == 0, c == 7, rd=["owb", bk], wr=[pk])
                    if bias_name is not None:
                        kk.act(ps[:, :n], ps[:, :n], AF.Identity, rd=[pk, "cols"], wr=[pk], bias=self.col(bias_name, oc, 1))
                    kk.stt(x[:, oc, :n], ps[:, :n], self.mod[:, L, 2 * 8 + oc, s:s + 1], x[:, oc, :n], ALU.mult, ALU.add,
                           rd=[pk, xk, ("mod", L)], wr=[xk])
                kk.dma("sp", self.XT[:, c0:c0 + n].rearrange("(k p) t -> p k t", p=128), x[:, :, :n], rd=[xk], wr=[("XTo", ti)])
            pipelined(tiles, ld, comp)

    def phase_ffn(self, L):
        kk = self.k
        nc = kk.nc
        last = (L == DEPTH - 1)
        with kk.scope():
            hT = kk.sbuf("hT2", [128, 8, NT], BF16)
            self.phase_norm(L, 1, hT)
            with kk.scope():
                cvt = [kk.sbuf("fcv%d" % i, [128, NT]) for i in range(2)]
                gsb = kk.sbuf("fgs", [128, NT], BF16)
                arow = [kk.sbuf("far%d" % i, [128, NT], BF16) for i in range(2)]
                state = {}

                def conv_row(oc, row, rk, dst, dk):
                    w0 = self.col("ffn_cw%d_0" % L, oc, 1)
                    w1 = self.col("ffn_cw%d_1" % L, oc, 1)
                    w2 = self.col("ffn_cw%d_2" % L, oc, 1)
                    b = self.col("ffn_cb%d" % L, oc, 1)
                    kk.act(dst[:, 1:NT - 1], row[:, 1:NT - 1], AF.Identity, rd=[rk, "cols"], wr=[dk], scale=w1, bias=b)
                    kk.stt(dst[:, 1:NT - 1], row[:, 0:NT - 2], w0, dst[:, 1:NT - 1], ALU.mult, ALU.add,
                           rd=[rk, dk, "cols"], wr=[dk])
                    kk.stt(dst[:, 1:NT - 1], row[:, 2:NT], w2, dst[:, 1:NT - 1], ALU.mult, ALU.add,
                           rd=[rk, dk, "cols"], wr=[dk])

                def row_fn(oi, row, rk):
                    i = oi // 2
                    if oi % 2 == 0:
                        conv_row(i, row, rk, cvt[0], "fcv0")
                        kk.act(gsb[:, :], cvt[0][:, :], AF.Silu, rd=["fcv0"], wr=["fgs"])
                    else:
                        conv_row(22 + i, row, rk, cvt[1], "fcv1")
                        a, ak = arow[i % 2], "far%d" % (i % 2)
                        kk.tt(a[:, :], gsb[:, :], cvt[1][:, :], ALU.mult, rd=["fgs", "fcv1"], wr=[ak])
                        kk.dma("sp", self.AT[i * 128:(i + 1) * 128, :], a[:, :], rd=[ak], wr=["AT"])

                for i in range(2):
                    kk.memset(cvt[i][:, :], 0.0, wr=["fcv%d" % i])
                Wup = self.ffn_up[L]

                class WV:
                    def __getitem__(s_, idx):
                        rows, cs = idx
                        oi = cs.start // 128
                        i = oi // 2
                        col0 = (i if oi % 2 == 0 else 22 + i) * 128
                        return Wup[:, col0:col0 + 128]
                self.proj_rows(hT, WV(), 44, row_fn)
        with kk.scope():
            stg = [kk.sbuf("dstg%d" % i, [128, 1024]) for i in range(2)]
            wd = kk.sbuf("dwd", [128, 22, 1024], BF16)
            at = [kk.sbuf("dat%d" % i, [128, 22, 512], BF16) for i in range(2)]
            xt = [kk.sbuf("dxt%d" % i, [128, 8, 512]) for i in range(2)]
            pss = [kk.psum("dps%d" % i, [128, 512]) for i in range(4)]
            for c in range(22):
                kk.dma("sp", stg[c % 2][:], self.ffn_down[L, c * 128:(c + 1) * 128, :], rd=[], wr=["dstg%d" % (c % 2)])
                kk.cp(wd[:, c, :], stg[c % 2][:], rd=["dstg%d" % (c % 2)], wr=["dwd"], e="pool")
            pi = [0]
            tiles = [t for t in TILES if not (last and t[2] == 1)]

            def ld(ti, t):
                c0, n, s = t
                kk.dma("sp", at[ti % 2][:, :, :n], self.AT[:, c0:c0 + n].rearrange("(k p) t -> p k t", p=128), rd=[], wr=["dat%d" % (ti % 2)])
                kk.dma("sp", xt[ti % 2][:, :, :n], self.XT[:, c0:c0 + n].rearrange("(k p) t -> p k t", p=128), rd=[], wr=["dxt%d" % (ti % 2)])

            def comp(ti, t):
                c0, n, s = t
                a, ak = at[ti % 2], "dat%d" % (ti % 2)
                x, xk = xt[ti % 2], "dxt%d" % (ti % 2)
                for oc in range(8):
                    ps, pk = pss[pi[0] % 4], "dps%d" % (pi[0] % 4)
                    pi[0] += 1
                    for c in range(22):
                        kk.mm(ps[:, :n], wd[:, c, oc * 128:(oc + 1) * 128], a[:, c, :n], c == 0, c == 21,
                              rd=["dwd", ak], wr=[pk])
                    kk.stt(x[:, oc, :n], ps[:, :n], self.mod[:, L, 5 * 8 + oc, s:s + 1], x[:, oc, :n], ALU.mult, ALU.add,
                           rd=[pk, xk, ("mod", L)], wr=[xk])
                if last:
                    kk.dma("sp", self.outT[:, c0 - LAT0:c0 - LAT0 + n].rearrange("(k p) t -> p k t", p=128), x[:, :, :n],
                           rd=[xk], wr=[("outT", ti)])
                else:
                    kk.dma("sp", self.XT[:, c0:c0 + n].rearrange("(k p) t -> p k t", p=128), x[:, :, :n], rd=[xk], wr=[("XTo", ti)])
            pipelined(tiles, ld, comp)


def build_colpack(inp):
    cp = ColPack()
    cp.addvec("eps", np.full(128, 1e-6, np.float32))
    cp.addvec("gneps", np.full(128, 64e-5, np.float32))
    cp.addvec("negpi", np.full(128, -3.14159, np.float32))
    for L in range(DEPTH):
        cp.addvec("ada_b%d" % L, inp["ada_b"][L])
        cp.addvec("norm1_g%d" % L, inp["norm1_g"][L])
        cp.addvec("norm2_g%d" % L, inp["norm2_g"][L])
        for t in range(3):
            cp.addvec("ffn_cw%d_%d" % (L, t), inp["ffn_conv_w"][L, t])
        cp.addvec("ffn_cb%d" % L, inp["ffn_conv_b"][L])
    for j in range(2):
        cp.addvec("mu_prev%d" % j, inp["ev_mu_prev"][j])
        cp.addvec("mu_next%d" % j, inp["ev_mu_next"][j])
        for d in range(2):
            cp.addvec("w0_%d_%d" % (j, d), inp["ev_w0"][j, d])
            cp.addvec("a0_%d_%d" % (j, d), inp["ev_a0"][j, d])
        cp.addvec("k_k%d" % j, inp["ev_k_k"][j])
        cp.addvec("k_a%d" % j, inp["ev_k_a"][j])
        cp.addvec("r_k%d" % j, inp["ev_r_k"][j])
        cp.addvec("ln_w%d" % j, inp["ev_ln_w"][j])
        cp.addvec("ln_b%d" % j, inp["ev_ln_b"][j])
        cp.addvec("q_norm%d" % j, np.tile(inp["ev_q_norm"][j], 2))
        cp.addvec("k_norm%d" % j, np.tile(inp["ev_k_norm"][j], 2))
        cp.add("sink%d" % j, np.tile(inp["ev_sink"][j][None, :], (128, 1)))
        cp.addvec("b_in%d" % j, inp["od_b_in"][j])
        for t in range(3):
            cp.addvec("od_cw%d_%d" % (j, t), inp["od_conv_w"][j, t])
        cp.addvec("od_cb%d" % j, inp["od_conv_b"][j])
        cp.addvec("b_out%d" % j, inp["od_b_out"][j])
        for o in range(2):
            cp.addvec("skip%d_%d" % (j, o), inp["od_skip"][j, o])
        for nm in ("f_b1", "f_b2", "f_b3", "f_freq"):
            cp.addvec("%s%d" % (nm, j), inp["od_" + nm][j])
    return cp


def host_inputs(inp, cp):
    f32 = np.float32
    shared = {
        "cols": cp.array(),
        "ident": np.eye(128, dtype=f32),
    }
    for nm in ("ada_w", "ffn_up", "ffn_down", "ev_w_in", "ev_w_out", "od_w_in", "od_w_out"):
        shared[nm] = np.ascontiguousarray(inp[nm], dtype=f32)
    maps = []
    cc = np.asarray(inp["c_ctx"], f32).reshape(8, 128).T
    for b in range(8):
        m = dict(shared)
        m["xT"] = np.ascontiguousarray(np.asarray(inp["x"][b], f32).T)
        m["ctxT"] = np.ascontiguousarray(np.asarray(inp["ctx"][b], f32).T)
        cb = np.asarray(inp["c"][b], f32).reshape(8, 128).T
        m["condc"] = np.ascontiguousarray(np.concatenate([cb, cc], axis=1))
        maps.append(m)
    return maps


def host_even_extra(maps, inp):
    consts = host_consts()
    rc, rs = host_rope()
    sr = np.zeros((1, 2, 2, 512), np.float32)
    for j in range(2):
        for g in range(2):
            for cb in range(4):
                h = 4 * g + 2 * (cb % 2) + cb // 2
                sr[0, j, g, cb * 128:(cb + 1) * 128] = inp["ev_sink"][j][h]
    for m in maps:
        m["consts"] = consts
        m["rope_cos"] = rc
        m["rope_sin"] = rs
        m["sinkrow"] = sr


def host_rwkv_extra(maps, inp):
    rm = host_rwkv_consts()
    w2 = np.ascontiguousarray(np.asarray(inp["ev_w2"], np.float32).reshape(2, 128, 512))
    a2 = np.ascontiguousarray(np.asarray(inp["ev_a2"], np.float32).reshape(2, 128, 512))
    g2 = np.ascontiguousarray(np.asarray(inp["ev_g2"], np.float32))
    for m in maps:
        m["rmask"] = rm
        m["ev_w2"] = w2
        m["ev_a2"] = a2
        m["ev_g2"] = g2


def host_hy_extra(maps, inp):
    hc = host_hy_consts()
    for m in maps:
        m.update(hc)
        for nm in ("od_f_w1", "od_f_w2", "od_f_w3", "od_f_out", "od_skip"):
            m[nm] = np.ascontiguousarray(inp[nm], dtype=np.float32)
C_BONES, C_RMT, C_SEL0, C_SEL1, C_MASKL, C_MASKR, NCONST = 0, 1, 2, 3, 4, 5, 6


def host_consts():
    c = np.zeros((NCONST, 128, 128), np.float32)
    for p in range(128):
        for q in range(128):
            if p // 64 == q // 64:
                c[C_BONES, p, q] = 1.0
    for d in range(128):
        if d % 64 < 32:
            c[C_RMT, d + 32, d] = -1.0
        else:
            c[C_RMT, d - 32, d] = 1.0
    for g in range(2):
        for p in range(128):
            c[C_SEL0 + g, g * 64 + p % 64, p] = 1.0
    kj = np.arange(128)[:, None]
    qi = np.arange(128)[None, :]
    c[C_MASKL] = (kj >= qi)
    c[C_MASKR] = (kj <= qi)
    return np.ascontiguousarray(c.transpose(1, 0, 2))


def host_rope():
    t = np.arange(SEQ)
    row = (t // 64).astype(np.float32)
    colp = (t % 64).astype(np.float32)
    inv = (np.float32(10000.0) ** (-np.arange(16, dtype=np.float32) / np.float32(16))).astype(np.float32)
    ang = np.concatenate([row[:, None] * inv, colp[:, None] * inv], axis=-1).astype(np.float32)
    cos = np.cos(ang).astype(np.float32)
    sin = np.sin(ang).astype(np.float32)
    idx = np.arange(128) % 32
    return np.ascontiguousarray(cos[:, idx].T), np.ascontiguousarray(sin[:, idx].T)


def even_inputs(P):
    k = P.k
    P.in_names += ["consts", "rope_cos", "rope_sin", "sinkrow"]
    P.constsD = k.dram("consts", [128, NCONST, 128], F32, "ExternalInput")
    P.ropecD = k.dram("rope_cos", [128, SEQ], F32, "ExternalInput")
    P.ropesD = k.dram("rope_sin", [128, SEQ], F32, "ExternalInput")
    P.sinkrowD = k.dram("sinkrow", [1, 2, 2, 512], F32, "ExternalInput")
    P.consts = k.sbuf("consts_sb", [128, NCONST, 128])
    k.dma("sp", P.consts[:], P.constsD[:, :, :], rd=[], wr=["consts"])


def phase_proj_even(P, L):
    kk = P.k
    j = L // 2
    with kk.scope():
        hT = kk.sbuf("hT1", [128, 8, NT], BF16)
        P.phase_norm(L, 0, hT)
        with kk.scope():
            sh = [kk.sbuf("psh%d" % i, [128, NT]) for i in range(2)]
            c0t = kk.sbuf("muc0", [128, 15])
            mp = P.col("mu_prev%d" % j, 0, 15)
            mn = P.col("mu_next%d" % j, 0, 15)
            kk.tt(c0t[:, :], mp, mn, ALU.add, rd=["cols"], wr=["muc0"])
            kk.ts(c0t[:, :], c0t[:, :], -1.0, ALU.mult, rd=["muc0"], wr=["muc0"], s2=1.0, op1=ALU.add)

            def row_fn(oc, row, rk):
                if oc < 15:
                    s_, sk_ = sh[oc % 2], "psh%d" % (oc % 2)
                    kk.act(s_[:, 1:NT - 1], row[:, 1:NT - 1], AF.Identity, rd=[rk, "muc0"], wr=[sk_], scale=c0t[:, oc:oc + 1])
                    kk.stt(s_[:, 1:NT - 1], row[:, 0:NT - 2], P.col("mu_prev%d" % j, oc, 1), s_[:, 1:NT - 1], ALU.mult, ALU.add,
                           rd=[rk, sk_, "cols"], wr=[sk_])
                    kk.stt(s_[:, 1:NT - 1], row[:, 2:NT], P.col("mu_next%d" % j, oc, 1), s_[:, 1:NT - 1], ALU.mult, ALU.add,
                           rd=[rk, sk_, "cols"], wr=[sk_])
                    kk.dma("sp", P.PT[oc * 128:(oc + 1) * 128, 1:NT - 1], s_[:, 1:NT - 1], rd=[sk_], wr=[("PT", oc)])
                else:
                    kk.dma("sp", P.PT[oc * 128:(oc + 1) * 128, :], row[:, :], rd=[rk], wr=[("PT", oc)])
            P.proj_rows(hT, P.ev_w_in[j], 21, row_fn)


def phase_attn(P, L, stop=99):
    kk = P.k
    nc = kk.nc
    j = L // 2
    QOC, KOC, VOC = 15, 19, 20
    with kk.scope():
        qT = kk.sbuf("aqT", [128, 4, NT], BF16)
        kT2 = kk.sbuf("akT2", [128, 2, NT], BF16)
        vtok = kk.sbuf("avtok", [128, 34, 128], BF16)
        ones_bf = kk.sbuf("aones", [128, 64], BF16)
        esink = kk.sbuf("aesink", [1, 2, 512], BF16)
        srow = kk.sbuf("asrow", [1, 2, 512])
        maskb = kk.sbuf("amask", [128, 2, 128], BF16)
        kk.memset(ones_bf[:, :], 1.0, wr=["aones"])
        kk.dma("sp", srow[:], P.sinkrowD[:, j], rd=[], wr=["asrow"])
        kk.act(esink[:], srow[:], AF.Exp, rd=["asrow"], wr=["aesink"])
        kk.cp(maskb[:, :, :], P.consts[:, C_MASKL:C_MASKR + 1, :], rd=["consts"], wr=["amask"])
        with kk.scope():
            raw = [kk.sbuf("araw%d" % i, [128, 512]) for i in range(2)]
            sq = kk.sbuf("asq", [128, 512])
            rstd = kk.sbuf("arstd", [128, 512])
            qn = kk.sbuf("aqn", [128, 512])
            t1 = kk.sbuf("at1", [128, 512])
            t2 = kk.sbuf("at2", [128, 512])
            kf = kk.sbuf("akf", [128, 512])
            cs = [kk.sbuf("acos%d" % i, [128, 512]) for i in range(2)]
            sn = [kk.sbuf("asin%d" % i, [128, 512]) for i in range(2)]
            ps_ss = kk.psum("aps_ss", [128, 512])
            ps_rot = kk.psum("aps_rot", [128, 512])
            ps_sel = [kk.psum("aps_sel%d" % i, [128, 512]) for i in range(2)]
            it = 0
            for ch in range(5):
                isq = ch < 4
                oc = QOC + ch if isq else KOC
                nname = ("q_norm%d" if isq else "k_norm%d") % j
                for ti, (c0, n, s) in enumerate(TILES):
                    r_, rk = raw[it % 2], "araw%d" % (it % 2)
                    cs_, ck = cs[it % 2], "acos%d" % (it % 2)
                    sn_, snk = sn[it % 2], "asin%d" % (it % 2)
                    it += 1
                    kk.dma("sp", r_[:, :n], P.PT[oc * 128:(oc + 1) * 128, c0:c0 + n], rd=[], wr=[rk])
                    if s == 0:
                        kk.dma("sp", cs_[:, :n], P.ropecD[:, c0 - LAT0:c0 - LAT0 + n], rd=[], wr=[ck])
                        kk.dma("sp", sn_[:, :n], P.ropesD[:, c0 - LAT0:c0 - LAT0 + n], rd=[], wr=[snk])
                    kk.act(sq[:, :n], r_[:, :n], AF.Square, rd=[rk], wr=["asq"])
                    kk.mm(ps_ss[:, :n], P.consts[:, C_BONES, :], sq[:, :n], True, True, rd=["asq", "consts"], wr=["aps_ss"])
                    kk.act(rstd[:, :n], ps_ss[:, :n], AF.Sqrt, rd=["aps_ss", "cols"], wr=["arstd"], bias=P.col("eps"), scale=1.0 / 64)
                    kk.op("dve", nc.vector.reciprocal, rd=["arstd"], wr=["arstd"], out=rstd[:, :n], in_=rstd[:, :n])
                    kk.tt(qn[:, :n], r_[:, :n], rstd[:, :n], ALU.mult, rd=[rk, "arstd"], wr=["aqn"])
                    kk.ts(qn[:, :n], qn[:, :n], P.col(nname), ALU.mult, rd=["aqn", "cols"], wr=["aqn"],
                          s2=(0.125 if isq else 1.0), op1=ALU.mult)
                    fin = qn
                    fk = "aqn"
                    if s == 0:
                        kk.mm(ps_rot[:, :n], P.consts[:, C_RMT, :], qn[:, :n], True, True, rd=["aqn", "consts"], wr=["aps_rot"])
                        kk.tt(t1[:, :n], qn[:, :n], cs_[:, :n], ALU.mult, rd=["aqn", ck], wr=["at1"])
                        kk.tt(t2[:, :n], ps_rot[:, :n], sn_[:, :n], ALU.mult, rd=["aps_rot", snk], wr=["at2"])
                        fin = kf
                        fk = "akf"
                        kk.tt(kf[:, :n], t1[:, :n], t2[:, :n], ALU.add, rd=["at1", "at2"], wr=["akf"], e="pool")
                    if isq:
                        kk.act(qT[:, ch, c0:c0 + n], fin[:, :n], AF.Copy, rd=[fk], wr=["aqT"])
                    else:
                        for g in range(2):
                            kk.mm(ps_sel[g][:, :n], P.consts[:, C_SEL0 + g, :], fin[:, :n], True, True, rd=[fk, "consts"],
                                  wr=["aps_sel%d" % g])
                            kk.act(kT2[:, g, c0:c0 + n], ps_sel[g][:, :n], AF.Copy, rd=["aps_sel%d" % g], wr=["akT2"])
            vraw = kk.sbuf("avraw", [128, NT])
            ps_t = [kk.psum("aps_t%d" % i, [128, 128]) for i in range(2)]
            kk.dma("sp", vraw[:, :], P.PT[VOC * 128:(VOC + 1) * 128, :], rd=[], wr=["avraw"])
            for blk in range(34):
                c0 = CTX0 + blk * 128 if blk < 2 else LAT0 + (blk - 2) * 128
                pt, pk = ps_t[blk % 2], "aps_t%d" % (blk % 2)
                kk.op("pe", nc.tensor.transpose, rd=["avraw", "ident"], wr=[pk], out=pt[:, :], in_=vraw[:, c0:c0 + 128],
                      identity=P.ident[:, :])
                kk.act(vtok[:, blk, :], pt[:, :], AF.Copy, rd=[pk], wr=["avtok"])
        if stop < 1:
            return
        with kk.scope():
            NS = 2
            ps_s = [kk.psum("aps_s%d" % i, [128, 2, 512]) for i in range(NS)]
            ps_n = [kk.psum("aps_n%d" % i, [64, 512]) for i in range(2)]
            ps_d = [kk.psum("aps_d%d" % i, [64, 512]) for i in range(2)]
            eb = [kk.sbuf("aeb%d" % i, [128, 512], BF16) for i in range(NS)]
            rden = kk.sbuf("arden", [64, 512])
            osb = [kk.sbuf("aosb%d" % i, [64, 512]) for i in range(2)]
            si = 0
            ui = 0
            qblocks = [(1, i) for i in range(2)] + [(0, i) for i in range(32)]
            for (s, qi) in qblocks[:stop]:
                qc0 = (CTX0 if s == 1 else LAT0) + qi * 128
                kbs = [(0, CTX0, None), (1, CTX0 + 128, None)]
                if s == 0:
                    for dlt, mk in ((-1, 0), (0, None), (1, 1)):
                        kb = qi + dlt
                        if 0 <= kb < 32:
                            kbs.append((2 + kb, LAT0 + kb * 128, mk))
                for g in range(2):
                    pn, pnk = ps_n[ui % 2], "aps_n%d" % (ui % 2)
                    pd, pdk = ps_d[ui % 2], "aps_d%d" % (ui % 2)
                    o_, ok_ = osb[ui % 2], "aosb%d" % (ui % 2)
                    ui += 1
                    for bi, (vb, kc0, mk) in enumerate(kbs):
                        ps, psk = ps_s[si % NS], "aps_s%d" % (si % NS)
                        e_, ek = eb[si % NS], "aeb%d" % (si % NS)
                        si += 1
                        for half in range(2):
                            kk.mm(ps[:, half, 0:256].rearrange("p (a q) -> p a q", a=2),
                                  kT2[half * 64:(half + 1) * 64, g, kc0:kc0 + 128],
                                  qT[half * 64:(half + 1) * 64, 2 * g:2 * g + 2, qc0:qc0 + 128], True, True,
                                  rd=["akT2", "aqT"], wr=[psk])
                        kk.act(e_[:, :].rearrange("p (h c) -> p h c", h=2), ps[:, :, 0:256], AF.Exp, rd=[psk], wr=[ek])
                        if mk is not None:
                            ev = e_[:, :].rearrange("p (a q) -> p a q", a=4)
                            kk.tt(ev, ev, bc(maskb[:, mk, :].unsqueeze(1), [128, 4, 128]), ALU.mult, rd=[ek, "amask"], wr=[ek],
                                  e="pool")
                        first = bi == 0
                        kk.mm(pn[:, :], vtok[:, vb, g * 64:(g + 1) * 64], e_[:, :], first, bi == len(kbs) - 1, rd=["avtok", ek], wr=[pnk])
                        kk.mm(pd[:, :], ones_bf[:, :], e_[:, :], first, False, rd=["aones", ek], wr=[pdk])
                    kk.mm(pd[:, :], ones_bf[0:1, :], esink[0:1, g, :], False, True, rd=["aones", "aesink"], wr=[pdk])
                    kk.op("dve", nc.vector.reciprocal, rd=[pdk], wr=["arden"], out=rden[:, :], in_=pd[:, :])
                    kk.tt(o_[:, :], pn[:, :], rden[:, :], ALU.mult, rd=[pnk, "arden"], wr=[ok_])
                    dst = P.MT[512 + 4 * g * 64:512 + (4 * g + 4) * 64, qc0:qc0 + 128].rearrange(
                        "(b a r) t -> a r b t", a=2, b=2)
                    for a in range(2):
                        kk.dma("sp", dst[a], o_[:, a * 256:(a + 1) * 256].rearrange("p (b t) -> p b t", b=2), rd=[ok_],
                               wr=[("MT", s, qi, g, a)])
CH = 64
NCHUNK = NTOK // CH
R_OC, K_OC, V_OC, WD_OC, AD_OC, GD_OC = 0, 4, 8, 12, 13, 14
M_SU, M_IU, M_SL, M_IL = 0, 1, 2, 3
EXPM05 = math.exp(-0.5)


def host_rwkv_consts():
    r = np.arange(64)[:, None]
    c = np.arange(64)[None, :]
    m = np.stack([(r < c), (r <= c), (r > c), (r >= c)], 0).astype(np.float32)
    return np.ascontiguousarray(m.transpose(1, 0, 2))


def rwkv_inputs(P):
    k = P.k
    P.in_names += ["rmask", "ev_w2", "ev_a2", "ev_g2"]
    P.rmaskD = k.dram("rmask", [64, 4, 64], F32, "ExternalInput")
    P.ev_w2 = k.dram("ev_w2", [2, 128, 512], F32, "ExternalInput")
    P.ev_a2 = k.dram("ev_a2", [2, 128, 512], F32, "ExternalInput")
    P.ev_g2 = k.dram("ev_g2", [2, 128, 512], F32, "ExternalInput")
    P.RQ = P.scr("RQ", [12, 512, NT], BF16)
    P.PCD = P.scr("PCD", [2, 512, NCHUNK])
    P.GD = P.scr("GD", [512, NT])
    P.KS = P.scr("KS", [512, NT])
    P.YD = P.scr("YD", [2, 512, NT])


def chunk_base(c0, s):
    return 0 if s == 1 else 4 + (c0 - LAT0) // CH


def phase_rwkv_prep(P, L):
    kk = P.k
    nc = kk.nc
    j = L // 2
    with kk.scope():
        w2 = kk.sbuf("r_w2", [128, 512])
        a2 = kk.sbuf("r_a2", [128, 512])
        g2 = kk.sbuf("r_g2", [128, 512])
        onesf = kk.sbuf("r_ones", [128, 512])
        kt = kk.sbuf("r_k", [128, 4, 512])
        rt = kk.sbuf("r_r", [128, 4, 512])
        wd = kk.sbuf("r_wd", [128, 512])
        ad = kk.sbuf("r_ad", [128, 512])
        gd = kk.sbuf("r_gd", [128, 512])
        kkr = kk.sbuf("r_kkr", [128, 512])
        sq = kk.sbuf("r_sq", [128, 512])
        nrm = kk.sbuf("r_nrm", [128, 512])
        kkn = kk.sbuf("r_kkn", [128, 512])
        gsb = kk.sbuf("r_gsb", [128, 512])
        sg = kk.sbuf("r_sg", [128, 512])
        csz = kk.sbuf("r_csz", [128, 513])
        av = kk.sbuf("r_a", [128, 512])
        kd = kk.sbuf("r_kd", [128, 512])
        nb = kk.sbuf("r_nb", [128, 512])
        ks = kk.sbuf("r_ks", [128, 512])
        arg = [kk.sbuf("r_arg%d" % i, [128, 512]) for i in range(3)]
        ex = [kk.sbuf("r_ex%d" % i, [128, 512]) for i in range(4)]
        outq = [kk.sbuf("r_out%d" % i, [128, 6, 512], BF16) for i in range(2)]
        pcs = kk.sbuf("r_pcs", [128, 8])
        ps_g = kk.psum("r_psg", [128, 512])
        ps_ss = kk.psum("r_psss", [128, 512])
        ps_w = [kk.psum("r_psw%d" % i, [128, 512]) for i in range(2)]
        ps_a = [kk.psum("r_psa%d" % i, [128, 512]) for i in range(2)]
        kk.dma("sp", w2[:], P.ev_w2[j], rd=[], wr=["r_w2"])
        kk.dma("sp", a2[:], P.ev_a2[j], rd=[], wr=["r_a2"])
        kk.dma("sp", g2[:], P.ev_g2[j], rd=[], wr=["r_g2"])
        kk.memset(onesf[:, :], 1.0, wr=["r_ones"])
        kk.memset(csz[:, 0:1], 0.0, wr=["r_csz"])
        oi = 0
        for (c0, n, s) in TILES:
            nch = n // CH
            cb = chunk_base(c0, s)
            kk.dma("sp", kt[:, :, :n], P.PT[K_OC * 128:(K_OC + 4) * 128, c0:c0 + n].rearrange("(c p) t -> p c t", p=128), rd=[], wr=["r_k"])
            kk.dma("sp", rt[:, :, :n], P.PT[R_OC * 128:(R_OC + 4) * 128, c0:c0 + n].rearrange("(c p) t -> p c t", p=128), rd=[], wr=["r_r"])
            kk.dma("sp", wd[:, :n], P.PT[WD_OC * 128:(WD_OC + 1) * 128, c0:c0 + n], rd=[], wr=["r_wd"])
            kk.dma("sp", ad[:, :n], P.PT[AD_OC * 128:(AD_OC + 1) * 128, c0:c0 + n], rd=[], wr=["r_ad"])
            kk.dma("sp", gd[:, :n], P.PT[GD_OC * 128:(GD_OC + 1) * 128, c0:c0 + n], rd=[], wr=["r_gd"])
            kk.act(wd[:, :n], wd[:, :n], AF.Tanh, rd=["r_wd"], wr=["r_wd"])
            kk.act(gd[:, :n], gd[:, :n], AF.Sigmoid, rd=["r_gd"], wr=["r_gd"])
            for fc in range(4):
                fs = slice(fc * 128, (fc + 1) * 128)
                kk.mm(ps_g[:, :n], g2[:, fs], gd[:, :n], True, True, rd=["r_g2", "r_gd"], wr=["r_psg"])
                kk.act(gsb[:, :n], ps_g[:, :n], AF.Copy, rd=["r_psg"], wr=["r_gsb"])
                kk.dma("sp", P.GD[fs, c0:c0 + n], gsb[:, :n], rd=["r_gsb"], wr=[("GD", fc, c0)])
                kk.ts(kkr[:, :n], kt[:, fc, :n], P.col("k_k%d" % j, fc), ALU.mult, rd=["r_k", "cols"], wr=["r_kkr"])
                kk.act(sq[:, :n], kkr[:, :n], AF.Square, rd=["r_kkr"], wr=["r_sq"])
                kk.mm(ps_ss[:, :n], P.consts[:, C_BONES, :], sq[:, :n], True, True, rd=["r_sq", "consts"], wr=["r_psss"])
                kk.act(nrm[:, :n], ps_ss[:, :n], AF.Sqrt, rd=["r_psss"], wr=["r_nrm"])
                kk.ts(nrm[:, :n], nrm[:, :n], 1e-12, ALU.max, rd=["r_nrm"], wr=["r_nrm"])
                kk.op("dve", nc.vector.reciprocal, rd=["r_nrm"], wr=["r_nrm"], out=nrm[:, :n], in_=nrm[:, :n])
                kk.tt(kkn[:, :n], kkr[:, :n], nrm[:, :n], ALU.mult, rd=["r_kkr", "r_nrm"], wr=["r_kkn"])
                for d in range(2):
                    ds_ = slice(d * 64, (d + 1) * 64)
                    kk.mm(ps_w[d][:, :n], w2[ds_, fs], wd[ds_, :n], True, True, rd=["r_w2", "r_wd"], wr=["r_psw%d" % d])
                    kk.act(sg[:, :n], ps_w[d][:, :n], AF.Sigmoid, rd=["r_psw%d" % d, "cols"], wr=["r_sg"],
                           bias=P.col("w0_%d_%d" % (j, d), fc))
                    kk.ts(sg[:, :n], sg[:, :n], -EXPM05, ALU.mult, rd=["r_sg"], wr=["r_sg"])
                    kk.op("dve", nc.vector.tensor_tensor_scan, rd=["r_sg", "r_ones"], wr=["r_csz"], out=csz[:, 1:1 + n],
                          data0=onesf[:, :n], data1=sg[:, :n], initial=0.0, op0=ALU.mult, op1=ALU.add)
                    kk.mm(ps_a[d][:, :n], a2[ds_, fs], ad[ds_, :n], True, True, rd=["r_a2", "r_ad"], wr=["r_psa%d" % d])
                    kk.act(av[:, :n], ps_a[d][:, :n], AF.Sigmoid, rd=["r_psa%d" % d, "cols"], wr=["r_a"],
                           bias=P.col("a0_%d_%d" % (j, d), fc))
                    kk.ts(kd[:, :n], av[:, :n], -1.0, ALU.add, rd=["r_a", "cols"], wr=["r_kd"], s2=P.col("k_a%d" % j, fc), op1=ALU.mult)
                    kk.stt(kd[:, :n], kd[:, :n], 1.0, kt[:, fc, :n], ALU.add, ALU.mult, rd=["r_kd", "r_k"], wr=["r_kd"])
                    kk.stt(nb[:, :n], kkn[:, :n], -1.0, av[:, :n], ALU.mult, ALU.mult, rd=["r_kkn", "r_a"], wr=["r_nb"])
                    if d == 0:
                        kk.cp(ks[:, :n], kd[:, :n], rd=["r_kd"], wr=["r_ks"], e="pool")
                    else:
                        kk.tt(ks[:, :n], ks[:, :n], kd[:, :n], ALU.add, rd=["r_kd", "r_ks"], wr=["r_ks"], e="pool")
                        kk.dma("sp", P.KS[fs, c0:c0 + n], ks[:, :n], rd=["r_ks"], wr=[("KS", fc, c0)])
                    csv = csz[:, 1:1 + n].rearrange("p (c t) -> p c t", t=CH)
                    csp = csz[:, 0:n].rearrange("p (c t) -> p c t", t=CH)
                    sb_ = bc(csz[:, 0:n:CH].unsqueeze(2), [128, nch, CH])
                    eb_ = bc(csz[:, CH:n + 1:CH].unsqueeze(2), [128, nch, CH])
                    a3 = [arg[i][:, :n].rearrange("p (c t) -> p c t", t=CH) for i in range(3)]
                    if d == 0:
                        kk.tt(a3[0], csv, sb_, ALU.subtract, rd=["r_csz"], wr=["r_arg0"])
                        kk.tt(a3[1], csp, sb_, ALU.subtract, rd=["r_csz"], wr=["r_arg1"])
                        kk.tt(a3[2], eb_, csv, ALU.subtract, rd=["r_csz"], wr=["r_arg2"])
                    else:
                        kk.tt(a3[0], eb_, csp, ALU.subtract, rd=["r_csz"], wr=["r_arg0"])
                        kk.tt(a3[1], eb_, csv, ALU.subtract, rd=["r_csz"], wr=["r_arg1"])
                        kk.tt(a3[2], csp, sb_, ALU.subtract, rd=["r_csz"], wr=["r_arg2"])
                    kk.tt(pcs[:, :nch], csz[:, CH:n + 1:CH], csz[:, 0:n:CH], ALU.subtract, rd=["r_csz"], wr=["r_pcs"])
                    kk.act(pcs[:, :nch], pcs[:, :nch], AF.Exp, rd=["r_pcs"], wr=["r_pcs"])
                    kk.dma("sp", P.PCD[d, fs, cb:cb + nch], pcs[:, :nch], rd=["r_pcs"], wr=[("PCD", d, fc, c0)])
                    kk.act(ex[0][:, :n], arg[0][:, :n], AF.Exp, rd=["r_arg0"], wr=["r_ex0"])
                    kk.act(ex[1][:, :n], arg[1][:, :n], AF.Exp, rd=["r_arg1"], wr=["r_ex1"])
                    kk.act(ex[2][:, :n], arg[0][:, :n], AF.Exp, rd=["r_arg0"], wr=["r_ex2"], scale=-1.0)
                    kk.act(ex[3][:, :n], arg[2][:, :n], AF.Exp, rd=["r_arg2"], wr=["r_ex3"])
                    o_, ok_ = outq[oi % 2], "r_out%d" % (oi % 2)
                    oi += 1
                    kk.tt(o_[:, 0, :n], kkn[:, :n], ex[1][:, :n], ALU.mult, rd=["r_kkn", "r_ex1"], wr=[ok_])
                    kk.tt(o_[:, 1, :n], rt[:, fc, :n], ex[0][:, :n], ALU.mult, rd=["r_r", "r_ex0"], wr=[ok_], e="pool")
                    kk.tt(o_[:, 2, :n], kd[:, :n], ex[2][:, :n], ALU.mult, rd=["r_kd", "r_ex2"], wr=[ok_])
                    kk.tt(o_[:, 3, :n], nb[:, :n], ex[2][:, :n], ALU.mult, rd=["r_nb", "r_ex2"], wr=[ok_], e="pool")
                    kk.tt(o_[:, 4, :n], kd[:, :n], ex[3][:, :n], ALU.mult, rd=["r_kd", "r_ex3"], wr=[ok_])
                    kk.tt(o_[:, 5, :n], nb[:, :n], ex[3][:, :n], ALU.mult, rd=["r_nb", "r_ex3"], wr=[ok_], e="pool")
                    kk.dma("sp", P.RQ[d * 6:(d + 1) * 6, fs, c0:c0 + n].rearrange("q p t -> p q t"), o_[:, :, :n], rd=[ok_],
                           wr=[("RQ", d, fc, c0)])


def chunk_col(cg):
    return CTX0 + cg * CH if cg < 4 else LAT0 + (cg - 4) * CH


def phase_rwkv_main(P, L, nsteps=NCHUNK):
    kk = P.k
    nc = kk.nc
    with kk.scope():
        masks = kk.sbuf("m_masks", [64, 4, 64])
        identb = kk.sbuf("m_identb", [64, 64], BF16)
        PCs = kk.sbuf("m_pcs", [64, 2, 8, NCHUNK])
        H = [kk.sbuf("m_H%d" % d, [64, 8, 64]) for d in range(2)]
        Hb = [kk.sbuf("m_Hb%d" % d, [64, 8, 64], BF16) for d in range(2)]
        QS = [[kk.sbuf("m_QS%d_%d" % (d, p), [64, 6, 8, 128], BF16) for p in range(2)] for d in range(2)]
        VS = [[kk.sbuf("m_VS%d_%d" % (d, p), [64, 8, 128]) for p in range(2)] for d in range(2)]
        vtok = kk.sbuf("m_vtok", [64, 8, 64], BF16)
        katok = kk.sbuf("m_katok", [64, 2, 8, 64], BF16)
        akl = kk.sbuf("m_akl", [64, 8, 2, 64], BF16)
        lra = kk.sbuf("m_lra", [64, 8, 64], BF16)
        X = [kk.sbuf("m_X%d" % i, [64, 8, 64]) for i in range(2)]
        Y = [kk.sbuf("m_Y%d" % i, [64, 8, 64]) for i in range(2)]
        TT = kk.sbuf("m_TT", [64, 8, 64])
        TTb = kk.sbuf("m_TTb", [64, 8, 64], BF16)
        x1 = kk.sbuf("m_x1", [64, 8, 64], BF16)
        usb = kk.sbuf("m_usb", [64, 8, 64], BF16)
        ysb = [kk.sbuf("m_ysb%d" % i, [64, 8, 64]) for i in range(2)]
        htmp = kk.sbuf("m_htmp", [64, 8, 64])
        pb = [kk.psum("m_pb%d" % i, [64, 512]) for i in range(7)]
        ptb = kk.psum("m_ptb", [64, 2, 8, 64], BF16)

        def pbv(i, shape):
            ap = pb[i][:, :]
            if len(shape) == 2:
                return ap.rearrange("p (a b) -> p a b", a=shape[0])
            return ap.rearrange("p (a b c) -> p a b c", a=shape[0], b=shape[1])
        kk.dma("sp", masks[:], P.rmaskD[:, :, :], rd=[], wr=["m_masks"])
        kk.cp(identb[:, :], P.ident[0:64, 0:64], rd=["ident"], wr=["m_identb"])
        for d in range(2):
            kk.dma("sp", PCs[:, d], P.PCD[d].rearrange("(h j) c -> j h c", j=64), rd=[], wr=["m_pcs"])
            kk.memset(H[d][:], 0.0, wr=["m_H%d" % d])
            kk.memset(Hb[d][:], 0.0, wr=["m_Hb%d" % d])
        order = [list(range(NCHUNK)), [3, 2, 1, 0] + list(range(NCHUNK - 1, 3, -1))]
        yi = 0

        def load_group(d, gs):
            cg0 = min(order[d][2 * gs], order[d][2 * gs + 1])
            c0 = chunk_col(cg0)
            par = gs % 2
            for q in range(6):
                kk.dma("sp", QS[d][par][:, q], P.RQ[d * 6 + q].rearrange("(h j) t -> j h t", j=64)[:, :, c0:c0 + 128], rd=[],
                       wr=[("m_QS", d, par)])
            kk.dma("sp", VS[d][par][:], P.PT[V_OC * 128:(V_OC + 4) * 128, :].rearrange("(h j) t -> j h t", j=64)[:, :, c0:c0 + 128],
                   rd=[], wr=[("m_VS", d, par)])
            return cg0

        for d in range(2):
            load_group(d, 0)
        for n in range(nsteps):
            gs = n // 2
            for d in range(2):
                cg = order[d][n]
                if n % 2 == 0 and 2 * (gs + 1) < NCHUNK:
                    load_group(d, gs + 1)
                par = gs % 2
                cg0 = min(order[d][2 * gs], order[d][2 * gs + 1])
                co = (cg - cg0) * CH
                qs, qk = QS[d][par], ("m_QS", d, par)
                vs, vk = VS[d][par], ("m_VS", d, par)
                csl = slice(co, co + CH)
                m_strict, m_incl, m_a = (M_SU, M_IU, M_SL) if d == 0 else (M_SL, M_IL, M_SU)
                for h in range(8):
                    kk.op("pe", nc.tensor.transpose, rd=[vk, "ident"], wr=["m_pb0"], out=pbv(0, [8, 64])[:, h, :], in_=vs[:, h, csl],
                          identity=P.ident[0:64, 0:64])
                kk.act(vtok[:], pbv(0, [8, 64]), AF.Copy, rd=["m_pb0"], wr=["m_vtok"])
                for qi_, q in enumerate((4, 5)):
                    for h in range(8):
                        kk.op("pe", nc.tensor.transpose, rd=[qk, "m_identb"], wr=["m_ptb"], out=ptb[:, qi_, h, :], in_=qs[:, q, h, csl],
                              identity=identb[:, :])
                kk.act(katok[:], ptb[:], AF.Copy, rd=["m_ptb"], wr=["m_katok"])
                for u in range(8):
                    br = qs[:, 0:2, u, csl]
                    kk.mm(pbv(1 + u // 4, [4, 2, 64])[:, u % 4], qs[:, 2, u, csl], br, True, True, rd=[qk], wr=["m_pb%d" % (1 + u // 4)])
                    kk.mm(pbv(3 + u // 4, [4, 2, 64])[:, u % 4], qs[:, 3, u, csl], br, True, True, rd=[qk], wr=["m_pb%d" % (3 + u // 4)])
                    kk.mm(pbv(5, [8, 64])[:, u], qs[:, 0, u, csl], qs[:, 3, u, csl], True, True, rd=[qk], wr=["m_pb5"])
                mk2 = masks[:, m_strict:m_strict + 2, :]
                for hb in range(2):
                    kk.tt(akl[:, hb * 4:(hb + 1) * 4], pbv(1 + hb, [4, 2, 64]), bc(mk2.unsqueeze(1), [64, 4, 2, 64]), ALU.mult,
                          rd=["m_pb%d" % (1 + hb), "m_masks"], wr=["m_akl"])
                    kk.tt(Y[0][:, hb * 4:(hb + 1) * 4], pbv(3 + hb, [4, 2, 64])[:, :, 0, :],
                          bc(masks[:, m_strict, :].unsqueeze(1), [64, 4, 64]), ALU.mult, rd=["m_pb%d" % (3 + hb), "m_masks"], wr=["m_Y0"])
                    kk.tt(lra[:, hb * 4:(hb + 1) * 4], pbv(3 + hb, [4, 2, 64])[:, :, 1, :],
                          bc(masks[:, m_incl, :].unsqueeze(1), [64, 4, 64]), ALU.mult, rd=["m_pb%d" % (3 + hb), "m_masks"], wr=["m_lra"])
                kk.tt(X[0][:], pbv(5, [8, 64]), bc(masks[:, m_a, :].unsqueeze(1), [64, 8, 64]), ALU.mult, rd=["m_pb5", "m_masks"], wr=["m_X0"])
                kk.tt(TT[:], Y[0][:], bc(P.ident[0:64, 0:64].unsqueeze(1), [64, 8, 64]), ALU.add, rd=["m_Y0", "ident"], wr=["m_TT"])
                for lv in range(5):
                    xi, xo = X[lv % 2], X[(lv + 1) % 2]
                    yi_, yo = Y[lv % 2], Y[(lv + 1) % 2]
                    xik, xok = "m_X%d" % (lv % 2), "m_X%d" % ((lv + 1) % 2)
                    yik, yok = "m_Y%d" % (lv % 2), "m_Y%d" % ((lv + 1) % 2)
                    for u in range(8):
                        kk.mm(pbv(1, [8, 64])[:, u], yi_[:, u], xi[:, u], True, True, rd=[xik, yik], wr=["m_pb1"])
                        kk.mm(pbv(2, [8, 64])[:, u], xi[:, u], yi_[:, u], True, True, rd=[xik, yik], wr=["m_pb2"])
                    kk.act(xo[:], pbv(1, [8, 64]), AF.Copy, rd=["m_pb1"], wr=[xok])
                    kk.cp(yo[:], pbv(2, [8, 64]), rd=["m_pb2"], wr=[yok])
                    for u in range(8):
                        kk.mm(pbv(3, [8, 64])[:, u], xo[:, u], TT[:, u], True, True, rd=[xok, "m_TT"], wr=["m_pb3"])
                    kk.tt(TT[:], TT[:], pbv(3, [8, 64]), ALU.add, rd=["m_pb3", "m_TT"], wr=["m_TT"])
                kk.cp(TTb[:], TT[:], rd=["m_TT"], wr=["m_TTb"])
                hk, hbk = "m_H%d" % d, "m_Hb%d" % d
                for u in range(8):
                    kk.mm(pbv(4, [8, 64])[:, u], qs[:, 0, u, csl], Hb[d][:, u], True, False, rd=[qk, hbk], wr=["m_pb4"])
                    kk.mm(pbv(4, [8, 64])[:, u], akl[:, u, 0, :], vtok[:, u], False, True, rd=["m_akl", "m_vtok"], wr=["m_pb4"])
                kk.act(x1[:], pbv(4, [8, 64]), AF.Copy, rd=["m_pb4"], wr=["m_x1"])
                for u in range(8):
                    kk.mm(pbv(5, [8, 64])[:, u], TTb[:, u], x1[:, u], True, True, rd=["m_TTb", "m_x1"], wr=["m_pb5"])
                kk.act(usb[:], pbv(5, [8, 64]), AF.Copy, rd=["m_pb5"], wr=["m_usb"])
                for u in range(8):
                    kk.mm(pbv(6, [8, 64])[:, u], Hb[d][:, u], qs[:, 1, u, csl], True, False, rd=[qk, hbk], wr=["m_pb6"])
                    kk.mm(pbv(6, [8, 64])[:, u], vtok[:, u], akl[:, u, 1, :], False, False, rd=["m_akl", "m_vtok"], wr=["m_pb6"])
                    kk.mm(pbv(6, [8, 64])[:, u], usb[:, u], lra[:, u], False, True, rd=["m_usb", "m_lra"], wr=["m_pb6"])
                    kk.mm(pbv(0, [8, 64])[:, u], katok[:, 0, u], vtok[:, u], True, False, rd=["m_katok", "m_vtok"], wr=["m_pb0"])
                    kk.mm(pbv(0, [8, 64])[:, u], katok[:, 1, u], usb[:, u], False, True, rd=["m_katok", "m_usb"], wr=["m_pb0"])
                ys, ysk = ysb[yi % 2], "m_ysb%d" % (yi % 2)
                yi += 1
                kk.act(ys[:], pbv(6, [8, 64]), AF.Copy, rd=["m_pb6"], wr=[ysk])
                c0 = chunk_col(cg)
                kk.dma("sp", P.YD[d].rearrange("(h i) t -> i h t", i=64)[:, :, c0:c0 + CH], ys[:], rd=[ysk], wr=[("YD", d, cg)])
                kk.tt(htmp[:], H[d][:], bc(PCs[:, d, :, cg].unsqueeze(2), [64, 8, 64]), ALU.mult, rd=[hk, "m_pcs"], wr=["m_htmp"])
                kk.tt(H[d][:], htmp[:], pbv(0, [8, 64]), ALU.add, rd=["m_htmp", "m_pb0"], wr=[hk])
                kk.cp(Hb[d][:], H[d][:], rd=[hk], wr=[hbk], e="pool")


def phase_rwkv_out(P, L):
    kk = P.k
    nc = kk.nc
    j = L // 2
    with kk.scope():
        names = ["o_y0", "o_y1", "o_r", "o_v", "o_ks", "o_g"]
        inb = {nm: [kk.sbuf("%s_%d" % (nm, i), [128, 512]) for i in range(2)] for nm in names}
        mean = kk.sbuf("o_mean", [128, 512])
        yc = kk.sbuf("o_yc", [128, 512])
        sq = kk.sbuf("o_sq", [128, 512])
        rstd = kk.sbuf("o_rstd", [128, 512])
        rk = kk.sbuf("o_rk", [128, 512])
        res = [kk.sbuf("o_res%d" % i, [128, 512]) for i in range(2)]
        ps_m = kk.psum("o_psm", [128, 512])
        ps_v = kk.psum("o_psv", [128, 512])
        ps_b = kk.psum("o_psb", [128, 512])
        items = [(c0, n, s, fc) for (c0, n, s) in TILES for fc in range(4)]

        def ld(i, it):
            c0, n, s, fc = it
            fs = slice(fc * 128, (fc + 1) * 128)
            p = i % 2
            srcs = {"o_y0": P.YD[0, fs, c0:c0 + n], "o_y1": P.YD[1, fs, c0:c0 + n],
                    "o_r": P.PT[(R_OC + fc) * 128:(R_OC + fc + 1) * 128, c0:c0 + n],
                    "o_v": P.PT[(V_OC + fc) * 128:(V_OC + fc + 1) * 128, c0:c0 + n],
                    "o_ks": P.KS[fs, c0:c0 + n], "o_g": P.GD[fs, c0:c0 + n]}
            for nm in names:
                kk.dma("sp", inb[nm][p][:, :n], srcs[nm], rd=[], wr=["%s_%d" % (nm, p)])

        def comp(i, it):
            c0, n, s, fc = it
            fs = slice(fc * 128, (fc + 1) * 128)
            p = i % 2
            y0, y1, rr, vv, ksb, gg = [inb[nm][p] for nm in names]
            k0, k1, kr, kv, kks, kg = ["%s_%d" % (nm, p) for nm in names]
            kk.tt(y0[:, :n], y0[:, :n], y1[:, :n], ALU.add, rd=[k0, k1], wr=[k0])
            kk.mm(ps_m[:, :n], P.consts[:, C_BONES, :], y0[:, :n], True, True, rd=[k0, "consts"], wr=["o_psm"])
            kk.stt(yc[:, :n], ps_m[:, :n], -1.0 / 64, y0[:, :n], ALU.mult, ALU.add, rd=["o_psm", k0], wr=["o_yc"])
            kk.act(sq[:, :n], yc[:, :n], AF.Square, rd=["o_yc"], wr=["o_sq"])
            kk.mm(ps_v[:, :n], P.consts[:, C_BONES, :], sq[:, :n], True, True, rd=["o_sq", "consts"], wr=["o_psv"])
            kk.act(rstd[:, :n], ps_v[:, :n], AF.Sqrt, rd=["o_psv", "cols"], wr=["o_rstd"], bias=P.col("gneps"), scale=1.0 / 64)
            kk.op("dve", nc.vector.reciprocal, rd=["o_rstd"], wr=["o_rstd"], out=rstd[:, :n], in_=rstd[:, :n])
            kk.tt(yc[:, :n], yc[:, :n], rstd[:, :n], ALU.mult, rd=["o_yc", "o_rstd"], wr=["o_yc"])
            kk.ts(yc[:, :n], yc[:, :n], P.col("ln_w%d" % j, fc), ALU.mult, rd=["o_yc", "cols"], wr=["o_yc"],
                  s2=P.col("ln_b%d" % j, fc), op1=ALU.add)
            kk.stt(rk[:, :n], rr[:, :n], P.col("r_k%d" % j, fc), ksb[:, :n], ALU.mult, ALU.mult, rd=[kr, kks, "cols"], wr=["o_rk"])
            kk.mm(ps_b[:, :n], P.consts[:, C_BONES, :], rk[:, :n], True, True, rd=["o_rk", "consts"], wr=["o_psb"])
            kk.tt(rk[:, :n], ps_b[:, :n], vv[:, :n], ALU.mult, rd=["o_psb", kv], wr=["o_rk"])
            kk.tt(yc[:, :n], yc[:, :n], rk[:, :n], ALU.add, rd=["o_yc", "o_rk"], wr=["o_yc"], e="pool")
            r_, rk_ = res[i % 2], "o_res%d" % (i % 2)
            kk.tt(r_[:, :n], yc[:, :n], gg[:, :n], ALU.mult, rd=["o_yc", kg], wr=[rk_], e="pool")
            kk.dma("sp", P.MT[fs, c0:c0 + n], r_[:, :n], rd=[rk_], wr=[("MT", fc, c0)])
        pipelined(items, ld, comp)


def phase_rwkv_main2(P, L, nsteps=NCHUNK):
    kk = P.k
    nc = kk.nc
    with kk.scope():
        masks = kk.sbuf("m_masks", [64, 4, 64])
        identb = kk.sbuf("m_identb", [64, 64], BF16)
        PCs = kk.sbuf("m_pcs", [64, 2, 8, NCHUNK])
        kk.dma("sp", masks[:], P.rmaskD[:, :, :], rd=[], wr=["m_masks"])
        kk.cp(identb[:, :], P.ident[0:64, 0:64], rd=["ident"], wr=["m_identb"])
        order = [list(range(NCHUNK)), [3, 2, 1, 0] + list(range(NCHUNK - 1, 3, -1))]
        ident64 = P.ident[0:64, 0:64]

        class Dir:
            pass
        dirs = []
        for d in range(2):
            o = Dir()
            o.d = d
            t = "%d" % d
            o.H = kk.sbuf("m_H" + t, [64, 8, 64])
            o.Hb = kk.sbuf("m_Hb" + t, [64, 8, 64], BF16)
            o.QS = [kk.sbuf("m_QS%s_%d" % (t, p), [64, 6, 8, 128], BF16) for p in range(2)]
            o.VS = [kk.sbuf("m_VS%s_%d" % (t, p), [64, 8, 128]) for p in range(2)]
            o.vtok = kk.sbuf("m_vtok" + t, [64, 8, 64], BF16)
            o.katok = kk.sbuf("m_katok" + t, [64, 2, 8, 64], BF16)
            o.akl = kk.sbuf("m_akl" + t, [64, 8, 2, 64], BF16)
            o.lra = kk.sbuf("m_lra" + t, [64, 8, 64], BF16)
            o.X = [kk.sbuf("m_X%d_%s" % (i, t), [64, 8, 64]) for i in range(2)]
            o.Y = [kk.sbuf("m_Y%d_%s" % (i, t), [64, 8, 64]) for i in range(2)]
            o.TT = kk.sbuf("m_TT" + t, [64, 8, 64])
            o.TTb = kk.sbuf("m_TTb" + t, [64, 8, 64], BF16)
            o.x1 = kk.sbuf("m_x1" + t, [64, 8, 64], BF16)
            o.usb = kk.sbuf("m_usb" + t, [64, 8, 64], BF16)
            o.ysb = [kk.sbuf("m_ysb%d_%s" % (i, t), [64, 8, 64]) for i in range(2)]
            o.htmp = kk.sbuf("m_htmp" + t, [64, 8, 64])
            o.pb = [kk.psum("m_pb%d_%s" % (i, t), [64, 512]) for i in range(4)]
            o.yi = 0
            kk.dma("sp", PCs[:, d], P.PCD[d].rearrange("(h j) c -> j h c", j=64), rd=[], wr=["m_pcs"])
            kk.memset(o.H[:], 0.0, wr=["m_H" + t])
            kk.memset(o.Hb[:], 0.0, wr=["m_Hb" + t])
            dirs.append(o)

        def K(o, nm):
            return "%s_%d" % (nm, o.d)

        def pbv(o, i, shape):
            ap = o.pb[i][:, :]
            if len(shape) == 2:
                return ap.rearrange("p (a b) -> p a b", a=shape[0])
            return ap.rearrange("p (a b c) -> p a b c", a=shape[0], b=shape[1])

        def pbk(o, i):
            return "m_pb%d_%d" % (i, o.d)

        def load_group(o, gs):
            d = o.d
            cg0 = min(order[d][2 * gs], order[d][2 * gs + 1])
            c0 = chunk_col(cg0)
            par = gs % 2
            for q in range(6):
                kk.dma("sp", o.QS[par][:, q], P.RQ[d * 6 + q].rearrange("(h j) t -> j h t", j=64)[:, :, c0:c0 + 128], rd=[],
                       wr=[("m_QS", d, par)])
            kk.dma("sp", o.VS[par][:], P.PT[V_OC * 128:(V_OC + 4) * 128, :].rearrange("(h j) t -> j h t", j=64)[:, :, c0:c0 + 128],
                   rd=[], wr=[("m_VS", d, par)])

        def stages(o, n):
            d = o.d
            gs = n // 2
            cg = order[d][n]
            par = gs % 2
            cg0 = min(order[d][2 * gs], order[d][2 * gs + 1])
            co = (cg - cg0) * CH
            qs, qk = o.QS[par], ("m_QS", d, par)
            vs, vk = o.VS[par], ("m_VS", d, par)
            csl = slice(co, co + CH)
            m_strict, m_incl, m_a = (M_SU, M_IU, M_SL) if d == 0 else (M_SL, M_IL, M_SU)
            hk, hbk = K(o, "m_H"), K(o, "m_Hb")
            st = []

            def s_load():
                if n % 2 == 0 and 2 * (gs + 1) < NCHUNK:
                    load_group(o, gs + 1)
            st.append(s_load)

            def s_tr():
                for h in range(8):
                    kk.op("pe", nc.tensor.transpose, rd=[vk, "ident"], wr=[pbk(o, 0)], out=pbv(o, 0, [8, 64])[:, h, :], in_=vs[:, h, csl],
                          identity=ident64)
                ptb = o.pb[1][:, :].bitcast(BF16).rearrange("p (a b c) -> p a b c", a=2, b=8)
                for qi_, q in enumerate((4, 5)):
                    for h in range(8):
                        kk.op("pe", nc.tensor.transpose, rd=[qk, "m_identb"], wr=[pbk(o, 1)], out=ptb[:, qi_, h, :], in_=qs[:, q, h, csl],
                              identity=identb[:, :])
                kk.act(o.vtok[:], pbv(o, 0, [8, 64]), AF.Copy, rd=[pbk(o, 0)], wr=[K(o, "m_vtok")])
                kk.act(o.katok[:], ptb, AF.Copy, rd=[pbk(o, 1)], wr=[K(o, "m_katok")])
            st.append(s_tr)

            def s_m1():
                for u in range(8):
                    br = qs[:, 0:2, u, csl]
                    kk.mm(pbv(o, 2 + u // 4, [4, 2, 64])[:, u % 4], qs[:, 2, u, csl], br, True, True, rd=[qk], wr=[pbk(o, 2 + u // 4)])
                    kk.mm(pbv(o, 0, [8, 64])[:, u], qs[:, 0, u, csl], qs[:, 3, u, csl], True, True, rd=[qk], wr=[pbk(o, 0)])
                mk2 = masks[:, m_strict:m_strict + 2, :]
                for hb in range(2):
                    kk.tt(o.akl[:, hb * 4:(hb + 1) * 4], pbv(o, 2 + hb, [4, 2, 64]), bc(mk2.unsqueeze(1), [64, 4, 2, 64]), ALU.mult,
                          rd=[pbk(o, 2 + hb), "m_masks"], wr=[K(o, "m_akl")])
                kk.tt(o.X[0][:], pbv(o, 0, [8, 64]), bc(masks[:, m_a, :].unsqueeze(1), [64, 8, 64]), ALU.mult,
                      rd=[pbk(o, 0), "m_masks"], wr=[K(o, "m_X0")])
            st.append(s_m1)

            def s_m2():
                for u in range(8):
                    br = qs[:, 0:2, u, csl]
                    kk.mm(pbv(o, 2 + u // 4, [4, 2, 64])[:, u % 4], qs[:, 3, u, csl], br, True, True, rd=[qk], wr=[pbk(o, 2 + u // 4)])
                for hb in range(2):
                    kk.tt(o.Y[0][:, hb * 4:(hb + 1) * 4], pbv(o, 2 + hb, [4, 2, 64])[:, :, 0, :],
                          bc(masks[:, m_strict, :].unsqueeze(1), [64, 4, 64]), ALU.mult, rd=[pbk(o, 2 + hb), "m_masks"], wr=[K(o, "m_Y0")])
                    kk.tt(o.lra[:, hb * 4:(hb + 1) * 4], pbv(o, 2 + hb, [4, 2, 64])[:, :, 1, :],
                          bc(masks[:, m_incl, :].unsqueeze(1), [64, 4, 64]), ALU.mult, rd=[pbk(o, 2 + hb), "m_masks"], wr=[K(o, "m_lra")])
                kk.tt(o.TT[:], o.Y[0][:], bc(ident64.unsqueeze(1), [64, 8, 64]), ALU.add, rd=[K(o, "m_Y0"), "ident"], wr=[K(o, "m_TT")])
            st.append(s_m2)

            for lv in range(5):
                def s_sq(lv=lv):
                    xi, yi_ = o.X[lv % 2], o.Y[lv % 2]
                    xik, yik = K(o, "m_X%d" % (lv % 2)), K(o, "m_Y%d" % (lv % 2))
                    for u in range(8):
                        kk.mm(pbv(o, 0, [8, 64])[:, u], yi_[:, u], xi[:, u], True, True, rd=[xik, yik], wr=[pbk(o, 0)])
                        if lv < 4:
                            kk.mm(pbv(o, 1, [8, 64])[:, u], xi[:, u], yi_[:, u], True, True, rd=[xik, yik], wr=[pbk(o, 1)])
                    kk.act(o.X[(lv + 1) % 2][:], pbv(o, 0, [8, 64]), AF.Copy, rd=[pbk(o, 0)], wr=[K(o, "m_X%d" % ((lv + 1) % 2))])
                    if lv < 4:
                        kk.cp(o.Y[(lv + 1) % 2][:], pbv(o, 1, [8, 64]), rd=[pbk(o, 1)], wr=[K(o, "m_Y%d" % ((lv + 1) % 2))])
                st.append(s_sq)

                def s_tt(lv=lv):
                    xo, xok = o.X[(lv + 1) % 2], K(o, "m_X%d" % ((lv + 1) % 2))
                    for u in range(8):
                        kk.mm(pbv(o, 2, [8, 64])[:, u], xo[:, u], o.TT[:, u], True, True, rd=[xok, K(o, "m_TT")], wr=[pbk(o, 2)])
                    kk.tt(o.TT[:], o.TT[:], pbv(o, 2, [8, 64]), ALU.add, rd=[pbk(o, 2), K(o, "m_TT")], wr=[K(o, "m_TT")])
                st.append(s_tt)

            def s_a():
                kk.cp(o.TTb[:], o.TT[:], rd=[K(o, "m_TT")], wr=[K(o, "m_TTb")])
                for u in range(8):
                    kk.mm(pbv(o, 3, [8, 64])[:, u], qs[:, 0, u, csl], o.Hb[:, u], True, False, rd=[qk, hbk], wr=[pbk(o, 3)])
                    kk.mm(pbv(o, 3, [8, 64])[:, u], o.akl[:, u, 0, :], o.vtok[:, u], False, True, rd=[K(o, "m_akl"), K(o, "m_vtok")], wr=[pbk(o, 3)])
                kk.act(o.x1[:], pbv(o, 3, [8, 64]), AF.Copy, rd=[pbk(o, 3)], wr=[K(o, "m_x1")])
            st.append(s_a)

            def s_c():
                for u in range(8):
                    kk.mm(pbv(o, 0, [8, 64])[:, u], o.TTb[:, u], o.x1[:, u], True, True, rd=[K(o, "m_TTb"), K(o, "m_x1")], wr=[pbk(o, 0)])
                kk.act(o.usb[:], pbv(o, 0, [8, 64]), AF.Copy, rd=[pbk(o, 0)], wr=[K(o, "m_usb")])
            st.append(s_c)

            def s_de():
                for u in range(8):
                    kk.mm(pbv(o, 1, [8, 64])[:, u], o.Hb[:, u], qs[:, 1, u, csl], True, False, rd=[qk, hbk], wr=[pbk(o, 1)])
                    kk.mm(pbv(o, 1, [8, 64])[:, u], o.vtok[:, u], o.akl[:, u, 1, :], False, False, rd=[K(o, "m_akl"), K(o, "m_vtok")], wr=[pbk(o, 1)])
                    kk.mm(pbv(o, 1, [8, 64])[:, u], o.usb[:, u], o.lra[:, u], False, True, rd=[K(o, "m_usb"), K(o, "m_lra")], wr=[pbk(o, 1)])
                    kk.mm(pbv(o, 2, [8, 64])[:, u], o.katok[:, 0, u], o.vtok[:, u], True, False, rd=[K(o, "m_katok"), K(o, "m_vtok")], wr=[pbk(o, 2)])
                    kk.mm(pbv(o, 2, [8, 64])[:, u], o.katok[:, 1, u], o.usb[:, u], False, True, rd=[K(o, "m_katok"), K(o, "m_usb")], wr=[pbk(o, 2)])
                ys, ysk = o.ysb[o.yi % 2], K(o, "m_ysb%d" % (o.yi % 2))
                o.yi += 1
                kk.act(ys[:], pbv(o, 1, [8, 64]), AF.Copy, rd=[pbk(o, 1)], wr=[ysk])
                c0 = chunk_col(cg)
                kk.dma("sp", P.YD[d].rearrange("(h i) t -> i h t", i=64)[:, :, c0:c0 + CH], ys[:], rd=[ysk], wr=[("YD", d, cg)])
                kk.tt(o.htmp[:], o.H[:], bc(PCs[:, d, :, cg].unsqueeze(2), [64, 8, 64]), ALU.mult, rd=[hk, "m_pcs"], wr=[K(o, "m_htmp")])
                kk.tt(o.H[:], o.htmp[:], pbv(o, 2, [8, 64]), ALU.add, rd=[K(o, "m_htmp"), pbk(o, 2)], wr=[hk])
                kk.cp(o.Hb[:], o.H[:], rd=[hk], wr=[hbk], e="pool")
            st.append(s_de)
            return st

        for o in dirs:
            load_group(o, 0)
        seqs = [[], []]
        for n in range(nsteps):
            for o in dirs:
                seqs[o.d] += stages(o, n)
        lag = 8
        nn = max(len(seqs[0]), len(seqs[1]) + lag)
        for k in range(nn):
            if k < len(seqs[0]):
                seqs[0][k]()
            if 0 <= k - lag < len(seqs[1]):
                seqs[1][k - lag]()
NFFT = 2 * SEQ
CG = 4


def host_hy_consts():
    f32 = np.float32
    t1 = np.arange(64)[:, None]
    f1 = np.arange(64)[None, :]
    ph1 = 2 * np.pi * (t1 * f1 % 64) / 64.0
    F1 = np.concatenate([np.cos(ph1), -np.sin(ph1)], 1).astype(f32)
    t2 = np.arange(128)[:, None]
    tw = 2 * np.pi * (t2 * f1) / float(NFFT)
    TW1 = np.stack([np.cos(tw), -np.sin(tw)], 1).astype(f32)
    f2 = np.arange(128)[None, :]
    th = 2 * np.pi * (t2 * f2 % 128) / 128.0
    C2, S2 = np.cos(th), np.sin(th)
    F2 = np.stack([C2, S2, -S2, C2], 1).astype(f32)
    tw2 = 2 * np.pi * (np.arange(64)[:, None] * np.arange(128)[None, :]) / float(NFFT)
    TW2 = np.stack([np.cos(tw2), np.sin(tw2)], 1).astype(f32)
    ph = 2 * np.pi * (np.arange(64)[:, None] * np.arange(32)[None, :] % 64) / 64.0
    FI = np.stack([np.cos(ph) / NFFT, -np.sin(ph) / NFFT], 1).astype(f32)
    out = {"hyF1": F1, "hyTW1": TW1, "hyF2": F2, "hyTW2": TW2, "hyFI": FI}
    for n, nm in ((SEQ, "lat"), (CTXL, "ctx")):
        t = np.linspace(0.0, 1.0, n, dtype=f32)[:, None]
        ang = (f32(2.0 * math.pi) * np.arange(n, dtype=f32)[:, None] / f32(n)).astype(f32)
        f = np.linspace(1e-4, 15, 16, dtype=f32)[None, :]
        z = np.concatenate([t, np.cos(f * ang), -np.sin(f * ang)], -1).astype(f32)
        out["hyZ_" + nm] = np.ascontiguousarray(z.T)
        out["hyT_" + nm] = np.ascontiguousarray(np.tile(t.T, (128, 1)))
    zl = out["hyZ_lat"]
    tl = out["hyT_lat"]
    idx = (SEQ - np.arange(SEQ)) % SEQ
    out["hyZ_rev"] = np.ascontiguousarray(zl[:, idx])
    out["hyT_rev"] = np.ascontiguousarray(tl[:, idx])
    deltas = np.abs(np.linspace(math.log(1e-2) / 1.5, math.log(1e-2) / 0.3, D, dtype=f32)).astype(f32)
    out["hyDcol"] = np.ascontiguousarray(-deltas.reshape(8, 128).T)
    out["hyDrow"] = np.ascontiguousarray(np.tile(deltas[None, :], (128, 1)))
    tc = np.linspace(0.0, 1.0, CTXL, dtype=f32)
    out["hyTcol"] = np.ascontiguousarray(-tc.reshape(2, 128).T)
    tt_ = np.arange(256)[:, None]
    ff_ = np.arange(512)[None, :]
    a = 2 * np.pi * (tt_ * ff_ % 512) / 512.0
    out["hyCF"] = np.stack([np.cos(a), -np.sin(a)], 0).astype(f32).reshape(2, 2, 128, 512)
    out["hyCI"] = np.stack([np.cos(a.T) / 512.0, -np.sin(a.T) / 512.0], 0).astype(f32).reshape(2, 4, 128, 256)
    return out


HY_SHAPES = {"hyF1": [64, 128], "hyTW1": [128, 2, 64], "hyF2": [128, 4, 128], "hyTW2": [64, 2, 128], "hyFI": [64, 2, 32],
             "hyZ_lat": [33, SEQ], "hyT_lat": [128, SEQ], "hyZ_rev": [33, SEQ], "hyT_rev": [128, SEQ], "hyZ_ctx": [33, CTXL], "hyT_ctx": [128, CTXL],
             "hyDcol": [128, 8], "hyDrow": [128, D], "hyTcol": [128, 2], "hyCF": [2, 2, 128, 512], "hyCI": [2, 4, 128, 256]
             }


def hy_inputs(P):
    k = P.k
    P.hy = {}
    for nm, shp in HY_SHAPES.items():
        P.in_names.append(nm)
        P.hy[nm] = k.dram(nm, shp, F32, "ExternalInput")
    for nm, shp in (("od_f_w1", [2, 33, 64]), ("od_f_w2", [2, 64, 64]), ("od_f_w3", [2, 64, 64]), ("od_f_out", [2, 64, 4096]),
                    ("od_skip", [2, 2, D])):
        P.in_names.append(nm)
        P.hy[nm] = k.dram(nm, shp, F32, "ExternalInput")
    P.HFD = P.scr("HFD", [2 * D, NFFT])
    P.KF = P.scr("KF", [2, 128, D, 2, 64], BF16)
    P.HFC = P.scr("HFC", [CTXL, 4 * D])


def phase_hy_proj(P, L):
    kk = P.k
    j = L // 2
    with kk.scope():
        hT = kk.sbuf("hT1", [128, 8, NT], BF16)
        P.phase_norm(L, 0, hT)
        with kk.scope():
            cv = [kk.sbuf("hcv%d" % i, [128, NT]) for i in range(2)]

            def row_fn(oc, row, rk):
                dst, dk = cv[oc % 2], "hcv%d" % (oc % 2)
                w0 = P.col("od_cw%d_0" % j, oc)
                w1 = P.col("od_cw%d_1" % j, oc)
                w2 = P.col("od_cw%d_2" % j, oc)
                b = P.col("od_cb%d" % j, oc)
                kk.act(dst[:, 1:NT - 1], row[:, 1:NT - 1], AF.Identity, rd=[rk, "cols"], wr=[dk], scale=w1, bias=b)
                kk.stt(dst[:, 1:NT - 1], row[:, 0:NT - 2], w0, dst[:, 1:NT - 1], ALU.mult, ALU.add, rd=[rk, dk, "cols"], wr=[dk])
                kk.stt(dst[:, 1:NT - 1], row[:, 2:NT], w2, dst[:, 1:NT - 1], ALU.mult, ALU.add, rd=[rk, dk, "cols"], wr=[dk])
                kk.dma("sp", P.PT[oc * 128:(oc + 1) * 128, 1:NT - 1], dst[:, 1:NT - 1], rd=[dk], wr=[("PT", oc)])
            P.proj_rows(hT, P.od_w_in[j], 24, row_fn, bias_name="b_in%d" % j)


def hy_mlp(P, j, nm, n, sink_fn):
    kk = P.k
    nc = kk.nc
    w1 = kk.sbuf("f_w1", [33, 64])
    w2 = kk.sbuf("f_w2", [64, 64])
    w3 = kk.sbuf("f_w3", [64, 64])
    kk.dma("sp", w1[:], P.hy["od_f_w1"][j], rd=[], wr=["f_w1"])
    kk.dma("sp", w2[:], P.hy["od_f_w2"][j], rd=[], wr=["f_w2"])
    kk.dma("sp", w3[:], P.hy["od_f_w3"][j], rd=[], wr=["f_w3"])
    zt = kk.sbuf("f_zt", [33, 512])
    hb = [kk.sbuf("f_h%d" % i, [64, 512]) for i in range(2)]
    r = kk.sbuf("f_r", [64, 512])
    ri = kk.sbuf("f_ri", [64, 512], mybir.dt.int32)
    rf = kk.sbuf("f_rf", [64, 512])
    msk = kk.sbuf("f_m", [64, 512])
    ps = kk.psum("f_ps", [64, 512])
    freq = P.col("f_freq%d" % j)[0:64, :]
    KOFF = 64.5

    def sin_layer(src_ps, bname, dst, dk, nt):
        b = P.col(bname)[0:64, :]
        kk.ts(r[:, :nt], src_ps[:, :nt], b, ALU.add, rd=["f_ps", "cols"], wr=["f_r"], s2=freq, op1=ALU.mult)
        kk.ts(r[:, :nt], r[:, :nt], 1.0 / (2 * math.pi), ALU.mult, rd=["f_r"], wr=["f_r"], s2=KOFF, op1=ALU.add)
        kk.cp(ri[:, :nt], r[:, :nt], rd=["f_r"], wr=["f_ri"])
        kk.cp(rf[:, :nt], ri[:, :nt], rd=["f_ri"], wr=["f_rf"])
        kk.tt(r[:, :nt], r[:, :nt], rf[:, :nt], ALU.subtract, rd=["f_r", "f_rf"], wr=["f_r"])
        kk.ts(msk[:, :nt], r[:, :nt], 0.0, ALU.is_lt, rd=["f_r"], wr=["f_m"])
        kk.tt(r[:, :nt], r[:, :nt], msk[:, :nt], ALU.add, rd=["f_r", "f_m"], wr=["f_r"])
        kk.act(dst[:, :nt], r[:, :nt], AF.Sin, rd=["f_r", "cols"], wr=[dk], scale=6.28318, bias=P.col("negpi")[0:64, :])

    ti = 0
    for c0 in range(0, n, 512):
        nt = min(512, n - c0)
        kk.dma("sp", zt[:, :nt], P.hy["hyZ_" + nm][:, c0:c0 + nt], rd=[], wr=["f_zt"])
        kk.mm(ps[:, :nt], w1[:, :], zt[:, :nt], True, True, rd=["f_w1", "f_zt"], wr=["f_ps"])
        sin_layer(ps, "f_b1%d" % j, hb[0], "f_h0", nt)
        kk.mm(ps[:, :nt], w2[:, :], hb[0][:, :nt], True, True, rd=["f_w2", "f_h0"], wr=["f_ps"])
        sin_layer(ps, "f_b2%d" % j, hb[1], "f_h1", nt)
        kk.mm(ps[:, :nt], w3[:, :], hb[1][:, :nt], True, True, rd=["f_w3", "f_h1"], wr=["f_ps"])
        sin_layer(ps, "f_b3%d" % j, hb[0], "f_h0", nt)
        sink_fn(ti, c0, nt, hb[0], "f_h0")
        ti += 1


def phase_hy_filter_lat(P, L):
    kk = P.k
    j = L // 2
    with kk.scope():
        fo = kk.sbuf("f_out", [64, 4096])
        tl = kk.sbuf("f_tl", [128, 512])
        win = kk.sbuf("f_win", [128, 8, 512])
        dcol = kk.sbuf("f_dcol", [128, 8])
        hb0 = kk.sbuf("f_hb0", [128, 16])
        ob = [kk.sbuf("f_ob%d" % i, [128, 512]) for i in range(2)]
        pso = [kk.psum("f_pso%d" % i, [128, 512]) for i in range(2)]
        kk.dma("sp", fo[:], P.hy["od_f_out"][j], rd=[], wr=["f_out"])
        kk.dma("sp", dcol[:], P.hy["hyDcol"][:, :], rd=[], wr=["f_dcol"])
        cnt = [0]

        def make_sink(mode):
            tname = "hyT_rev" if mode == 2 else "hyT_lat"

            def sink(ti, c0, nt, h3, hk):
                kk.dma("sp", tl[:, :nt], P.hy[tname][:, c0:c0 + nt], rd=[], wr=["f_tl"])
                for cc in range(8):
                    kk.act(win[:, cc, :nt], tl[:, :nt], AF.Exp, rd=["f_tl", "f_dcol"], wr=["f_win"], scale=dcol[:, cc:cc + 1])
                for o in range(2):
                    for cc in range(8):
                        oc = (o * 2 + (1 if mode != 1 else 0)) * 8 + cc
                        i = cnt[0]
                        cnt[0] += 1
                        ps, pk = pso[i % 2], "f_pso%d" % (i % 2)
                        o_, ok_ = ob[i % 2], "f_ob%d" % (i % 2)
                        kk.mm(ps[:, :nt], fo[:, oc * 128:(oc + 1) * 128], h3[:, :nt], True, True, rd=["f_out", hk], wr=[pk])
                        if mode == 0:
                            kk.stt(hb0[:, o * 8 + cc:o * 8 + cc + 1], win[:, cc, 0:1], 0.05, ps[:, 0:1], ALU.add, ALU.mult,
                                   rd=["f_win", pk], wr=["f_hb0"])
                            continue
                        kk.stt(o_[:, :nt], win[:, cc, :nt], 0.05, ps[:, :nt], ALU.add, ALU.mult, rd=["f_win", pk], wr=[ok_])
                        if c0 == 0 and mode == 1:
                            kk.tt(o_[:, 0:1], o_[:, 0:1], hb0[:, o * 8 + cc:o * 8 + cc + 1], ALU.add, rd=[ok_, "f_hb0"], wr=[ok_])
                            kk.tt(o_[:, 0:1], o_[:, 0:1], P.col("skip%d_%d" % (j, o), cc), ALU.add, rd=[ok_, "cols"], wr=[ok_])
                        if c0 == 0 and mode == 2:
                            kk.memset(o_[:, 0:1], 0.0, wr=[ok_], e="dve")
                        col0 = c0 if mode == 1 else SEQ + c0
                        kk.dma("sp", P.HFD[(o * 8 + cc) * 128:(o * 8 + cc + 1) * 128, col0:col0 + nt], o_[:, :nt], rd=[ok_],
                               wr=[("HFD", mode, o, cc, c0)])
            return sink
        with kk.scope():
            hy_mlp(P, j, "lat", 2, make_sink(0))
        with kk.scope():
            hy_mlp(P, j, "lat", SEQ, make_sink(1))
        with kk.scope():
            hy_mlp(P, j, "rev", SEQ, make_sink(2))


class HyFFT:
    def __init__(self, P, C, tag, ka=32, fwd_only=False):
        kk = P.k
        self.P = P
        self.ka = ka
        self.C = C
        self.tag = tag
        g = tag
        self.psA = kk.psum("y_psA" + g, [128, CG, 128])
        self.psX = kk.psum("y_psX" + g, [128, 2, CG, 64])
        self.tmp = [kk.sbuf("y_tmp%d%s" % (i, g), [128, CG, 64]) for i in range(4)]
        self.Ap = kk.sbuf("y_Ap" + g, [128, CG, 2, 64], BF16)
        if not fwd_only:
            self.psB = kk.psum("y_psB" + g, [64, CG, 2, 128])
            self.tmpB = [kk.sbuf("y_tmpB%d%s" % (i, g), [64, CG, 128]) for i in range(4)]
            self.Z = kk.sbuf("y_Z" + g, [128, CG, 2, 64], BF16)
            self.Bp = kk.sbuf("y_Bp" + g, [64, CG, 2, 128], BF16)
        self.ub = kk.sbuf("y_ub" + g, [ka, CG, 128], BF16)
        self.k = lambda nm: nm + g

    def cmul(self, out_re, out_im, ar, ai, br, bi, tmps, tkeys, rd, wr):
        kk = self.P.k
        kk.tt(tmps[0], ar, br, ALU.mult, rd=rd, wr=[tkeys[0]])
        kk.tt(tmps[1], ai, bi, ALU.mult, rd=rd, wr=[tkeys[1]])
        kk.tt(tmps[2], ar, bi, ALU.mult, rd=rd, wr=[tkeys[2]])
        kk.tt(tmps[3], ai, br, ALU.mult, rd=rd, wr=[tkeys[3]])
        kk.tt(out_re, tmps[0], tmps[1], ALU.subtract, rd=[tkeys[0], tkeys[1]], wr=wr, e="pool")
        kk.tt(out_im, tmps[2], tmps[3], ALU.add, rd=[tkeys[2], tkeys[3]], wr=wr, e="pool")

    def s_cast(self, src, srck):
        self.P.k.act(self.ub[:], src, AF.Copy, rd=[srck], wr=[self.k("y_ub")])

    def s_A(self):
        kk, C = self.P.k, self.C
        for ch in range(CG):
            kk.mm(self.psA[:, ch, :], self.ub[:, ch, :], C["F1"][0:self.ka, :], True, True, rd=[self.k("y_ub"), "y_F1"], wr=[self.k("y_psA")])

    def s_tw1(self):
        C = self.C
        t = [x[:] for x in self.tmp]
        tk = [self.k("y_tmp%d" % i) for i in range(4)]
        self.cmul(self.Ap[:, :, 0, :], self.Ap[:, :, 1, :], self.psA[:, :, 0:64], self.psA[:, :, 64:128],
                  bc(C["TW1"][:, 0, :].unsqueeze(1), [128, CG, 64]), bc(C["TW1"][:, 1, :].unsqueeze(1), [128, CG, 64]),
                  t, tk, rd=[self.k("y_psA"), "y_TW1"], wr=[self.k("y_Ap")])

    def s_C(self):
        kk, C = self.P.k, self.C
        are, aim = self.Ap[:, :, 0, :], self.Ap[:, :, 1, :]
        rd = [self.k("y_Ap"), "y_F2"]
        kk.mm(self.psX[:, 0], C["F2"][:, 0, :], are, True, False, rd=rd, wr=[self.k("y_psX")])
        kk.mm(self.psX[:, 0], C["F2"][:, 1, :], aim, False, True, rd=rd, wr=[self.k("y_psX")])
        kk.mm(self.psX[:, 1], C["F2"][:, 2, :], are, True, False, rd=rd, wr=[self.k("y_psX")])
        kk.mm(self.psX[:, 1], C["F2"][:, 0, :], aim, False, True, rd=rd, wr=[self.k("y_psX")])

    def s_Z(self, kf, kfk):
        t = [x[:] for x in self.tmp]
        tk = [self.k("y_tmp%d" % i) for i in range(4)]
        self.cmul(self.Z[:, :, 0, :], self.Z[:, :, 1, :], self.psX[:, 0], self.psX[:, 1], kf[:, :, 0, :], kf[:, :, 1, :], t, tk,
                  rd=[self.k("y_psX"), kfk], wr=[self.k("y_Z")])

    def s_Cp(self):
        kk, C = self.P.k, self.C
        for ch in range(CG):
            kk.mm(self.psB[:, ch], self.Z[:, ch, 0, :], C["F2"][:, 0:2, :], True, False, rd=[self.k("y_Z"), "y_F2"], wr=[self.k("y_psB")])
            kk.mm(self.psB[:, ch], self.Z[:, ch, 1, :], C["F2"][:, 2:4, :], False, True, rd=[self.k("y_Z"), "y_F2"], wr=[self.k("y_psB")])

    def s_tw2(self):
        C = self.C
        tb = [x[:] for x in self.tmpB]
        tbk = [self.k("y_tmpB%d" % i) for i in range(4)]
        self.cmul(self.Bp[:, :, 0, :], self.Bp[:, :, 1, :], self.psB[:, :, 0, :], self.psB[:, :, 1, :],
                  bc(C["TW2"][:, 0, :].unsqueeze(1), [64, CG, 128]), bc(C["TW2"][:, 1, :].unsqueeze(1), [64, CG, 128]),
                  tb, tbk, rd=[self.k("y_psB"), "y_TW2"], wr=[self.k("y_Bp")])

    def s_Ap(self):
        kk, C = self.P.k, self.C
        kk.mm(self.psA[0:32, :, :], C["FI"][:, 0, :], self.Bp[:, :, 0, :], True, False, rd=[self.k("y_Bp"), "y_FI"], wr=[self.k("y_psA")])
        kk.mm(self.psA[0:32, :, :], C["FI"][:, 1, :], self.Bp[:, :, 1, :], False, True, rd=[self.k("y_Bp"), "y_FI"], wr=[self.k("y_psA")])


def hy_consts(P):
    kk = P.k
    C = {}
    f1 = kk.sbuf("y_f1s", [64, 128])
    C["F1"] = kk.sbuf("y_F1", [64, 128], BF16)
    f2 = kk.sbuf("y_f2s", [128, 4, 128])
    C["F2"] = kk.sbuf("y_F2", [128, 4, 128], BF16)
    fi = kk.sbuf("y_fis", [64, 2, 32])
    C["FI"] = kk.sbuf("y_FI", [64, 2, 32], BF16)
    C["TW1"] = kk.sbuf("y_TW1", [128, 2, 64])
    C["TW2"] = kk.sbuf("y_TW2", [64, 2, 128])
    kk.dma("sp", f1[:], P.hy["hyF1"][:, :], rd=[], wr=["y_f1s"])
    kk.dma("sp", f2[:], P.hy["hyF2"][:, :, :], rd=[], wr=["y_f2s"])
    kk.dma("sp", fi[:], P.hy["hyFI"][:, :, :], rd=[], wr=["y_fis"])
    kk.dma("sp", C["TW1"][:], P.hy["hyTW1"][:, :, :], rd=[], wr=["y_TW1"])
    kk.dma("sp", C["TW2"][:], P.hy["hyTW2"][:, :, :], rd=[], wr=["y_TW2"])
    kk.cp(C["F1"][:], f1[:], rd=["y_f1s"], wr=["y_F1"])
    kk.cp(C["F2"][:], f2[:], rd=["y_f2s"], wr=["y_F2"])
    kk.cp(C["FI"][:], fi[:], rd=["y_fis"], wr=["y_FI"])
    return C


def emit_skewed(seq0, seq1, lag):
    n = max(len(seq0), len(seq1) + lag)
    for k in range(n):
        if k < len(seq0):
            seq0[k]()
        if 0 <= k - lag < len(seq1):
            seq1[k - lag]()


def phase_hy_kf(P, L, ngroups=D // CG):
    kk = P.k
    with kk.scope():
        C = hy_consts(P)
        S = [HyFFT(P, C, "_s%d" % i, ka=64, fwd_only=True) for i in range(2)]
        us = [[kk.sbuf("k_us%d_%d" % (st, i), [64, CG, 128]) for i in range(2)] for st in range(2)]
        ko = [[kk.sbuf("k_ko%d_%d" % (st, i), [128, CG, 2, 64], BF16) for i in range(2)] for st in range(2)]
        items = [(o, g) for o in range(2) for g in range(0, ngroups, 2)]

        def ld(i, st):
            o, g = items[i]
            row0 = o * D + (g + st) * CG
            kk.dma("sp", us[st][i % 2][:], P.HFD[row0:row0 + CG, :].rearrange("c (a b) -> a c b", b=128), rd=[],
                   wr=["k_us%d_%d" % (st, i % 2)])

        def out(i, st):
            o, g = items[i]
            k_, kk_ = ko[st][i % 2], "k_ko%d_%d" % (st, i % 2)
            kk.act(k_[:].rearrange("p c r f -> p r c f"), S[st].psX[:], AF.Copy, rd=[S[st].k("y_psX")], wr=[kk_])
            kk.dma("sp", P.KF[o, :, (g + st) * CG:(g + st + 1) * CG], k_[:], rd=[kk_], wr=[("KF", o, g + st)])
        seqs = [[], []]
        for st in range(2):
            ld(0, st)
        for i in range(len(items)):
            for st in range(2):
                q = seqs[st]
                if i + 1 < len(items):
                    q.append(lambda i=i, st=st: ld(i + 1, st))
                else:
                    q.append(lambda: None)
                q.append(lambda i=i, st=st: S[st].s_cast(us[st][i % 2][:], "k_us%d_%d" % (st, i % 2)))
                q.append(lambda st=st: S[st].s_A())
                q.append(lambda st=st: S[st].s_tw1())
                q.append(lambda st=st: S[st].s_C())
                q.append(lambda i=i, st=st: out(i, st))
        emit_skewed(seqs[0], seqs[1], 0)


def kf_side(P, L, ngroups=D // CG):
    kk = P.k
    C = hy_consts(P)
    S = HyFFT(P, C, "_k", ka=64, fwd_only=True)
    us = [kk.sbuf("k_us%d" % i, [64, CG, 128]) for i in range(2)]
    ko = [kk.sbuf("k_ko%d" % i, [128, CG, 2, 64], BF16) for i in range(2)]
    items = [(o, g) for o in range(2) for g in range(ngroups)]

    def ld(i, it):
        o, g = it
        row0 = o * D + g * CG
        kk.dma("sp", us[i % 2][:], P.HFD[row0:row0 + CG, :].rearrange("c (a b) -> a c b", b=128), rd=[], wr=["k_us%d" % (i % 2)])

    def comp(i, it):
        o, g = it
        S.s_cast(us[i % 2][:], "k_us%d" % (i % 2))
        S.s_A()
        S.s_tw1()
        S.s_C()
        k_, kk_ = ko[i % 2], "k_ko%d" % (i % 2)
        kk.act(k_[:].rearrange("p c r f -> p r c f"), S.psX[:], AF.Copy, rd=[S.k("y_psX")], wr=[kk_])
        kk.dma("sp", P.KF[o, :, g * CG:(g + 1) * CG], k_[:], rd=[kk_], wr=[("KF", o, g)])
    ld(0, items[0])
    for i, it in enumerate(items):
        if i + 1 < len(items):
            ld(i + 1, items[i + 1])
        comp(i, it)
        yield


def phase_hy_conv(P, L, ngroups=D // CG):
    kk = P.k
    with kk.scope():
        C = hy_consts(P)
        S = [HyFFT(P, C, "_s%d" % i) for i in range(2)]
        vx = [[[kk.sbuf("c_vx%d_%d_%d" % (q, st, i), [32, CG, 128]) for i in range(2)] for st in range(2)] for q in range(3)]
        kf = [[[kk.sbuf("c_kf%d_%d_%d" % (o, st, i), [128, CG, 2, 64], BF16) for i in range(2)] for st in range(2)] for o in range(2)]
        y1 = [kk.sbuf("c_y1_%d" % st, [32, CG, 128]) for st in range(2)]
        y2 = [[kk.sbuf("c_y2_%d_%d" % (st, i), [32, CG, 128]) for i in range(2)] for st in range(2)]
        items = list(range(0, ngroups, 2))

        def ld(i, st):
            c0 = (items[i] + st) * CG
            for q in range(3):
                kk.dma("sp", vx[q][st][i % 2][:], P.PT[q * D + c0:q * D + c0 + CG, LAT0:LAT0 + SEQ].rearrange("c (a b) -> a c b", b=128),
                       rd=[], wr=["c_vx%d_%d_%d" % (q, st, i % 2)])
            for o in range(2):
                kk.dma("sp", kf[o][st][i % 2][:], P.KF[o, :, c0:c0 + CG], rd=[], wr=["c_kf%d_%d_%d" % (o, st, i % 2)])

        def cast(i, st, o):
            p = i % 2
            if o == 0:
                S[st].s_cast(vx[0][st][p][:], "c_vx0_%d_%d" % (st, p))
            else:
                S[st].s_cast(y1[st][:], "c_y1_%d" % st)

        def gate(i, st, o):
            p = i % 2
            if o == 0:
                kk.tt(y1[st][:], S[st].psA[0:32, :, :], vx[1][st][p][:], ALU.mult,
                      rd=[S[st].k("y_psA"), "c_vx1_%d_%d" % (st, p)], wr=["c_y1_%d" % st])
            else:
                o_, ok_ = y2[st][p], "c_y2_%d_%d" % (st, p)
                kk.tt(o_[:], S[st].psA[0:32, :, :], vx[2][st][p][:], ALU.mult,
                      rd=[S[st].k("y_psA"), "c_vx2_%d_%d" % (st, p)], wr=[ok_])
                c0 = (items[i] + st) * CG
                kk.dma("sp", P.MT[c0:c0 + CG, LAT0:LAT0 + SEQ].rearrange("c (a b) -> a c b", b=128), o_[:], rd=[ok_],
                       wr=[("MT", items[i] + st)])
        seqs = [[], []]
        for st in range(2):
            ld(0, st)
        for i in range(len(items)):
            for st in range(2):
                q = seqs[st]
                for o in range(2):
                    q.append(lambda i=i, st=st, o=o: cast(i, st, o))
                    q.append(lambda st=st: S[st].s_A())
                    q.append(lambda st=st: S[st].s_tw1())
                    q.append(lambda st=st: S[st].s_C())
                    q.append(lambda i=i, st=st, o=o: S[st].s_Z(kf[o][st][i % 2], "c_kf%d_%d_%d" % (o, st, i % 2)))
                    q.append(lambda st=st: S[st].s_Cp())
                    if o == 0:
                        if i + 1 < len(items):
                            q.append(lambda i=i, st=st: ld(i + 1, st))
                        else:
                            q.append(lambda: None)
                    q.append(lambda st=st: S[st].s_tw2())
                    q.append(lambda st=st: S[st].s_Ap())
                    q.append(lambda i=i, st=st, o=o: gate(i, st, o))
        emit_skewed(seqs[0], seqs[1], 0)


def phase_hy_ctx(P, L):
    kk = P.k
    nc = kk.nc
    j = L // 2
    HC = 512
    with kk.scope():
        cf = kk.sbuf("x_cf", [128, 2, 2, 512])
        ci = kk.sbuf("x_ci", [128, 2, 4, 256])
        fo = kk.sbuf("x_fo", [64, 4096])
        h3s = kk.sbuf("x_h3", [64, 256])
        drow = kk.sbuf("x_drow", [128, D])
        tcol = kk.sbuf("x_tcol", [128, 2])
        win = kk.sbuf("x_win", [128, 2, D])
        skipb = kk.sbuf("x_skip", [128, 2, D])
        hfc = kk.sbuf("x_hfc", [128, 2, 4, HC])
        kfc = kk.sbuf("x_kfc", [128, 4, 2, HC])
        Z = kk.sbuf("x_Z", [128, 4, 2, HC])
        raw = [kk.sbuf("x_raw%d" % i, [128, 256]) for i in range(2)]
        vtok = kk.sbuf("x_vtok", [128, 2, 3, HC])
        y1 = kk.sbuf("x_y1", [128, 2, HC])
        y2 = kk.sbuf("x_y2", [128, 2, HC])
        tmp = [kk.sbuf("x_tmp%d" % i, [128, HC]) for i in range(4)]
        orow = [kk.sbuf("x_orow%d" % i, [128, 256]) for i in range(2)]
        psr = kk.psum("x_psr", [128, 512])
        psi = kk.psum("x_psi", [128, 512])
        psy = [kk.psum("x_psy%d" % i, [128, 512]) for i in range(2)]
        pst = [kk.psum("x_pst%d" % i, [128, 128]) for i in range(2)]
        kk.dma("sp", cf[:], P.hy["hyCF"].rearrange("r c p f -> p r c f"), rd=[], wr=["x_cf"])
        kk.dma("sp", ci[:], P.hy["hyCI"].rearrange("r c p t -> p r c t"), rd=[], wr=["x_ci"])
        kk.dma("sp", fo[:], P.hy["od_f_out"][j], rd=[], wr=["x_fo"])
        kk.dma("sp", drow[:], P.hy["hyDrow"][:, :], rd=[], wr=["x_drow"])
        kk.dma("sp", tcol[:], P.hy["hyTcol"][:, :], rd=[], wr=["x_tcol"])
        kk.dma("sp", skipb[:], P.hy["od_skip"][j].partition_broadcast(128), rd=[], wr=["x_skip"])
        for tt in range(2):
            kk.act(win[:, tt, :], drow[:, :], AF.Exp, rd=["x_drow", "x_tcol"], wr=["x_win"], scale=tcol[:, tt:tt + 1])

        def sink(ti, c0, nt, h3, hk):
            kk.cp(h3s[:, :], h3[:, :256], rd=[hk], wr=["x_h3"])
        with kk.scope():
            hy_mlp(P, j, "ctx", CTXL, sink)

        def cmul(ore, oim, ar, ai, br, bi, rd, wr):
            tk = ["x_tmp%d" % i for i in range(4)]
            kk.tt(tmp[0][:], ar, br, ALU.mult, rd=rd, wr=[tk[0]])
            kk.tt(tmp[1][:], ai, bi, ALU.mult, rd=rd, wr=[tk[1]])
            kk.tt(tmp[2][:], ar, bi, ALU.mult, rd=rd, wr=[tk[2]])
            kk.tt(tmp[3][:], ai, br, ALU.mult, rd=rd, wr=[tk[3]])
            kk.tt(ore, tmp[0][:], tmp[1][:], ALU.subtract, rd=[tk[0], tk[1]], wr=wr, e="pool")
            kk.tt(oim, tmp[2][:], tmp[3][:], ALU.add, rd=[tk[2], tk[3]], wr=wr, e="pool")

        def fwd(src_fn, srckeys, fk):
            for tt in range(2):
                kk.mm(psr[:, :], cf[:, 0, tt, fk * 128:(fk + 1) * 128], src_fn(tt), tt == 0, tt == 1, rd=["x_cf"] + srckeys, wr=["x_psr"])
            for tt in range(2):
                kk.mm(psi[:, :], cf[:, 1, tt, fk * 128:(fk + 1) * 128], src_fn(tt), tt == 0, tt == 1, rd=["x_cf"] + srckeys, wr=["x_psi"])

        oi = 0
        for half in range(2):
            hc0 = half * HC
            for tt in range(2):
                for sig in range(4):
                    kk.mm(psr[:, :], h3s[:, tt * 128:(tt + 1) * 128], fo[:, sig * D + hc0:sig * D + hc0 + HC], True, True,
                          rd=["x_h3", "x_fo"], wr=["x_psr"])
                    kk.stt(hfc[:, tt, sig, :], win[:, tt, hc0:hc0 + HC], 0.05, psr[:, :], ALU.add, ALU.mult, rd=["x_win", "x_psr"], wr=["x_hfc"])
            ri_ = 0
            for q in range(3):
                for cc in range(HC // 128):
                    r_, rk = raw[ri_ % 2], "x_raw%d" % (ri_ % 2)
                    ri_ += 1
                    row0 = q * D + hc0 + cc * 128
                    kk.dma("sp", r_[:], P.PT[row0:row0 + 128, CTX0:CTX0 + CTXL], rd=[], wr=[rk])
                    for tt in range(2):
                        p_, pk = pst[tt], "x_pst%d" % tt
                        kk.op("pe", nc.tensor.transpose, rd=[rk, "ident"], wr=[pk], out=p_[:, :], in_=r_[:, tt * 128:(tt + 1) * 128],
                              identity=P.ident[:, :])
                        kk.act(vtok[:, tt, q, cc * 128:(cc + 1) * 128], p_[:, :], AF.Copy, rd=[pk], wr=["x_vtok"])
            cur = lambda tt: vtok[:, tt, 0, :]
            curk = ["x_vtok"]
            for o in range(2):
                for fk in range(4):
                    fwd(lambda tt: hfc[:, tt, o * 2 + 0, :], ["x_hfc"], fk)
                    kk.act(kfc[:, fk, 0, :], psr[:, :], AF.Copy, rd=["x_psr"], wr=["x_kfc"])
                    kk.act(kfc[:, fk, 1, :], psi[:, :], AF.Copy, rd=["x_psi"], wr=["x_kfc"])
                    fwd(lambda tt: hfc[:, tt, o * 2 + 1, :], ["x_hfc"], fk)
                    kk.tt(kfc[:, fk, 0, :], kfc[:, fk, 0, :], psr[:, :], ALU.add, rd=["x_kfc", "x_psr"], wr=["x_kfc"])
                    kk.tt(kfc[:, fk, 1, :], kfc[:, fk, 1, :], psi[:, :], ALU.subtract, rd=["x_kfc", "x_psi"], wr=["x_kfc"])
                for fk in range(4):
                    fwd(cur, curk, fk)
                    cmul(Z[:, fk, 0, :], Z[:, fk, 1, :], psr[:, :], psi[:, :], kfc[:, fk, 0, :], kfc[:, fk, 1, :],
                         rd=["x_psr", "x_psi", "x_kfc"], wr=["x_Z"])
                dst = y1 if o == 0 else y2
                dk = "x_y1" if o == 0 else "x_y2"
                for tt in range(2):
                    p_, pk = psy[tt], "x_psy%d" % tt
                    n_ = 0
                    for fk in range(4):
                        for r in range(2):
                            kk.mm(p_[:, :], ci[:, r, fk, tt * 128:(tt + 1) * 128], Z[:, fk, r, :], n_ == 0, n_ == 7, rd=["x_ci", "x_Z"], wr=[pk])
                            n_ += 1
                    kk.tt(tmp[0][:], cur(tt), skipb[:, o, hc0:hc0 + HC], ALU.mult, rd=curk + ["x_skip"], wr=["x_tmp0"])
                    kk.tt(tmp[0][:], tmp[0][:], p_[:, :], ALU.add, rd=["x_tmp0", pk], wr=["x_tmp0"])
                    kk.tt(dst[:, tt, :], tmp[0][:], vtok[:, tt, 1 + o, :], ALU.mult, rd=["x_tmp0", "x_vtok"], wr=[dk])
                cur = lambda tt: y1[:, tt, :]
                curk = ["x_y1"]
            for cc in range(HC // 128):
                o_, ok_ = orow[oi % 2], "x_orow%d" % (oi % 2)
                oi += 1
                for tt in range(2):
                    p_, pk = pst[tt], "x_pst%d" % tt
                    kk.op("pe", nc.tensor.transpose, rd=["x_y2", "ident"], wr=[pk], out=p_[:, :], in_=y2[:, tt, cc * 128:(cc + 1) * 128],
                          identity=P.ident[:, :])
                    kk.act(o_[:, tt * 128:(tt + 1) * 128], p_[:, :], AF.Copy, rd=[pk], wr=[ok_])
                kk.dma("sp", P.MT[hc0 + cc * 128:hc0 + (cc + 1) * 128, CTX0:CTX0 + CTXL], o_[:, :], rd=[ok_], wr=[("MTc", half, cc)])


def build_program(cp, debug=False, nlayers=DEPTH):
    P = Prog(cp.off, cp.n, debug=debug)
    even_inputs(P)
    rwkv_inputs(P)
    hy_inputs(P)
    P.phase_init()
    P.phase_cond()
    for L in range(nlayers):
        j = L // 2
        if L % 2 == 0:
            phase_proj_even(P, L)
            phase_rwkv_prep(P, L)
            phase_rwkv_main2(P, L)
            phase_rwkv_out(P, L)
            phase_attn(P, L)
            P.phase_outproj(L, P.ev_w_out[j], None)
        else:
            phase_hy_proj(P, L)
            phase_hy_filter_lat(P, L)
            phase_hy_kf(P, L)
            phase_hy_conv(P, L)
            if L < DEPTH - 1:
                phase_hy_ctx(P, L)
            P.phase_outproj(L, P.od_w_out[j], "b_out%d" % j)
        P.phase_ffn(L)
    P.k.barrier()
    return P


def kernel(**inputs):
    inp = {k: np.asarray(v) for k, v in inputs.items()}
    cp = build_colpack(inp)
    maps = host_inputs(inp, cp)
    host_even_extra(maps, inp)
    host_rwkv_extra(maps, inp)
    host_hy_extra(maps, inp)
    P = build_program(cp)
    need = set(P.in_names)
    maps = [{k: v for k, v in m.items() if k in need} for m in maps]
    res = run_bass_kernel_spmd(P.k.nc, maps, core_ids=list(range(8)))
    out = np.stack([np.ascontiguousarray(np.asarray(r["outT"], dtype=np.float32).T) for r in res.results], axis=0)
    return out
```
